# Optimizing a Trainium2 kernel written in Bass

```python
import jax, jax.numpy as jnp
from jax import lax
import numpy as np

D_MODEL = 1024
BATCH = 2
SEQ = 8192
DEPTH = 1

N_META = 16
BLOCK = 128
ATTN_WIDTH = D_MODEL // 2
ATTN_HEADS = 8
ATTN_HEAD_DIM = ATTN_WIDTH // ATTN_HEADS
LRU_WIDTH = D_MODEL // 2
LRU_HEADS = 8
LRU_HEAD_DIM = LRU_WIDTH // LRU_HEADS
CONV_WIDTH = 4
LRU_C = 8.0
MIX_WIDTH = ATTN_WIDTH + LRU_WIDTH
IN_COLS = 3 * ATTN_WIDTH + 2 * LRU_WIDTH
PEER_HEADS = 8
PEER_KEY_DIM = 256
PEER_HALF = PEER_KEY_DIM // 2
N_KEYS = 128
N_EXPERTS = N_KEYS * N_KEYS
PEER_TOPK = 16
DEEPNORM_ALPHA = (2.0 * DEPTH) ** 0.25
DEEPNORM_BETA = (8.0 * DEPTH) ** -0.25
NORM_EPS = 1e-5

kernel_name = "hymba_sb_rglru_peer_deepnorm"


def layer_norm(x, g, b):
    xf = x.astype(jnp.float32)
    mu = jnp.mean(xf, axis=-1, keepdims=True)
    var = jnp.mean(jnp.square(xf - mu), axis=-1, keepdims=True)
    return ((xf - mu) * lax.rsqrt(var + NORM_EPS) * g.astype(jnp.float32) + b.astype(jnp.float32)).astype(x.dtype)


def rms_norm(x, g):
    xf = x.astype(jnp.float32)
    ms = jnp.mean(jnp.square(xf), axis=-1, keepdims=True)
    return (xf * lax.rsqrt(ms + NORM_EPS) * g.astype(jnp.float32)).astype(x.dtype)


def over_blocks(fn, *arrs):
    head = fn(*(a[:, :N_META] for a in arrs))
    n_blk = (arrs[0].shape[1] - N_META) // BLOCK

    def to_blocks(a):
        r = a[:, N_META:].reshape(a.shape[0], n_blk, BLOCK, *a.shape[2:])
        return jnp.moveaxis(r, 1, 0)

    body = lax.map(lambda blk: fn(*blk), tuple(to_blocks(a) for a in arrs))
    body = jnp.moveaxis(body, 0, 1)
    body = body.reshape(body.shape[0], n_blk * BLOCK, *body.shape[3:])
    return jnp.concatenate([head, body], axis=1)


def stick_breaking_block(q_blk, q_pos, k, v):
    z = jnp.einsum('bchd,bshd->bhcs', q_blk, k).astype(jnp.float32) * (ATTN_HEAD_DIM ** -0.5)
    k_pos = jnp.arange(k.shape[1], dtype=jnp.int32)
    mask = k_pos[None, :] < q_pos[:, None]
    log_beta = jax.nn.log_sigmoid(z)
    log_1m_beta = jnp.where(mask, jax.nn.log_sigmoid(-z), 0.0)
    suffix = lax.cumsum(log_1m_beta, axis=3, reverse=True) - log_1m_beta
    att = jnp.where(mask, jnp.exp(log_beta + suffix), 0.0)
    out = jnp.einsum('bhcs,bshd->bchd', att, v.astype(jnp.float32))
    return out.astype(q_blk.dtype)


def causal_depthwise_conv(x, w, b):
    y = lax.conv_general_dilated(
        x, w[:, None, :].astype(x.dtype), window_strides=(1,), padding=[(CONV_WIDTH - 1, 0)],
        dimension_numbers=('NWC', 'WIO', 'NWC'), feature_group_count=x.shape[-1])
    return y + b.astype(x.dtype)


def rg_lru(x, gate_a_w, gate_a_b, gate_x_w, gate_x_b, lam):
    bsz, t_len, ch = x.shape
    xh = x.reshape(bsz, t_len, LRU_HEADS, LRU_HEAD_DIM)
    r = jax.nn.sigmoid(jnp.einsum('bthi,hij->bthj', xh, gate_a_w).reshape(bsz, t_len, ch) + gate_a_b)
    i = jax.nn.sigmoid(jnp.einsum('bthi,hij->bthj', xh, gate_x_w).reshape(bsz, t_len, ch) + gate_x_b)
    log_a = -LRU_C * r.astype(jnp.float32) * jax.nn.softplus(-lam.astype(jnp.float32))
    a = jnp.exp(log_a)
    b_in = jnp.sqrt(-jnp.expm1(2.0 * log_a)) * (i * x).astype(jnp.float32)

    def combine(left, right):
        a1, b1 = left
        a2, b2 = right
        return a1 * a2, a2 * b1 + b2

    _, h = lax.associative_scan(combine, (a, b_in), axis=1)
    return h.astype(x.dtype)


def peer_block(xb, w_query, sub_keys, down, up):
    q = jnp.einsum('bcd,dhk->bchk', xb, w_query).astype(jnp.float32)
    q1, q2 = q[..., :PEER_HALF], q[..., PEER_HALF:]
    s1 = jnp.einsum('bchk,hnk->bchn', q1, sub_keys[:, 0].astype(jnp.float32))
    s2 = jnp.einsum('bchk,hnk->bchn', q2, sub_keys[:, 1].astype(jnp.float32))
    v1, i1 = lax.top_k(s1, PEER_TOPK)
    v2, i2 = lax.top_k(s2, PEER_TOPK)
    lead = v1.shape[:-1]
    cand = (v1[..., :, None] + v2[..., None, :]).reshape(*lead, PEER_TOPK * PEER_TOPK)
    cand_idx = (i1[..., :, None] * N_KEYS + i2[..., None, :]).reshape(*lead, PEER_TOPK * PEER_TOPK)
    sc, sel = lax.top_k(cand, PEER_TOPK)
    idx = jnp.take_along_axis(cand_idx, sel, axis=-1)
    g = jax.nn.softmax(sc, axis=-1)
    u_sel = down[idx]
    act = jax.nn.gelu(jnp.einsum('bchkd,bcd->bchk', u_sel, xb).astype(jnp.float32))
    w = (g * act).astype(xb.dtype)
    return jnp.einsum('bchk,bchkd->bcd', w, up[idx])


def setup_inputs(seed: int = 0) -> dict:
    key = jax.random.key(seed)
    ks = jax.random.split(key, 24)
    f32 = jnp.float32
    nrm = lambda k, shape, s: jax.random.normal(k, shape, f32) * s
    s_u = jax.random.uniform(ks[9], (DEPTH, LRU_WIDTH), f32, minval=0.9, maxval=0.999) ** (1.0 / LRU_C)
    return {
        "x": jax.random.normal(ks[0], (BATCH, SEQ, D_MODEL), f32),
        "meta_tokens": nrm(ks[1], (N_META, D_MODEL), 1.0),
        "ln_emb_g": 1.0 + nrm(ks[2], (D_MODEL,), 0.02),
        "ln_emb_b": nrm(ks[3], (D_MODEL,), 0.02),
        "w_in": nrm(ks[4], (DEPTH, D_MODEL, IN_COLS), D_MODEL ** -0.5),
        "conv_w": nrm(ks[5], (DEPTH, CONV_WIDTH, LRU_WIDTH), CONV_WIDTH ** -0.5),
        "conv_b": nrm(ks[6], (DEPTH, LRU_WIDTH), 0.01),
        "gate_a_w": nrm(ks[7], (DEPTH, LRU_HEADS, LRU_HEAD_DIM, LRU_HEAD_DIM), LRU_HEAD_DIM ** -0.5),
        "gate_a_b": nrm(ks[8], (DEPTH, LRU_WIDTH), 0.01),
        "gate_x_w": nrm(ks[10], (DEPTH, LRU_HEADS, LRU_HEAD_DIM, LRU_HEAD_DIM), LRU_HEAD_DIM ** -0.5),
        "gate_x_b": nrm(ks[11], (DEPTH, LRU_WIDTH), 0.01),
        "lru_lambda": jnp.log(s_u) - jnp.log1p(-s_u),
        "attn_norm_g": 1.0 + nrm(ks[12], (DEPTH, ATTN_WIDTH), 0.02),
        "lru_norm_g": 1.0 + nrm(ks[13], (DEPTH, LRU_WIDTH), 0.02),
        "w_out": nrm(ks[14], (DEPTH, MIX_WIDTH, D_MODEL), DEEPNORM_BETA * MIX_WIDTH ** -0.5),
        "ln1_g": 1.0 + nrm(ks[15], (DEPTH, D_MODEL), 0.02),
        "ln1_b": nrm(ks[16], (DEPTH, D_MODEL), 0.02),
        "peer_query_w": nrm(ks[17], (DEPTH, D_MODEL, PEER_HEADS, PEER_KEY_DIM), D_MODEL ** -0.5),
        "peer_sub_keys": nrm(ks[18], (DEPTH, PEER_HEADS, 2, N_KEYS, PEER_HALF), PEER_HALF ** -0.5),
        "peer_down": nrm(ks[19], (DEPTH, N_EXPERTS, D_MODEL), D_MODEL ** -0.5),
        "peer_up": nrm(ks[20], (DEPTH, N_EXPERTS, D_MODEL), DEEPNORM_BETA * PEER_HEADS ** -0.5),
        "ln2_g": 1.0 + nrm(ks[21], (DEPTH, D_MODEL), 0.02),
        "ln2_b": nrm(ks[22], (DEPTH, D_MODEL), 0.02),
    }


def reference(x, meta_tokens, ln_emb_g, ln_emb_b, w_in, conv_w, conv_b, gate_a_w, gate_a_b,
              gate_x_w, gate_x_b, lru_lambda, attn_norm_g, lru_norm_g, w_out, ln1_g, ln1_b,
              peer_query_w, peer_sub_keys, peer_down, peer_up, ln2_g, ln2_b):
    bsz = x.shape[0]
    meta = jnp.broadcast_to(meta_tokens[None].astype(x.dtype), (bsz, N_META, D_MODEL))
    h = layer_norm(jnp.concatenate([meta, x], axis=1), ln_emb_g, ln_emb_b)
    t_len = h.shape[1]
    pos = jnp.arange(t_len, dtype=jnp.int32)[None, :]
    splits = [ATTN_WIDTH, 2 * ATTN_WIDTH, 3 * ATTN_WIDTH, 3 * ATTN_WIDTH + LRU_WIDTH]
    for l in range(DEPTH):
        proj = h @ w_in[l].astype(h.dtype)
        q, k, v, xr, gr = jnp.split(proj, splits, axis=-1)
        q = q.reshape(bsz, t_len, ATTN_HEADS, ATTN_HEAD_DIM)
        k = k.reshape(bsz, t_len, ATTN_HEADS, ATTN_HEAD_DIM)
        v = v.reshape(bsz, t_len, ATTN_HEADS, ATTN_HEAD_DIM)
        attn = over_blocks(lambda qb, pb: stick_breaking_block(qb, pb[0], k, v), q, pos)
        attn = attn.reshape(bsz, t_len, ATTN_WIDTH)
        xr = causal_depthwise_conv(xr, conv_w[l], conv_b[l])
        lru = rg_lru(xr, gate_a_w[l], gate_a_b[l], gate_x_w[l], gate_x_b[l], lru_lambda[l]) * jax.nn.gelu(gr)
        mixed = jnp.concatenate([rms_norm(attn, attn_norm_g[l]), rms_norm(lru, lru_norm_g[l])], axis=-1)
        mixed = mixed @ w_out[l].astype(h.dtype)
        h = layer_norm(DEEPNORM_ALPHA * h + mixed, ln1_g[l], ln1_b[l])
        ffn = over_blocks(lambda xb: peer_block(xb, peer_query_w[l], peer_sub_keys[l], peer_down[l], peer_up[l]), h)
        h = layer_norm(DEEPNORM_ALPHA * h + ffn, ln2_g[l], ln2_b[l])
    return h[:, N_META:]
```

```python
import os
from contextlib import ExitStack
import numpy as np
import concourse.bass as bass
import concourse.mybir as mybir
from concourse.bass_utils import run_bass_kernel_spmd

F32 = mybir.dt.float32
BF16 = mybir.dt.bfloat16
I32 = mybir.dt.int32
U32 = mybir.dt.uint32
ALU = mybir.AluOpType
AF = mybir.ActivationFunctionType
AX = mybir.AxisListType

D = 1024
SEQ = 8192
NMETA = 16
T = SEQ + NMETA
NT = 64
EPS = 1e-5
ALPHA = 2.0 ** 0.25
NTB = 16
NSLOT = 6
GELU_C = 1.5957691216057308


class Sched:
    def __init__(self, nc, es):
        self.nc = nc
        self.es = es
        self.epoch = 0
        self.engs = ["pe", "act", "dve", "pool", "sp"]
        self.csem = {e: es.enter_context(nc.semaphore("c_" + e)) for e in self.engs}
        self.csem_ids = {id(v) for v in self.csem.values()}
        self.cnt = {e: 0 for e in self.engs}
        self.slots = {}
        for e in ["sp", "pool", "act"]:
            self.slots[e] = [[es.enter_context(nc.semaphore("d_%s%d" % (e, i))), 0] for i in range(NSLOT)]
        self.slot_next = {e: 0 for e in self.slots}
        self.ccsem = es.enter_context(nc.semaphore("ccsem"))
        self.cccnt = 0
        self.ops = {e: [] for e in self.engs}
        self.waited = {e: {} for e in self.engs}
        self.last_w = {}
        self.readers = {}
        self.nwaits = 0

    def _wait(self, eng, ev):
        sem, val = ev
        k = id(sem)
        if self.waited[eng].get(k, 0) >= val:
            return
        self.waited[eng][k] = val
        self.ops[eng].append(("wait", sem, val))
        self.nwaits += 1

    PSUM_KEYS = {"pT", "pP", "pV", "pG", "pL", "pZ", "pE", "pO", "pX", "bpT", "bpY", "bpQ", "bpS", "bpU", "bpTb"}

    def _split(self, reads, writes):
        r2 = []; w2 = list(writes)
        for b in reads:
            nm = b[0] if isinstance(b, tuple) else b
            if nm in self.PSUM_KEYS:
                if b not in w2:
                    w2.append(b)
            else:
                r2.append(b)
        return r2, w2

    def _deps(self, eng, reads, writes):
        deps = []
        for b in reads:
            if b in self.last_w:
                deps.append(self.last_w[b])
        for b in writes:
            rd = self.readers.get(b, ())
            if rd:
                deps.extend(rd)
            elif b in self.last_w:
                deps.append(self.last_w[b])
        for ev in deps:
            if eng == "pe" and ev[0] is self.csem["pe"]:
                continue
            self._wait(eng, ev)

    def _commit(self, ev, reads, writes):
        for b in writes:
            self.last_w[b] = ev
            self.readers[b] = []
        for b in reads:
            self.readers.setdefault(b, []).append(ev)

    cur_level = 0

    def lv(self, level):
        self.cur_level = level

    def capture_begin(self):
        self.cap = []

    def capture_end(self):
        c = self.cap
        self.cap = None
        return c

    def replay(self, lst, n):
        for _ in range(min(n, len(lst))):
            kind, eng, fn, reads, writes = lst.pop(0)[:5]
            (self.op if kind == "op" else self.dma)(eng, fn, reads, writes)

    def op(self, eng, fn, reads=(), writes=()):
        if getattr(self, "cap", None) is not None:
            self.cap.append(("op", eng, fn, list(reads), list(writes), self.cur_level))
            return None
        reads, writes = self._split(reads, writes)
        self._deps(eng, reads, writes)
        self.cnt[eng] += 1
        ev = (self.csem[eng], self.cnt[eng])
        self.ops[eng].append(("op", fn, ev[0], 1))
        self._commit(ev, reads, writes)
        return ev

    def new_slot(self, name):
        return [self.es.enter_context(self.nc.semaphore(name)), 0]

    def dma(self, eng, fn, reads=(), writes=(), slot=None):
        if getattr(self, "cap", None) is not None:
            assert slot is None
            self.cap.append(("dma", eng, fn, list(reads), list(writes), self.cur_level))
            return None
        reads, writes = self._split(reads, writes)
        self._deps(eng, reads, writes)
        if slot is None:
            i = self.slot_next[eng]
            self.slot_next[eng] = (i + 1) % NSLOT
            slot = self.slots[eng][i]
            if slot[1] > 0:
                self._wait(eng, (slot[0], slot[1]))
            self.all_slots_touch = True
        slot[1] += 16
        ev = (slot[0], slot[1])
        self.ops[eng].append(("op", fn, ev[0], 16))
        self._commit(ev, reads, writes)
        return ev

    def cc(self, fn, reads=(), writes=()):
        eng = "pool"
        self._deps(eng, reads, writes)
        self.cccnt += 1
        ev = (self.ccsem, self.cccnt)
        self.ops[eng].append(("op", fn, ev[0], 1))
        self._commit(ev, reads, writes)
        return ev

    def fence(self, eng="sp"):
        for e in self.engs:
            if self.cnt[e] > 0:
                self._wait(eng, (self.csem[e], self.cnt[e]))
        for e in self.slots:
            for s in self.slots[e]:
                if s[1] > 0:
                    self._wait(eng, (s[0], s[1]))
        for s in getattr(self, "extra_slots", []):
            if s[1] > 0:
                self._wait(eng, (s[0], s[1]))
        if self.cccnt:
            self._wait(eng, (self.ccsem, self.cccnt))

    def emit(self):
        nc = self.nc
        ops = self.ops

        def mk(name):
            def body(e):
                for it in ops[name]:
                    if it[0] == "wait":
                        e.wait_ge(it[1], it[2])
                    else:
                        it[1](e).then_inc(it[2], it[3])
            return body

        with nc.Block() as block:
            block.tensor(mk("pe"))
            block.scalar(mk("act"))
            block.vector(mk("dve"))
            block.gpsimd(mk("pool"))
            block.sync(mk("sp"))
        self.ops = {e: [] for e in self.engs}
        self.new_epoch()

    def new_epoch(self):
        self.epoch += 1
        old_ids = self.csem_ids
        self.csem = {e: self.es.enter_context(self.nc.semaphore("c%d_%s" % (self.epoch, e))) for e in self.engs}
        self.csem_ids = old_ids | {id(v) for v in self.csem.values()}
        self.cnt = {e: 0 for e in self.engs}
        keep = lambda ev: id(ev[0]) not in old_ids
        self.last_w = {b: ev for b, ev in self.last_w.items() if keep(ev)}
        self.readers = {b: [ev for ev in evs if keep(ev)] for b, evs in self.readers.items()}


def APx(base, free, off=0):
    p = base.ap[0]
    return bass.AP(base.tensor, base.offset + off, [list(p)] + [list(f) for f in free])


def build(debug=False, stop=None, mode="fused"):
    stop = stop or os.environ.get('MK_STOP')
    if mode == "A":
        stop = "A2"
    nc = bass.Bass("TRN2", target_bir_lowering=False)

    A_IN = {"xb", "meta", "lnE_T", "w_in", "lru_p", "gate_w", "cst", "identf"}
    B_IN = {"xo", "cst", "w_out_p", "gmix", "lnB", "wq", "skT", "down", "up", "idxm", "iota16"}

    def din(name, shape, dt=F32):
        if (mode == "A" and name not in A_IN) or (mode == "B" and name not in B_IN):
            return None
        return nc.dram_tensor(name, list(shape), dt, kind="ExternalInput").ap()

    xb = din("xb", [SEQ, D])
    xo = din("xo", [NTB * 128, D])
    meta = din("meta", [NMETA, D])
    lnE_T = din("lnE_T", [128, 16])
    w_in = din("w_in", [D, 640])
    lru_p = din("lru_p", [128, 8])
    gate_w = din("gate_w", [128, 256])
    cst = din("cst", [128, 384 + 5 * 512])
    identf_d = din("identf", [128, 128])
    w_out_p = din("w_out_p", [D, D])
    gmix_d = din("gmix", [128, 8])
    lnB = din("lnB", [6, D])
    wq_d = din("wq", [D, 2048])
    skT_d = din("skT", [128, 2048])
    down = din("down", [16384, D])
    up = din("up", [16384, D])
    idxm_d = din("idxm", [128, 64], I32)
    iota_d = din("iota16", [128, 16])
    out = None if mode == "A" else nc.dram_tensor("out", [NTB * 128, D], F32, kind="ExternalOutput").ap()
    mixbuf = [nc.dram_tensor("mixbuf%d" % i, [1024, 256], F32, kind=("ExternalOutput" if mode == "A" else "Internal")).ap() for i in range(8)]
    mixall = [nc.dram_tensor("mixall%d" % i, [4 * 1024, 256], F32, kind=("ExternalInput" if mode == "B" else "Internal")).ap() for i in range(8)]

    def mixdst(ti):
        r0 = (ti % 8) * 128
        return mixbuf[ti // 8][r0:r0 + 128, :]
    downb = upb = None
    if mode != "A":
        downb = nc.dram_tensor("downb", [16384, D], BF16, kind="Internal").ap()
        upb = nc.dram_tensor("upb", [16384, D], BF16, kind="Internal").ap()
    conv_jobs = []
    if mode != "A":
        for j in range(8):
            conv_jobs.append((down, downb, "downb", j))
            conv_jobs.append((up, upb, "upb", j))

    def convert_step(S, n=1):
        for _ in range(n):
            if not conv_jobs:
                return
            src, dst, nm, j = conv_jobs.pop(0)
            S.dma("pool", lambda e, src=src, dst=dst, j=j: e.dma_start(out=dst[j * 2048:(j + 1) * 2048, :], in_=src[j * 2048:(j + 1) * 2048, :]),
                  writes=[(nm, j)])

    dbg = None
    if debug:
        dbg = nc.dram_tensor("dbg", [SEQ, 256], F32, kind="ExternalOutput").ap()

    top = ExitStack()
    with top:
        S = Sched(nc, top)

        def sb(es, name, shape, dt=F32):
            return es.enter_context(nc.sbuf_tensor(name, list(shape), dt))

        def ps(es, name, shape, dt=F32):
            return es.enter_context(nc.psum_tensor(name, list(shape), dt))

        esA = ExitStack()
        if mode == "B":
            build_phase_b(nc, S, sb, ps, locals())
            return nc
        with esA:
            ident = sb(esA, "ident", [128, 128], BF16)
            identf = sb(esA, "identf_s", [128, 128], F32)
            negtri = sb(esA, "negtri", [128, 128], BF16)
            negones = sb(esA, "negones", [128, 128], BF16)
            masks = sb(esA, "masks", [128, 5, 512], BF16)
            lnT = sb(esA, "lnT", [128, 16])
            lrup = sb(esA, "lrup", [128, 8])
            lruc = sb(esA, "lruc", [128, 8])
            gw = sb(esA, "gw", [128, 256], BF16)
            qT0 = sb(esA, "qT0", [128, T], BF16)
            qT1 = sb(esA, "qT1", [128, T], BF16)
            kT = sb(esA, "kT", [128, T], BF16)
            vtok = sb(esA, "vtok", [128, NT + 1, 128], BF16)

            S.dma("pool", lambda e: e.dma_start(out=ident[:], in_=cst[:, 0:128]), writes=["ident"])
            S.dma("pool", lambda e: e.dma_start(out=negtri[:], in_=cst[:, 128:256]), writes=["negtri"])
            S.dma("pool", lambda e: e.dma_start(out=negones[:], in_=cst[:, 256:384]), writes=["negones"])
            S.dma("pool", lambda e: e.dma_start(out=masks[:], in_=cst[:, 384:].rearrange("p (r c) -> p r c", r=5)),
                  writes=["masks"])
            S.dma("pool", lambda e: e.dma_start(out=gw[:], in_=gate_w), writes=["gw"])
            S.dma("sp", lambda e: e.dma_start(out=identf[:], in_=identf_d), writes=["identf"])
            S.dma("sp", lambda e: e.dma_start(out=lnT[:], in_=lnE_T), writes=["lnT"])
            S.dma("sp", lambda e: e.dma_start(out=lrup[:], in_=lru_p), writes=["lrup"])
            S.op("dve", lambda e: e.memset(qT0[:], 0.0), writes=["qT0"])
            S.op("pool", lambda e: e.memset(qT1[:], 0.0), writes=["qT1"])
            S.op("dve", lambda e: e.memset(vtok[:, NT, :], 0.0), writes=["vtok"])
            S.op("act", lambda e: e.activation(out=lruc[:, 2:3], in_=lrup[:, 7:8], func=AF.Exp, scale=-1.0),
                 reads=["lrup"], writes=["lruc"])
            S.op("act", lambda e: e.activation(out=lruc[:, 3:4], in_=lruc[:, 2:3], func=AF.Ln, bias=1.0),
                 reads=["lruc"], writes=["lruc"])
            S.op("dve", lambda e: e.tensor_scalar(lruc[:, 0:1], lruc[:, 3:4], -8.0, None, op0=ALU.mult),
                 reads=["lruc"], writes=["lruc"])
            S.op("dve", lambda e: e.tensor_scalar(lruc[:, 1:2], lruc[:, 3:4], -16.0, None, op0=ALU.mult),
                 reads=["lruc"], writes=["lruc"])
            S.op("dve", lambda e: e.tensor_scalar(lruc[:, 4:6], lrup[:, 5:7], -1.0, None, op0=ALU.mult),
                 reads=["lrup", "lruc"], writes=["lruc"])

            esA1 = ExitStack()
            with esA1:
                W = sb(esA1, "W", [128, 8, 640], BF16)
                NXB = 5
                NB6 = 8
                xt = [sb(esA1, "xt%d" % i, [128, D]) for i in range(NXB)]
                xn = [sb(esA1, "xn%d" % i, [128, D], BF16) for i in range(3)]
                hT = [sb(esA1, "hT%d" % i, [128, 8, 128], BF16) for i in range(3)]
                st = [sb(esA1, "st%d" % i, [128, 2, 6]) for i in range(4)]
                mv = [sb(esA1, "mv%d" % i, [128, 4]) for i in range(4)]
                xrb = [sb(esA1, "xrb%d" % i, [128, 3 + 128]) for i in range(2)]
                grb = [sb(esA1, "grb%d" % i, [128, 128]) for i in range(NB6)]
                L = {}
                for nm in ["xc", "e1", "e2", "ea", "ea2", "bi", "hh", "g1", "g2", "lo"]:
                    L[nm] = [sb(esA1, "L%s%d" % (nm, i), [128, 128]) for i in range(NB6)]
                xcb = [sb(esA1, "xcb%d" % i, [128, 128], BF16) for i in range(3)]
                hst = sb(esA1, "hst", [128, 1])
                lstage = [sb(esA1, "lstage%d" % i, [128, 128]) for i in range(3)]
                pT = [ps(esA1, "pT%d" % i, [128, 8, 128], BF16) for i in range(2)]
                pP = [ps(esA1, "pP%d" % i, [128, 512]) for i in range(2)]
                pV = [ps(esA1, "pV%d" % i, [128, 512]) for i in range(1)]
                pG = [ps(esA1, "pG%d" % i, [128, 512]) for i in range(2)]
                pL = ps(esA1, "pL", [128, 128])

                for kc in range(8):
                    S.dma("pool", lambda e, kc=kc: e.dma_start(out=W[:, kc, :], in_=w_in[kc * 128:(kc + 1) * 128, :]),
                          writes=[("W", kc)])
                S.op("dve", lambda e: e.memset(xrb[0][:], 0.0), writes=[("xrbh", 0), ("xrb", 0)])
                S.op("dve", lambda e: e.memset(hst[:], 0.0), writes=["hst"])

                tiles = [-1] + list(range(NT))

                def tinfo(ti):
                    if ti < 0:
                        return NMETA, 0, NT
                    return 128, NMETA + 128 * ti, ti

                def tile_prog(k):
                    ti = tiles[k]
                    n, c0, vs = tinfo(ti)
                    x_ = xt[k % NXB]; xk = ("xt", k % NXB)
                    s_ = st[k % 4]; m_ = mv[k % 4]
                    sk = ("st", k % 4); mk_ = ("mv", k % 4)
                    S.lv(0)
                    if ti < 0:
                        S.dma("sp", lambda e: e.dma_start(out=x_[0:n, :], in_=meta), writes=[xk])
                    else:
                        S.dma("sp", lambda e: e.dma_start(out=x_[:, :], in_=xb[ti * 128:(ti + 1) * 128, :]), writes=[xk])
                    S.lv(1)
                    S.op("dve", lambda e: e.bn_stats(s_[0:n, 0, :], x_[0:n, 0:512]), reads=[xk], writes=[sk])
                    S.op("dve", lambda e: e.bn_stats(s_[0:n, 1, :], x_[0:n, 512:1024]), reads=[xk], writes=[sk])
                    S.op("dve", lambda e: e.bn_aggr(m_[0:n, 0:2], s_[0:n].rearrange("p a b -> p (a b)")), reads=[sk], writes=[mk_])
                    S.lv(2)
                    S.op("act", lambda e: e.activation(out=m_[0:n, 2:3], in_=m_[0:n, 1:2], func=AF.Ln, bias=EPS),
                         reads=[mk_], writes=[mk_])
                    S.op("act", lambda e: e.activation(out=m_[0:n, 3:4], in_=m_[0:n, 2:3], func=AF.Exp, scale=-0.5),
                         reads=[mk_], writes=[mk_])
                    S.lv(3)
                    xn_ = xn[k % 3]; xnk = ("xn", k % 3)
                    S.op("dve", lambda e: e.tensor_scalar(xn_[0:n, :], x_[0:n, :], m_[0:n, 0:1], m_[0:n, 3:4],
                                                          op0=ALU.subtract, op1=ALU.mult), reads=[xk, mk_], writes=[xnk])
                    S.lv(4)
                    pT_ = pT[k % 2]; pk = ("pT", k % 2)
                    for kc in range(8):
                        S.op("pe", lambda e, kc=kc: e.transpose(pT_[:, kc, 0:n], xn_[0:n, kc * 128:(kc + 1) * 128], ident[0:n, 0:n]),
                             reads=[xnk, "ident"], writes=[pk])
                    S.lv(5)
                    h_ = hT[k % 3]; hk = ("hT", k % 3)
                    for kc in range(8):
                        S.op("act", lambda e, kc=kc: e.activation(out=h_[:, kc, 0:n], in_=pT_[:, kc, 0:n], func=AF.Identity,
                                                                    scale=lnT[:, kc:kc + 1], bias=lnT[:, 8 + kc:9 + kc]),
                             reads=[pk, "lnT"], writes=[hk])
                    S.lv(6)
                    pP_ = pP[k % 2]; ppk = ("pP", k % 2)
                    pV_ = pV[0]; pvk = ("pV", 0)
                    wk = [("W", kc) for kc in range(8)]
                    for j, cb in enumerate([0, 128, 384, 512]):
                        for kc in range(8):
                            S.op("pe", lambda e, j=j, cb=cb, kc=kc: e.matmul(pP_[:, j * 128:j * 128 + n], W[:, kc, cb:cb + 128], h_[:, kc, 0:n],
                                                                               start=(kc == 0), stop=(kc == 7)),
                                 reads=[hk] + wk, writes=[ppk])
                    for kc in range(8):
                        S.op("pe", lambda e, kc=kc: e.matmul(pV_[0:n, 0:128], h_[:, kc, 0:n], W[:, kc, 256:384],
                                                             start=(kc == 0), stop=(kc == 7)),
                             reads=[hk] + wk, writes=[pvk])
                    S.lv(7)
                    xr_ = xrb[k % 2]; xrn = xrb[(k + 1) % 2]
                    g_ = grb[k % NB6]; gk = ("grb", k % NB6)
                    S.op("act", lambda e: e.activation(out=qT0[0:64, c0:c0 + n], in_=pP_[0:64, 0:n], func=AF.Identity, scale=0.125),
                         reads=[ppk], writes=["qT0"])
                    S.op("act", lambda e: e.activation(out=qT1[64:128, c0:c0 + n], in_=pP_[64:128, 0:n], func=AF.Identity, scale=0.125),
                         reads=[ppk], writes=["qT1"])
                    S.op("act", lambda e: e.activation(out=kT[:, c0:c0 + n], in_=pP_[:, 128:128 + n], func=AF.Identity), reads=[ppk], writes=["kT"])
                    S.op("act", lambda e: e.activation(out=xr_[:, 3:3 + n], in_=pP_[:, 256:256 + n], func=AF.Identity),
                         reads=[ppk], writes=[("xrb", k % 2)])
                    S.op("act", lambda e: e.activation(out=g_[:, 0:n], in_=pP_[:, 384:384 + n], func=AF.Identity),
                         reads=[ppk], writes=[gk])
                    S.op("dve", lambda e: e.tensor_copy(vtok[0:n, vs, :], pV_[0:n, 0:128]), reads=[pvk], writes=["vtok"])
                    S.lv(8)
                    rk = [("xrb", k % 2), ("xrbh", k % 2)]
                    b = k % NB6
                    e1, e2, ea, ea2, bi, hh, g1, g2, lo, xc = (L[x][b] for x in ["e1", "e2", "ea", "ea2", "bi", "hh", "g1", "g2", "lo", "xc"])
                    K = lambda nm: (nm, b)
                    S.op("dve", lambda e: e.tensor_scalar(xc[:, 0:n], xr_[:, 3:3 + n], lrup[:, 3:4], lrup[:, 4:5], op0=ALU.mult, op1=ALU.add),
                         reads=rk + ["lrup"], writes=[K("xc")])
                    for kk in (2, 1, 0):
                        S.op("dve", lambda e, kk=kk: e.scalar_tensor_tensor(out=xc[:, 0:n], in0=xr_[:, kk:kk + n], scalar=lrup[:, kk:kk + 1],
                                                                             in1=xc[:, 0:n], op0=ALU.mult, op1=ALU.add),
                             reads=rk + ["lrup", K("xc")], writes=[K("xc")])
                    S.op("dve", lambda e: e.tensor_copy(xrn[:, 0:3], xr_[:, n:n + 3]), reads=rk, writes=[("xrbh", (k + 1) % 2)])
                    xb_ = xcb[k % 3]; xbk = ("xcb", k % 3)
                    S.op("dve", lambda e: e.tensor_copy(xb_[:, 0:n], xc[:, 0:n]), reads=[K("xc")], writes=[xbk])
                    if ti >= 0:
                        S.op("pool", lambda e: e.tensor_tensor(out=g1[:, 0:n], in0=g_[:, 0:n], in1=g_[:, 0:n], op=ALU.mult), reads=[gk], writes=[K("g1")])
                        S.op("pool", lambda e: e.tensor_scalar(g1[:, 0:n], g1[:, 0:n], 0.044715, 1.0, op0=ALU.mult, op1=ALU.add),
                             reads=[K("g1")], writes=[K("g1")])
                        S.op("pool", lambda e: e.tensor_tensor(out=g1[:, 0:n], in0=g1[:, 0:n], in1=g_[:, 0:n], op=ALU.mult), reads=[K("g1"), gk], writes=[K("g1")])
                    S.lv(9)
                    pG_ = pG[k % 2]; pgk = ("pG", k % 2)
                    S.op("pe", lambda e: e.matmul(pG_[:, 0:n], gw[:, 0:128], xb_[:, 0:n], start=True, stop=True),
                         reads=[xbk, "gw"], writes=[pgk])
                    S.op("pe", lambda e: e.matmul(pG_[:, 128:128 + n], gw[:, 128:256], xb_[:, 0:n], start=True, stop=True),
                         reads=[xbk, "gw"], writes=[pgk])
                    S.lv(10)
                    A = lambda out, in_, **kw: S.op("act", lambda e: e.activation(out=out, in_=in_, **kw["a"]), reads=kw["r"], writes=kw["w"])
                    A(e1[:, 0:n], pG_[:, 0:n], a=dict(func=AF.Exp, scale=-1.0, bias=lruc[:, 4:5]), r=[pgk, "lruc"], w=[K("e1")])
                    A(e2[:, 0:n], pG_[:, 128:128 + n], a=dict(func=AF.Exp, scale=-1.0, bias=lruc[:, 5:6]), r=[pgk, "lruc"], w=[K("e2")])
                    A(e1[:, 0:n], e1[:, 0:n], a=dict(func=AF.Ln, bias=1.0), r=[K("e1")], w=[K("e1")])
                    A(e2[:, 0:n], e2[:, 0:n], a=dict(func=AF.Ln, bias=1.0), r=[K("e2")], w=[K("e2")])
                    A(e1[:, 0:n], e1[:, 0:n], a=dict(func=AF.Exp, scale=-1.0), r=[K("e1")], w=[K("e1")])
                    A(e2[:, 0:n], e2[:, 0:n], a=dict(func=AF.Exp, scale=-1.0), r=[K("e2")], w=[K("e2")])
                    A(ea[:, 0:n], e1[:, 0:n], a=dict(func=AF.Exp, scale=lruc[:, 0:1]), r=[K("e1"), "lruc"], w=[K("ea")])
                    A(ea2[:, 0:n], e1[:, 0:n], a=dict(func=AF.Exp, scale=lruc[:, 1:2]), r=[K("e1"), "lruc"], w=[K("ea2")])
                    A(ea2[:, 0:n], ea2[:, 0:n], a=dict(func=AF.Ln, scale=-1.0, bias=1.0), r=[K("ea2")], w=[K("ea2")])
                    A(ea2[:, 0:n], ea2[:, 0:n], a=dict(func=AF.Exp, scale=0.5), r=[K("ea2")], w=[K("ea2")])
                    if ti >= 0:
                        A(g2[:, 0:n], g1[:, 0:n], a=dict(func=AF.Exp, scale=-GELU_C), r=[K("g1")], w=[K("g2")])
                        A(g2[:, 0:n], g2[:, 0:n], a=dict(func=AF.Ln, bias=1.0), r=[K("g2")], w=[K("g2")])
                        A(g2[:, 0:n], g2[:, 0:n], a=dict(func=AF.Exp, scale=-1.0), r=[K("g2")], w=[K("g2")])
                    S.lv(11)
                    S.op("dve", lambda e: e.tensor_tensor(out=bi[:, 0:n], in0=e2[:, 0:n], in1=xc[:, 0:n], op=ALU.mult),
                         reads=[K("e2"), K("xc")], writes=[K("bi")])
                    S.op("dve", lambda e: e.tensor_tensor(out=bi[:, 0:n], in0=bi[:, 0:n], in1=ea2[:, 0:n], op=ALU.mult),
                         reads=[K("bi"), K("ea2")], writes=[K("bi")])
                    S.op("dve", lambda e: e.tensor_tensor_scan(hh[:, 0:n], ea[:, 0:n], bi[:, 0:n], hst[:, 0:1], op0=ALU.mult, op1=ALU.add),
                         reads=[K("ea"), K("bi"), "hst"], writes=[K("hh")])
                    S.op("dve", lambda e: e.tensor_copy(hst[:, 0:1], hh[:, n - 1:n]), reads=[K("hh")], writes=["hst"])
                    if ti < 0:
                        return
                    S.op("dve", lambda e: e.tensor_tensor(out=g2[:, 0:n], in0=g2[:, 0:n], in1=g_[:, 0:n], op=ALU.mult), reads=[K("g2"), gk], writes=[K("g2")])
                    S.op("dve", lambda e: e.tensor_tensor(out=lo[:, 0:n], in0=g2[:, 0:n], in1=hh[:, 0:n], op=ALU.mult),
                         reads=[K("g2"), K("hh")], writes=[K("lo")])
                    S.lv(12)
                    S.op("pe", lambda e: e.transpose(pL[:, :], lo[:, :], identf[:, :]), reads=[K("lo"), "identf"], writes=["pL"])
                    S.lv(13)
                    ls = lstage[k % 3]
                    S.op("act", lambda e: e.activation(out=ls[:, :], in_=pL[:, :], func=AF.Identity), reads=["pL"], writes=[("ls", k % 3)])
                    S.lv(14)
                    S.dma("sp", lambda e: e.dma_start(out=mixdst(ti)[:, 128:256], in_=ls[:, :]),
                          reads=[("ls", k % 3)], writes=[("mixbuf", ti, 2)])

                nk = len(tiles)
                NLV = 15
                buckets = {}
                for k in range(nk):
                    S.capture_begin()
                    tile_prog(k)
                    for ent in S.capture_end():
                        buckets.setdefault((k + ent[5], k), []).append(ent)
                S.lv(0)
                laststep = -1
                for (step, k) in sorted(buckets):
                    if step != laststep:
                        laststep = step
                        if step >= 8 and step % 4 == 0:
                            convert_step(S)
                    lst = buckets[(step, k)]
                    S.replay(lst, len(lst))
                if stop == "A1":
                    S.fence("sp")
                S.emit()
            if stop == "A1":
                return nc

            esA2 = ExitStack()
            with esA2:
                Eb = [sb(esA2, "Eb%d" % i, [128, 512]) for i in range(2)]
                Lb = [sb(esA2, "Lb%d" % i, [128, 512], BF16) for i in range(2)]
                Ls = [sb(esA2, "Ls%d" % i, [128, 512], BF16) for i in range(2)]
                At = [sb(esA2, "At%d" % i, [128, 512], BF16) for i in range(2)]
                osb = [sb(esA2, "osb%d" % i, [128, 512]) for i in range(2)]
                ostg = [sb(esA2, "ostg%d" % i, [128, 4, 64]) for i in range(2)]
                pZ = [ps(esA2, "pZ%d" % i, [128, 512]) for i in range(3)]
                pO = [ps(esA2, "pO%d" % i, [128, 512]) for i in range(2)]
                pX = ps(esA2, "pX", [128, 512])
                qTs = [qT0, qT1]
                def attn_group(i, hd, gi):
                    if True:
                        qT = qTs[hd]
                        qk = "qT%d" % hd
                        q0 = NMETA + 512 * i
                        seq = [(4 * i + r, r) for r in (3, 2, 1, 0)] + [(m, None) for m in range(4 * i - 1, -1, -1)] + [(NT, 4)]
                        N = len(seq)
                        pO_ = pO[gi % 2]; pok = ("pO", gi % 2)

                        def kcols(m):
                            if m == NT:
                                return slice(0, 128)
                            return slice(NMETA + 128 * m, NMETA + 128 * (m + 1))

                        def sZ(n):
                            m, mk = seq[n]
                            b = n % 2
                            b3 = n % 3
                            S.op("pe", lambda e: e.matmul(pZ[b3][:, :], kT[:, kcols(m)], qT[:, q0:q0 + 512], start=True, stop=False),
                                 reads=["kT", qk], writes=[("pZ", b3)])
                            S.op("act", lambda e: e.activation(out=Eb[b][:, :], in_=pZ[b3][:, :], func=AF.Exp),
                                 reads=[("pZ", b3)], writes=[("Eb", b)])

                        def sZ2(n):
                            m, mk = seq[n]
                            b = n % 2
                            S.op("act", lambda e: e.activation(out=Lb[b][:, :], in_=Eb[b][:, :], func=AF.Ln, bias=1.0),
                                 reads=[("Eb", b)], writes=[("Lb", b)])
                            if mk is not None:
                                S.op("dve", lambda e: e.tensor_tensor(out=Lb[b][:, :], in0=Lb[b][:, :], in1=masks[:, mk, :], op=ALU.mult),
                                     reads=[("Lb", b), "masks"], writes=[("Lb", b)])

                        def sE(n):
                            m, mk = seq[n]
                            b = n % 2
                            b3 = n % 3
                            S.op("pe", lambda e: e.matmul(pZ[b3][:, :], negtri[:, :], Lb[b][:, :], start=False, stop=(n == 0), skip_group_check=True),
                                 reads=["negtri", ("Lb", b)], writes=[("pZ", b3)])
                            if n > 0:
                                S.op("pe", lambda e: e.matmul(pZ[b3][:, :], negones[:, :], Ls[b][:, :], start=False, stop=True, skip_group_check=True),
                                     reads=["negones", ("Ls", b)], writes=[("pZ", b3)])
                            S.op("act", lambda e: e.activation(out=At[b][:, :], in_=pZ[b3][:, :], func=AF.Exp),
                                 reads=[("pZ", b3)], writes=[("At", b)])
                            if mk is not None:
                                S.op("dve", lambda e: e.tensor_tensor(out=At[b][:, :], in0=At[b][:, :], in1=masks[:, mk, :], op=ALU.mult),
                                     reads=[("At", b), "masks"], writes=[("At", b)])
                            if n + 1 < N:
                                nb = (n + 1) % 2
                                if n == 0:
                                    S.op("dve", lambda e: e.tensor_copy(Ls[nb][:, :], Lb[b][:, :]), reads=[("Lb", b)], writes=[("Ls", nb)])
                                else:
                                    S.op("dve", lambda e: e.tensor_tensor(out=Ls[nb][:, :], in0=Ls[b][:, :], in1=Lb[b][:, :], op=ALU.add),
                                         reads=[("Lb", b), ("Ls", b)], writes=[("Ls", nb)])

                        def sO(n):
                            m, mk = seq[n]
                            b = n % 2
                            S.op("pe", lambda e: e.matmul(pO_[:, :], vtok[:, m, :], At[b][:, :], start=(n == 0), stop=(n == N - 1)),
                                 reads=["vtok", ("At", b)], writes=[pok])

                        for n in range(-1, N + 1):
                            if 0 <= n + 1 < N:
                                sZ(n + 1)
                                sZ2(n + 1)
                            if 0 <= n < N:
                                sE(n)
                            if 0 <= n - 1 < N:
                                sO(n - 1)
                        ob = osb[gi % 2]; obk = ("osb", gi % 2)
                        S.op("dve", lambda e: e.tensor_copy(ob[:, :], pO_[:, :]), reads=[pok], writes=[obk])
                        for j in range(4):
                            S.op("pe", lambda e, j=j: e.transpose(pX[:, j * 128:(j + 1) * 128], ob[:, j * 128:(j + 1) * 128], identf[:, :]),
                                 reads=[obk, "identf"], writes=["pX"])
                        og = ostg[gi % 2]; ogk = ("ostg", gi % 2)
                        S.op("act", lambda e: e.activation(out=og[:, :, :], in_=pX[:, :].rearrange("p (j c) -> p j c", j=4)[:, :, 64 * hd:64 * hd + 64],
                                                           func=AF.Identity), reads=["pX"], writes=[ogk])
                        for j in range(4):
                            S.dma("sp", lambda e, j=j: e.dma_start(out=mixdst(4 * i + j)[:, 64 * hd:64 * hd + 64], in_=og[:, j, :]),
                                  reads=[ogk], writes=[("mixbuf", 4 * i + j, hd)])

                gi = 0
                for i in range(16):
                    for hd in range(2):
                        attn_group(i, hd, gi)
                        gi += 1
                    if i % 2 == 1 and stop != "A2":
                        sp_ = i // 2
                        ks = [("mixbuf", ti, q) for ti in range(8 * sp_, 8 * sp_ + 8) for q in range(3)]
                        S.cc(lambda e, sp_=sp_: e.collective_compute("AllGather", ALU.bypass, replica_groups=[[0, 1, 2, 3], [4, 5, 6, 7]],
                                                                     ins=[mixbuf[sp_]], outs=[mixall[sp_]]), reads=ks, writes=[("mixall", sp_)])

                S.emit()

        if debug:
            esD = ExitStack()
            with esD:
                dt_ = sb(esD, "dbgt", [128, 64, 256])
                allk = [("mixbuf", ti, q) for ti in range(NT) for q in range(3)]
                for ti in range(NT):
                    S.dma("sp", lambda e, ti=ti: e.dma_start(out=dt_[:, ti, :], in_=mixdst(ti)), reads=allk, writes=["dbgt"])
                S.dma("sp", lambda e: e.dma_start(out=dbg.rearrange("(j p) c -> p j c", p=128), in_=dt_[:]), reads=["dbgt"], writes=["dbg"])
                S.fence("sp")
                S.emit()
        if stop == "A2":
            if not debug:
                S.fence("sp")
                S.emit()
            return nc
        ndel = int(os.environ.get("MK_DELAY", 0))
        if ndel:
            esX = ExitStack()
            with esX:
                scr = sb(esX, "scr", [128, 2048])
                S.op("act", lambda e: e.activation(out=scr[:], in_=scr[:], func=AF.Identity, scale=0.0), writes=["scr"])
                for _ in range(ndel):
                    S.op("act", lambda e: e.activation(out=scr[:], in_=scr[:], func=AF.Identity), reads=["scr"], writes=["scr"])
                S.op("pool", lambda e: e.memset(scr[:, 0:1], 0.0), reads=["scr"], writes=["scr"])
                S.fence("sp")
                S.emit()

        if stop == "CC":
            S.fence("sp")
            S.emit()
            return nc
        build_phase_b(nc, S, sb, ps, locals())
    return nc


def build_phase_b(nc, S, sb, ps, env):
    xo = env["xo"]; w_out_p = env["w_out_p"]; gmix_d = env["gmix_d"]; lnB = env["lnB"]; wq_d = env["wq_d"]
    skT_d = env["skT_d"]; down = env["down"]; up = env["up"]; idxm_d = env["idxm_d"]; iota_d = env["iota_d"]
    out = env["out"]; mixall = env["mixall"]; cst = env["cst"]
    downb = env["downb"]; upb = env["upb"]
    env["convert_step"](S, 16)
    tabk_d = [("downb", j) for j in range(8)]
    tabk_u = [("upb", j) for j in range(8)]
    es = ExitStack()
    with es:
        ident = sb(es, "identB", [128, 128], BF16)
        Wo = sb(es, "Wo", [128, 8, D], BF16)
        gmix = sb(es, "gmix_s", [128, 8])
        Wq = sb(es, "Wq", [128, 8, 2048], BF16)
        SK = sb(es, "SK", [128, 16, 128], BF16)
        LNb = sb(es, "LNb", [128, 6, D])
        idxm = sb(es, "idxm_s", [128, 64], I32)
        iota = sb(es, "iota_s", [128, 16])
        WD = sb(es, "WD", [128, 128 * 128], BF16)

        S.dma("pool", lambda e: e.dma_start(out=ident[:], in_=cst[:, 0:128]), writes=["identB"])
        S.dma("sp", lambda e: e.dma_start(out=idxm[:], in_=idxm_d), writes=["idxm"])
        S.dma("sp", lambda e: e.dma_start(out=gmix[:], in_=gmix_d), writes=["gmix"])
        for r in range(6):
            S.dma("sp", lambda e, r=r: e.dma_start(out=LNb[:, r, :], in_=bass.AP(lnB.tensor, r * D, [[0, 128], [1, D]])),
                  writes=[("LNb", r)])
        S.dma("sp", lambda e: e.dma_start(out=iota[:], in_=iota_d), writes=["iota"])
        S.op("dve", lambda e: e.memset(WD[:], 0.0), writes=["WD"])

        def load_weights():
            for kc in range(8):
                S.dma("pool", lambda e, kc=kc: e.dma_start(out=Wo[:, kc, :], in_=w_out_p[kc * 128:(kc + 1) * 128, :]), writes=[("Wo", kc)])
            for kc in range(8):
                S.op("dve", lambda e, kc=kc: e.tensor_scalar(Wo[:, kc, :], Wo[:, kc, :], gmix[:, kc:kc + 1], None, op0=ALU.mult),
                     reads=[("Wo", kc), "gmix"], writes=[("Wo", kc)])
            for kc in range(8):
                S.dma("pool", lambda e, kc=kc: e.dma_start(out=Wq[:, kc, :], in_=wq_d[kc * 128:(kc + 1) * 128, :]), writes=[("Wq", kc)])
            S.dma("pool", lambda e: e.dma_start(out=SK[:], in_=skT_d.rearrange("p (c n) -> p c n", c=16)), writes=["SK"])

        xt = sb(es, "bx", [128, D])
        mix = sb(es, "mix", [128, 4, 256])
        mixn = sb(es, "mixn", [128, 4, 256], BF16)
        mixT = sb(es, "mixT", [128, 8, 128], BF16)
        ot = sb(es, "ot", [128, D])
        junk = sb(es, "junk", [128, D], BF16)
        sq = sb(es, "sq", [128, 4, 128], BF16)
        st = sb(es, "bst", [128, 2, 6])
        sm = sb(es, "bsm", [128, 16])
        st2 = sb(es, "bst2", [128, 2, 6])
        sm2 = sb(es, "bsm2", [128, 16])
        U2 = [sb(es, "bu%d" % i, [128, D]) for i in range(2)]
        H1B = [sb(es, "h1b%d" % i, [128, D], BF16) for i in range(2)]
        h1T = mixT
        qsb = sb(es, "qsb", [128, 16, 128], BF16)
        Ssb = sb(es, "Ssb", [128, 16, 128])
        wk = sb(es, "wk", [128, 128])
        wk2 = sb(es, "wk2", [128, 256])
        v1 = sb(es, "v1", [128, 8, 16]); v2 = sb(es, "v2", [128, 8, 16])
        i1 = sb(es, "i1", [128, 8, 16], U32); i2 = sb(es, "i2", [128, 8, 16], U32)
        i1f = sb(es, "i1f", [128, 8, 16]); i2f = sb(es, "i2f", [128, 8, 16])
        cand = sb(es, "cand", [128, 8, 256])
        eq = cand[:].rearrange("p h (a b) -> p h a b", a=16)
        scv = sb(es, "scv", [128, 8, 16])
        pos = sb(es, "pos", [128, 8, 16], U32)
        ph = sb(es, "ph", [128, 8, 16], U32)
        phf = sb(es, "phf", [128, 8, 16]); plf = sb(es, "plf", [128, 8, 16])
        S1 = [sb(es, "s1_%d" % i, [128, 8, 16]) for i in range(2)]
        S2 = [sb(es, "s2_%d" % i, [128, 8, 16]) for i in range(2)]
        GSM = [sb(es, "gsm%d" % i, [128, 8, 16]) for i in range(2)]
        zs = sb(es, "zs", [128, 16])
        idxf = sb(es, "idxf", [128, 128])
        IDX32 = [sb(es, "idx32_%d" % i, [128, 128], I32) for i in range(2)]
        actv = sb(es, "actv", [128, 128])
        gl = sb(es, "gl", [128, 128])
        tb = sb(es, "tb", [128, 3, 128], BF16)
        tT = sb(es, "tT", [128, 3, 128], BF16)
        idxT = sb(es, "idxT", [128, 128], I32)
        NB = 8
        Db = [sb(es, "Db%d" % i, [128, D], BF16) for i in range(NB)]
        Ub = [sb(es, "Ub%d" % i, [128, D], BF16) for i in range(NB)]
        Dslot = [S.new_slot("dsl%d" % i) for i in range(NB)]
        Uslot = [S.new_slot("usl%d" % i) for i in range(NB)]
        S.extra_slots = Dslot + Uslot
        pT = ps(es, "bpT", [128, 8, 128], BF16)
        pTb = ps(es, "bpTb", [128, 8, 128], BF16)
        pY = [ps(es, "bpY%d" % i, [128, 512]) for i in range(2)]
        pU = [ps(es, "bpU%d" % i, [128, 512]) for i in range(2)]
        pQ = [ps(es, "bpQ%d" % i, [128, 512]) for i in range(1)] * 2
        pS = [ps(es, "bpS%d" % i, [128, 512]) for i in range(1)] * 2

        if os.environ.get("MK_VERBOSE"):
            print("phase B sbuf bytes remaining per partition:", nc.sbuf_bytes_remaining)

        def layernorm(src, srck, dst, dstk, gi_, st=st, sm=sm, stk="bst", smk="bsm"):
            S.op("dve", lambda e: e.bn_stats(st[:, 0, :], src[:, 0:512]), reads=[srck], writes=[stk])
            S.op("dve", lambda e: e.bn_stats(st[:, 1, :], src[:, 512:1024]), reads=[srck], writes=[stk])
            S.op("dve", lambda e: e.bn_aggr(sm[:, 0:2], st[:].rearrange("p a b -> p (a b)")), reads=[stk], writes=[smk])
            S.op("act", lambda e: e.activation(out=sm[:, 2:3], in_=sm[:, 1:2], func=AF.Ln, bias=EPS), reads=[smk], writes=[smk])
            S.op("act", lambda e: e.activation(out=sm[:, 3:4], in_=sm[:, 2:3], func=AF.Exp, scale=-0.5), reads=[smk], writes=[smk])
            S.op("dve", lambda e: e.tensor_scalar(dst[:, :], src[:, :], sm[:, 0:1], sm[:, 3:4], op0=ALU.subtract, op1=ALU.mult),
                 reads=[srck, smk], writes=[dstk])
            S.op("dve", lambda e: e.tensor_tensor(out=dst[:, :], in0=dst[:, :], in1=LNb[:, gi_, :], op=ALU.mult),
                 reads=[dstk, ("LNb", gi_)], writes=[dstk])
            S.op("dve", lambda e: e.tensor_tensor(out=dst[:, :], in0=dst[:, :], in1=LNb[:, gi_ + 1, :], op=ALU.add),
                 reads=[dstk, ("LNb", gi_ + 1)], writes=[dstk])

        def front(jt):
            pb = jt % 2
            u = U2[pb]; h1 = u; uk = ("bu", pb)
            s1 = S1[pb]; s2 = S2[pb]; gsm = GSM[pb]; idx32 = IDX32[pb]
            s1k = ("s1", pb); s2k = ("s2", pb); gsmk = ("gsm", pb); idxk = ("idx32", pb)
            h1b = H1B[pb]; h1bk = ("h1b", pb)
            x_ = xt; xk = "bx"
            h_ = xt; hk = "bx"
            mx = mix; mxk = "mix"
            S.dma("sp", lambda e: e.dma_start(out=x_[:, :], in_=xo[jt * 128:(jt + 1) * 128, :]), writes=[xk])
            for r in range(4):
                S.dma("pool", lambda e, r=r: e.indirect_dma_start(out=mx[:, r, :], out_offset=None, in_=mixall[jt // 2],
                                                                   in_offset=bass.IndirectOffsetOnAxis(ap=idxm[:, jt * 4 + r:jt * 4 + r + 1], axis=0)),
                      reads=["idxm", ("mixall", jt // 2)], writes=[mxk])
            layernorm(x_, xk, h_, hk, 0)
            S.op("act", lambda e: e.activation(out=sq[:, :, :], in_=mx[:, :, 0:128], func=AF.Square, accum_out=sm[:, 4:5]),
                 reads=[mxk], writes=["sq", "bsm"])
            S.op("act", lambda e: e.activation(out=sq[:, :, :], in_=mx[:, :, 128:256], func=AF.Square, accum_out=sm[:, 5:6]),
                 reads=[mxk, "sq"], writes=["sq", "bsm"])
            S.op("act", lambda e: e.activation(out=sm[:, 6:8], in_=sm[:, 4:6], func=AF.Ln, scale=1.0 / 512, bias=EPS), reads=["bsm"], writes=["bsm"])
            S.op("act", lambda e: e.activation(out=sm[:, 6:8], in_=sm[:, 6:8], func=AF.Exp, scale=-0.5), reads=["bsm"], writes=["bsm"])
            S.op("dve", lambda e: e.tensor_scalar(mixn[:, :, 0:128], mx[:, :, 0:128], sm[:, 6:7], None, op0=ALU.mult),
                 reads=[mxk, "bsm"], writes=["mixn"])
            S.op("dve", lambda e: e.tensor_scalar(mixn[:, :, 128:256], mx[:, :, 128:256], sm[:, 7:8], None, op0=ALU.mult),
                 reads=[mxk, "bsm"], writes=["mixn"])
            mflat = mixn[:].rearrange("p a b -> p (a b)")
            for kc in range(8):
                S.op("pe", lambda e, kc=kc: e.transpose(pT[:, kc, :], mflat[:, kc * 128:(kc + 1) * 128], ident[:, :]),
                     reads=["mixn", "identB"], writes=["bpT"])
            S.op("act", lambda e: e.activation(out=mixT[:, :, :], in_=pT[:, :, :], func=AF.Identity), reads=["bpT"], writes=["mixT"])
            for hf in range(2):
                for kc in range(8):
                    S.op("pe", lambda e, hf=hf, kc=kc: e.matmul(pY[hf][:, :], mixT[:, kc, :], Wo[:, kc, hf * 512:(hf + 1) * 512],
                                                                  start=(kc == 0), stop=(kc == 7)),
                         reads=["mixT", ("Wo", kc)], writes=[("bpY", hf)])
            for hf in range(2):
                S.op("dve", lambda e, hf=hf: e.scalar_tensor_tensor(out=u[:, hf * 512:(hf + 1) * 512], in0=h_[:, hf * 512:(hf + 1) * 512], scalar=ALPHA,
                                                                     in1=pY[hf][:, :], op0=ALU.mult, op1=ALU.add),
                     reads=[hk, ("bpY", hf)], writes=[uk])
            layernorm(u, uk, u, uk, 2)
            S.op("act", lambda e: e.activation(out=h1b[:, :], in_=h1[:, :], func=AF.Identity), reads=[uk], writes=[h1bk])
            for kc in range(8):
                S.op("pe", lambda e, kc=kc: e.transpose(pT[:, kc, :], h1b[:, kc * 128:(kc + 1) * 128], ident[:, :]),
                     reads=[h1bk, "identB"], writes=["bpT"])
            S.op("act", lambda e: e.activation(out=h1T[:, :, :], in_=pT[:, :, :], func=AF.Identity), reads=["bpT"], writes=["mixT"])
            for c4 in range(4):
                pq = pQ[c4 % 2]; pqk = ("bpQ", 0)
                for cj in range(4):
                    cc = c4 * 4 + cj
                    for kc in range(8):
                        S.op("pe", lambda e, cc=cc, cj=cj, kc=kc, pq=pq: e.matmul(pq[:, cj * 128:(cj + 1) * 128], Wq[:, kc, cc * 128:(cc + 1) * 128], h1T[:, kc, :],
                                                                                   start=(kc == 0), stop=(kc == 7)),
                             reads=["mixT", ("Wq", kc)], writes=[pqk])
                eng = "act" if c4 % 2 == 0 else "dve"
                if eng == "act":
                    S.op("act", lambda e, c4=c4, pq=pq: e.activation(out=qsb[:, c4 * 4:(c4 + 1) * 4, :], in_=pq[:, :].rearrange("p (a b) -> p a b", a=4), func=AF.Identity),
                         reads=[pqk], writes=[("qsb", c4)])
                else:
                    S.op("dve", lambda e, c4=c4, pq=pq: e.tensor_copy(qsb[:, c4 * 4:(c4 + 1) * 4, :], pq[:, :].rearrange("p (a b) -> p a b", a=4)),
                         reads=[pqk], writes=[("qsb", c4)])
            for c4 in range(4):
                pz = pS[c4 % 2]; pzk = ("bpS", 0)
                for cj in range(4):
                    cc = c4 * 4 + cj
                    S.op("pe", lambda e, cc=cc, cj=cj, pz=pz: e.matmul(pz[:, cj * 128:(cj + 1) * 128], qsb[:, cc, :], SK[:, cc, :], start=True, stop=True),
                         reads=[("qsb", c4), "SK"], writes=[pzk])
                S.op("act", lambda e, c4=c4, pz=pz: e.activation(out=Ssb[:, c4 * 4:(c4 + 1) * 4, :], in_=pz[:, :].rearrange("p (a b) -> p a b", a=4), func=AF.Identity),
                     reads=[pzk], writes=[("Ssb", c4)])
            for hh in range(8):
                for half, (vv, ii) in enumerate(((v1, i1), (v2, i2))):
                    cc = 2 * hh + half
                    sk_ = ("Ssb", cc // 4)
                    S.op("dve", lambda e, cc=cc, vv=vv, hh=hh: e.max(out=vv[:, hh, 0:8], in_=Ssb[:, cc, :]), reads=[sk_], writes=["vv"])
                    S.op("dve", lambda e, cc=cc, vv=vv, hh=hh: e.match_replace(out=wk[:, :], in_to_replace=vv[:, hh, 0:8], in_values=Ssb[:, cc, :], imm_value=-1e30),
                         reads=[sk_, "vv"], writes=["wk"])
                    S.op("dve", lambda e, vv=vv, hh=hh: e.max(out=vv[:, hh, 8:16], in_=wk[:, :]), reads=["wk"], writes=["vv"])
                    S.op("dve", lambda e, cc=cc, vv=vv, ii=ii, hh=hh: e.max_index(out=ii[:, hh, 0:8], in_max=vv[:, hh, 0:8], in_values=Ssb[:, cc, :]),
                         reads=[sk_, "vv"], writes=["ii"])
                    S.op("dve", lambda e, cc=cc, vv=vv, ii=ii, hh=hh: e.max_index(out=ii[:, hh, 8:16], in_max=vv[:, hh, 8:16], in_values=Ssb[:, cc, :]),
                         reads=[sk_, "vv"], writes=["ii"])
            S.op("dve", lambda e: e.tensor_tensor(out=cand[:].rearrange("p h (a b) -> p h a b", a=16),
                                                  in0=APx(v1[:], [[16, 8], [1, 16], [0, 16]]),
                                                  in1=APx(v2[:], [[16, 8], [0, 16], [1, 16]]), op=ALU.add),
                 reads=["vv"], writes=["cand"])
            for hh in range(8):
                S.op("dve", lambda e, hh=hh: e.max(out=scv[:, hh, 0:8], in_=cand[:, hh, :]), reads=["cand"], writes=["scv"])
                S.op("dve", lambda e, hh=hh: e.match_replace(out=wk2[:, :], in_to_replace=scv[:, hh, 0:8], in_values=cand[:, hh, :], imm_value=-1e30),
                     reads=["cand", "scv"], writes=["wk2"])
                S.op("dve", lambda e, hh=hh: e.max(out=scv[:, hh, 8:16], in_=wk2[:, :]), reads=["wk2"], writes=["scv"])
                S.op("dve", lambda e, hh=hh: e.max_index(out=pos[:, hh, 0:8], in_max=scv[:, hh, 0:8], in_values=cand[:, hh, :]), reads=["cand", "scv"], writes=["pos"])
                S.op("dve", lambda e, hh=hh: e.max_index(out=pos[:, hh, 8:16], in_max=scv[:, hh, 8:16], in_values=cand[:, hh, :]), reads=["cand", "scv"], writes=["pos"])
            S.op("dve", lambda e: e.tensor_single_scalar(ph[:], pos[:], 4, op=ALU.logical_shift_right), reads=["pos"], writes=["ph"])
            S.op("dve", lambda e: e.tensor_copy(phf[:], ph[:]), reads=["ph"], writes=["phf"])
            S.op("dve", lambda e: e.tensor_single_scalar(ph[:], pos[:], 15, op=ALU.bitwise_and), reads=["pos", "phf"], writes=["ph"])
            S.op("dve", lambda e: e.tensor_copy(plf[:], ph[:]), reads=["ph"], writes=["plf"])
            S.op("dve", lambda e: e.tensor_copy(i1f[:], i1[:]), reads=["ii"], writes=["i1f"])
            S.op("dve", lambda e: e.tensor_copy(i2f[:], i2[:]), reads=["ii"], writes=["i2f"])
            for (pf, pfk, iif, iifk, so, sok) in ((phf, "phf", i1f, "i1f", s1, s1k), (plf, "plf", i2f, "i2f", s2, s2k)):
                S.op("dve", lambda e, pf=pf: e.tensor_tensor(out=eq[:], in0=APx(pf[:], [[16, 8], [1, 16], [0, 16]]),
                                                             in1=APx(iota[:], [[0, 8], [0, 16], [1, 16]]), op=ALU.is_equal),
                     reads=[pfk, "iota"], writes=["cand"])
                S.op("dve", lambda e, iif=iif: e.tensor_tensor(out=eq[:], in0=eq[:], in1=APx(iif[:], [[16, 8], [0, 16], [1, 16]]), op=ALU.mult),
                     reads=["cand", iifk], writes=["cand"])
                S.op("dve", lambda e, so=so: e.tensor_reduce(out=so[:], in_=eq[:], axis=AX.X, op=ALU.add), reads=["cand"], writes=[sok])
            S.op("dve", lambda e: e.scalar_tensor_tensor(out=idxf[:].rearrange("p (h k) -> p h k", h=8), in0=s1[:], scalar=128.0, in1=s2[:],
                                                         op0=ALU.mult, op1=ALU.add), reads=[s1k, s2k], writes=["idxf"])
            S.op("dve", lambda e: e.tensor_copy(idx32[:], idxf[:]), reads=["idxf"], writes=[idxk])
            S.op("dve", lambda e: e.tensor_tensor(out=gsm[:], in0=scv[:], in1=APx(scv[:], [[16, 8], [0, 16]]), op=ALU.subtract),
                 reads=["scv"], writes=[gsmk])
            S.op("act", lambda e: e.activation(out=gsm[:], in_=gsm[:], func=AF.Exp), reads=[gsmk], writes=[gsmk])
            S.op("dve", lambda e: e.tensor_reduce(out=zs[:, 0:8], in_=gsm[:], axis=AX.X, op=ALU.add), reads=[gsmk], writes=["zs"])
            S.op("dve", lambda e: e.reciprocal(zs[:, 8:16], zs[:, 0:8]), reads=["zs"], writes=["zs"])
            S.op("dve", lambda e: e.tensor_tensor(out=gsm[:], in0=gsm[:], in1=APx(zs[:, 8:16], [[1, 8], [0, 16]]), op=ALU.mult),
                 reads=[gsmk, "zs"], writes=[gsmk])

        def back(jt, cap, prev_tail):
            pb = jt % 2
            u = U2[pb]; h1 = u; uk = ("bu", pb)
            s1 = S1[pb]; s2 = S2[pb]; gsm = GSM[pb]; idx32 = IDX32[pb]
            s1k = ("s1", pb); s2k = ("s2", pb); gsmk = ("gsm", pb); idxk = ("idx32", pb)
            h1b = H1B[pb]; h1bk = ("h1b", pb)
            nrep = (len(cap) + 239) // 240 if cap else 0
            def prep_idxT():
              S.op("dve", lambda e: e.tensor_copy(tb[:, 1, :], s1[:].rearrange("p h k -> p (h k)")), reads=[s1k], writes=["tb12"])
              S.op("dve", lambda e: e.tensor_copy(tb[:, 2, :], s2[:].rearrange("p h k -> p (h k)")), reads=[s2k], writes=["tb12"])
              for j in (1, 2):
                S.op("pe", lambda e, j=j: e.transpose(pTb[:, j, :], tb[:, j, :], ident[:, :]), reads=["tb12", "identB"], writes=["bpTb"])
              S.op("act", lambda e: e.activation(out=tT[:, 1:3, :], in_=pTb[:, 1:3, :], func=AF.Identity), reads=["bpTb"], writes=["tT12"])
              S.op("dve", lambda e: e.scalar_tensor_tensor(out=idxT[:], in0=tT[:, 1, :], scalar=128.0, in1=tT[:, 2, :], op0=ALU.mult, op1=ALU.add),
                 reads=["tT12"], writes=["idxT"])
            S.op("dve", lambda e: e.memset(actv[:], 0.0), writes=["actv"])
            dot_ev = {}
            for s_ in range(128):
                db = Db[s_ % NB]; dk = ("Db", s_ % NB)
                if s_ % 2 == 0 and s_ - (NB - 1) in dot_ev:
                    S._wait("pool", dot_ev[s_ - (NB - 1)])
                S.dma("pool", lambda e, s_=s_, db=db: e.indirect_dma_start(out=db[:, :], out_offset=None, in_=downb,
                                                                            in_offset=bass.IndirectOffsetOnAxis(ap=idx32[:, s_:s_ + 1], axis=0)),
                      reads=[idxk] + tabk_d, writes=[dk], slot=Dslot[s_ % NB])
                dot_ev[s_] = S.op("dve", lambda e, s_=s_, db=db: e.scalar_tensor_tensor(out=junk[:, :], in0=db[:, :], scalar=1.0, in1=h1b[:, :], op0=ALU.mult, op1=ALU.mult,
                                                                            accum_out=actv[:, s_:s_ + 1]),
                     reads=[dk, h1bk, "actv"], writes=["junk", ("actv", s_)])
                if s_ == 12:
                    prep_idxT()
                if prev_tail and s_ >= 4:
                    S.replay(prev_tail, 1)
                if cap and s_ % 2 == 1:
                    S.replay(cap, 1)
            if prev_tail:
                S.replay(prev_tail, len(prev_tail))
            akeys = [("actv", s_) for s_ in range(128)]
            S.op("dve", lambda e: e.tensor_tensor(out=gl[:], in0=actv[:], in1=actv[:], op=ALU.mult), reads=akeys + ["actv"], writes=["gl"])
            S.op("dve", lambda e: e.tensor_scalar(gl[:], gl[:], 0.044715, 1.0, op0=ALU.mult, op1=ALU.add), reads=["gl"], writes=["gl"])
            S.op("dve", lambda e: e.tensor_tensor(out=gl[:], in0=gl[:], in1=actv[:], op=ALU.mult), reads=["gl"], writes=["gl"])
            S.op("act", lambda e: e.activation(out=gl[:], in_=gl[:], func=AF.Exp, scale=-GELU_C), reads=["gl"], writes=["gl"])
            S.op("dve", lambda e: e.tensor_scalar(gl[:], gl[:], 1.0, None, op0=ALU.add), reads=["gl"], writes=["gl"])
            S.op("dve", lambda e: e.reciprocal(gl[:], gl[:]), reads=["gl"], writes=["gl"])
            S.op("dve", lambda e: e.tensor_tensor(out=gl[:], in0=gl[:], in1=actv[:], op=ALU.mult), reads=["gl"], writes=["gl"])
            S.op("dve", lambda e: e.tensor_tensor(out=tb[:, 0, :], in0=gl[:], in1=gsm[:].rearrange("p h k -> p (h k)"), op=ALU.mult),
                 reads=["gl", gsmk], writes=["tb"])
            S.op("pe", lambda e: e.transpose(pTb[:, 0, :], tb[:, 0, :], ident[:, :]), reads=["tb", "identB"], writes=["bpTb"])
            S.op("act", lambda e: e.activation(out=tT[:, 0, :], in_=pTb[:, 0, :], func=AF.Identity), reads=["bpTb"], writes=["tT"])
            S.op("dve", lambda e: e.tensor_copy(APx(WD[:], [[129, 128]]), tT[:, 0, :]), reads=["tT"], writes=["WD"])
            mm_ev = {}
            nrep_up = (len(cap) + 99) // 100 if cap else 0
            for t_ in range(128):
                ub = Ub[t_ % NB]; ubk = ("Ub", t_ % NB)
                if t_ % 2 == 0 and t_ - (NB - 1) in mm_ev:
                    S._wait("pool", mm_ev[t_ - (NB - 1)])
                S.dma("pool", lambda e, t_=t_, ub=ub: e.indirect_dma_start(out=ub[:, :], out_offset=None, in_=upb,
                                                                            in_offset=bass.IndirectOffsetOnAxis(ap=idxT[:, t_:t_ + 1], axis=0)),
                      reads=["idxT"] + tabk_u, writes=[ubk], slot=Uslot[t_ % NB])
                for hf in range(2):
                    mm_ev[t_] = S.op("pe", lambda e, t_=t_, ub=ub, hf=hf: e.matmul(pU[hf][:, :], WD[:, t_ * 128:(t_ + 1) * 128], ub[:, hf * 512:(hf + 1) * 512],
                                                                        start=(t_ == 0), stop=(t_ == 127)),
                         reads=["WD", ubk], writes=[("bpU", hf)])
                if cap:
                    S.replay(cap, nrep_up)
            if cap:
                S.replay(cap, len(cap))
            S.capture_begin()
            for hf in range(2):
                S.op("dve", lambda e, hf=hf: e.scalar_tensor_tensor(out=ot[:, hf * 512:(hf + 1) * 512], in0=h1[:, hf * 512:(hf + 1) * 512], scalar=ALPHA,
                                                                     in1=pU[hf][:, :], op0=ALU.mult, op1=ALU.add),
                     reads=[uk, ("bpU", hf)], writes=["ot"])
            layernorm(ot, "ot", ot, "ot", 4, st=st2, sm=sm2, stk="bst2", smk="bsm2")
            S.dma("sp", lambda e: e.dma_start(out=out[jt * 128:(jt + 1) * 128, :], in_=ot[:, :]), reads=["ot"], writes=["out"])
            return S.capture_end()
        S.capture_begin()
        front(0)
        cap0 = S.capture_end()
        S.replay(cap0, 5)
        load_weights()
        S.replay(cap0, len(cap0))
        tail = None
        for jt in range(NTB):
            cap = None
            if jt + 1 < NTB:
                S.capture_begin()
                front(jt + 1)
                cap = S.capture_end()
            tail = back(jt, cap, tail)
        S.replay(tail, len(tail))
        S.fence("sp")
        S.emit()


def own_tiles(g):
    return [8 * (jt // 2) + 2 * g + (jt % 2) for jt in range(NTB)]


def host_inputs(x, meta_tokens, ln_emb_g, ln_emb_b, w_in, conv_w, conv_b, gate_a_w, gate_a_b,
                gate_x_w, gate_x_b, lru_lambda, attn_norm_g, lru_norm_g, w_out, ln1_g, ln1_b,
                peer_query_w, peer_sub_keys, peer_down, peer_up, ln2_g, ln2_b):
    f = lambda a: np.ascontiguousarray(np.asarray(a, dtype=np.float32))
    x = f(x); meta_tokens = f(meta_tokens)
    w_in0 = f(w_in)[0]
    lnE_T = np.concatenate([f(ln_emb_g).reshape(8, 128).T, f(ln_emb_b).reshape(8, 128).T], axis=1)
    p = np.arange(128)
    ident = np.eye(128, dtype=np.float32)
    negtri = -(p[:, None] >= p[None, :]).astype(np.float32)
    negones = -np.ones((128, 128), np.float32)
    j = np.arange(512)
    masks = [((j[None, :] - 128 * r - p[:, None]) > 0).astype(np.float32) for r in range(4)]
    masks.append(np.broadcast_to((p[:, None] < NMETA), (128, 512)).astype(np.float32))
    cst = np.ascontiguousarray(np.concatenate([ident, negtri, negones] + masks, axis=1))
    iota16 = np.ascontiguousarray(np.broadcast_to(np.arange(16, dtype=np.float32), (128, 16)))
    w_out0 = f(w_out)[0]
    rows = []
    gains = np.zeros((128, 8), np.float32)
    ag = f(attn_norm_g)[0]; lg = f(lru_norm_g)[0]
    for r in range(4):
        rows.append(w_out0[128 * r:128 * (r + 1)])
        rows.append(w_out0[512 + 128 * r:512 + 128 * (r + 1)])
        gains[:, 2 * r] = ag[128 * r:128 * (r + 1)]
        gains[:, 2 * r + 1] = lg[128 * r:128 * (r + 1)]
    w_out_p = np.ascontiguousarray(np.concatenate(rows, axis=0))
    lnB = np.ascontiguousarray(np.stack([f(ln_emb_g), f(ln_emb_b), f(ln1_g)[0], f(ln1_b)[0], f(ln2_g)[0], f(ln2_b)[0]], axis=0))
    wq = np.ascontiguousarray(f(peer_query_w)[0].reshape(D, 2048))
    sk = f(peer_sub_keys)[0]
    skT = np.ascontiguousarray(sk.transpose(3, 0, 1, 2).reshape(128, 2048))
    down = f(peer_down)[0]; upm = f(peer_up)[0]
    cw = f(conv_w)[0]; cb = f(conv_b)[0]; gab = f(gate_a_b)[0]; gxb = f(gate_x_b)[0]; lam = f(lru_lambda)[0]
    gaw = f(gate_a_w)[0]; gxw = f(gate_x_w)[0]
    in_maps = []
    for c in range(8):
        b, g = divmod(c, 4)
        cols = np.concatenate([np.arange(base + 128 * g, base + 128 * (g + 1)) for base in (0, 512, 1024, 1536, 2048)])
        ch = slice(128 * g, 128 * (g + 1))
        lru_p = np.stack([cw[0, ch], cw[1, ch], cw[2, ch], cw[3, ch], cb[ch], gab[ch], gxb[ch], lam[ch]], axis=1)
        gate_w = np.zeros((128, 256), np.float32)
        for hh in range(2):
            gate_w[64 * hh:64 * hh + 64, 64 * hh:64 * hh + 64] = gaw[2 * g + hh]
            gate_w[64 * hh:64 * hh + 64, 128 + 64 * hh:128 + 64 * hh + 64] = gxw[2 * g + hh]
        idxm = np.zeros((128, 64), np.int32)
        for jt in range(NTB):
            for r in range(4):
                idxm[:, jt * 4 + r] = r * 1024 + (2 * g + jt % 2) * 128 + p
        in_maps.append({
            "xb": x[b], "xo": np.ascontiguousarray(x[b].reshape(NT, 128, D)[own_tiles(g)].reshape(NTB * 128, D)), "meta": meta_tokens,
            "lnE_T": np.ascontiguousarray(lnE_T), "w_in": np.ascontiguousarray(w_in0[:, cols]),
            "lru_p": np.ascontiguousarray(lru_p), "gate_w": gate_w, "cst": cst, "identf": ident,
            "w_out_p": w_out_p, "gmix": gains, "lnB": lnB, "wq": wq, "skT": skT, "down": down, "up": upm,
            "idxm": idxm, "iota16": iota16,
        })
    return in_maps


_NC_CACHE = {}
FUSED = True
A_KEYS = ["xb", "meta", "lnE_T", "w_in", "lru_p", "gate_w", "cst", "identf"]
B_KEYS = ["xo", "cst", "w_out_p", "gmix", "lnB", "wq", "skT", "down", "up", "idxm", "iota16"]


def kernel(**inputs):
    debug = bool(os.environ.get("MK_DEBUG"))
    in_maps = host_inputs(**inputs)
    if not FUSED:
        if "A" not in _NC_CACHE:
            _NC_CACHE["A"] = build(False, mode="A")
            _NC_CACHE["B"] = build(False, mode="B")
        ra = run_bass_kernel_spmd(_NC_CACHE["A"], [{k: m[k] for k in A_KEYS} for m in in_maps], core_ids=list(range(8)))
        mapsb = []
        for c in range(8):
            b = c // 4
            m = {k: in_maps[c][k] for k in B_KEYS}
            for i in range(8):
                m["mixall%d" % i] = np.ascontiguousarray(np.concatenate([ra.results[4 * b + r]["mixbuf%d" % i] for r in range(4)], axis=0))
            mapsb.append(m)
        res = run_bass_kernel_spmd(_NC_CACHE["B"], mapsb, core_ids=list(range(8)))
        outp = np.zeros((2, SEQ, D), np.float32)
        for c in range(8):
            b, g = divmod(c, 4)
            outp[b].reshape(NT, 128, D)[own_tiles(g)] = res.results[c]["out"].reshape(NTB, 128, D)
        return outp
    if debug not in _NC_CACHE:
        _NC_CACHE[debug] = build(debug)
    nc = _NC_CACHE[debug]
    res = run_bass_kernel_spmd(nc, in_maps, core_ids=list(range(8)))
    outp = np.zeros((2, SEQ, D), np.float32)
    for c in range(8):
        b, g = divmod(c, 4)
        outp[b].reshape(NT, 128, D)[own_tiles(g)] = res.results[c]["out"].reshape(NTB, 128, D)
    if debug:
        kernel.dbg = [res.results[c]["dbg"] for c in range(8)]
    return outp
```

```python
import os
from contextlib import ExitStack
import numpy as np
import concourse.bass as bass
import concourse.mybir as mybir
from concourse.bass_utils import run_bass_kernel_spmd

F32 = mybir.dt.float32
BF16 = mybir.dt.bfloat16
I32 = mybir.dt.int32
U32 = mybir.dt.uint32
ALU = mybir.AluOpType
AF = mybir.ActivationFunctionType
AX = mybir.AxisListType

D = 1024
SEQ = 8192
NMETA = 16
T = SEQ + NMETA
NT = 64
EPS = 1e-5
ALPHA = 2.0 ** 0.25
NTB = 16
NSLOT = 6
GELU_C = 1.5957691216057308


class Sched:
    def __init__(self, nc, es):
        self.nc = nc
        self.es = es
        self.epoch = 0
        self.engs = ["pe", "act", "dve", "pool", "sp"]
        self.csem = {e: es.enter_context(nc.semaphore("c_" + e)) for e in self.engs}
        self.csem_ids = {id(v) for v in self.csem.values()}
        self.cnt = {e: 0 for e in self.engs}
        self.slots = {}
        for e in ["sp", "pool", "act"]:
            self.slots[e] = [[es.enter_context(nc.semaphore("d_%s%d" % (e, i))), 0] for i in range(NSLOT)]
        self.slot_next = {e: 0 for e in self.slots}
        self.ccsem = es.enter_context(nc.semaphore("ccsem"))
        self.cccnt = 0
        self.ops = {e: [] for e in self.engs}
        self.waited = {e: {} for e in self.engs}
        self.last_w = {}
        self.readers = {}
        self.nwaits = 0

    def _wait(self, eng, ev):
        sem, val = ev
        k = id(sem)
        if self.waited[eng].get(k, 0) >= val:
            return
        self.waited[eng][k] = val
        self.ops[eng].append(("wait", sem, val))
        self.nwaits += 1

    PSUM_KEYS = {"pT", "pP", "pV", "pG", "pL", "pZ", "pE", "pO", "pX", "bpT", "bpY", "bpQ", "bpS", "bpU", "bpTb"}

    def _split(self, reads, writes):
        r2 = []; w2 = list(writes)
        for b in reads:
            nm = b[0] if isinstance(b, tuple) else b
            if nm in self.PSUM_KEYS:
                if b not in w2:
                    w2.append(b)
            else:
                r2.append(b)
        return r2, w2

    def _deps(self, eng, reads, writes):
        deps = []
        for b in reads:
            if b in self.last_w:
                deps.append(self.last_w[b])
        for b in writes:
            rd = self.readers.get(b, ())
            if rd:
                deps.extend(rd)
            elif b in self.last_w:
                deps.append(self.last_w[b])
        for ev in deps:
            if eng == "pe" and ev[0] is self.csem["pe"]:
                continue
            self._wait(eng, ev)

    def _commit(self, ev, reads, writes):
        for b in writes:
            self.last_w[b] = ev
            self.readers[b] = []
        for b in reads:
            self.readers.setdefault(b, []).append(ev)

    cur_level = 0

    def lv(self, level):
        self.cur_level = level

    def capture_begin(self):
        self.cap = []

    def capture_end(self):
        c = self.cap
        self.cap = None
        return c

    def replay(self, lst, n):
        for _ in range(min(n, len(lst))):
            kind, eng, fn, reads, writes = lst.pop(0)[:5]
            (self.op if kind == "op" else self.dma)(eng, fn, reads, writes)

    def op(self, eng, fn, reads=(), writes=()):
        if getattr(self, "cap", None) is not None:
            self.cap.append(("op", eng, fn, list(reads), list(writes), self.cur_level))
            return None
        reads, writes = self._split(reads, writes)
        self._deps(eng, reads, writes)
        self.cnt[eng] += 1
        ev = (self.csem[eng], self.cnt[eng])
        self.ops[eng].append(("op", fn, ev[0], 1))
        self._commit(ev, reads, writes)
        return ev

    def new_slot(self, name):
        return [self.es.enter_context(self.nc.semaphore(name)), 0]

    def dma(self, eng, fn, reads=(), writes=(), slot=None):
        if getattr(self, "cap", None) is not None:
            assert slot is None
            self.cap.append(("dma", eng, fn, list(reads), list(writes), self.cur_level))
            return None
        reads, writes = self._split(reads, writes)
        self._deps(eng, reads, writes)
        if slot is None:
            i = self.slot_next[eng]
            self.slot_next[eng] = (i + 1) % NSLOT
            slot = self.slots[eng][i]
            if slot[1] > 0:
                self._wait(eng, (slot[0], slot[1]))
            self.all_slots_touch = True
        slot[1] += 16
        ev = (slot[0], slot[1])
        self.ops[eng].append(("op", fn, ev[0], 16))
        self._commit(ev, reads, writes)
        return ev

    def cc(self, fn, reads=(), writes=()):
        eng = "pool"
        self._deps(eng, reads, writes)
        self.cccnt += 1
        ev = (self.ccsem, self.cccnt)
        self.ops[eng].append(("op", fn, ev[0], 1))
        self._commit(ev, reads, writes)
        return ev

    def fence(self, eng="sp"):
        for e in self.engs:
            if self.cnt[e] > 0:
                self._wait(eng, (self.csem[e], self.cnt[e]))
        for e in self.slots:
            for s in self.slots[e]:
                if s[1] > 0:
                    self._wait(eng, (s[0], s[1]))
        for s in getattr(self, "extra_slots", []):
            if s[1] > 0:
                self._wait(eng, (s[0], s[1]))
        if self.cccnt:
            self._wait(eng, (self.ccsem, self.cccnt))

    def emit(self):
        nc = self.nc
        ops = self.ops

        def mk(name):
            def body(e):
                for it in ops[name]:
                    if it[0] == "wait":
                        e.wait_ge(it[1], it[2])
                    else:
                        it[1](e).then_inc(it[2], it[3])
            return body

        with nc.Block() as block:
            block.tensor(mk("pe"))
            block.scalar(mk("act"))
            block.vector(mk("dve"))
            block.gpsimd(mk("pool"))
            block.sync(mk("sp"))
        self.ops = {e: [] for e in self.engs}
        self.new_epoch()

    def new_epoch(self):
        self.epoch += 1
        old_ids = self.csem_ids
        self.csem = {e: self.es.enter_context(self.nc.semaphore("c%d_%s" % (self.epoch, e))) for e in self.engs}
        self.csem_ids = old_ids | {id(v) for v in self.csem.values()}
        self.cnt = {e: 0 for e in self.engs}
        keep = lambda ev: id(ev[0]) not in old_ids
        self.last_w = {b: ev for b, ev in self.last_w.items() if keep(ev)}
        self.readers = {b: [ev for ev in evs if keep(ev)] for b, evs in self.readers.items()}


def APx(base, free, off=0):
    p = base.ap[0]
    return bass.AP(base.tensor, base.offset + off, [list(p)] + [list(f) for f in free])


def build(debug=False, stop=None, mode="fused"):
    stop = stop or os.environ.get('MK_STOP')
    if mode == "A":
        stop = "A2"
    nc = bass.Bass("TRN2", target_bir_lowering=False)

    A_IN = {"xb", "meta", "lnE_T", "w_in", "lru_p", "gate_w", "cst", "identf"}
    B_IN = {"xo", "cst", "w_out_p", "gmix", "lnB", "wq", "skT", "down", "up", "idxm", "iota16"}

    def din(name, shape, dt=F32):
        if (mode == "A" and name not in A_IN) or (mode == "B" and name not in B_IN):
            return None
        return nc.dram_tensor(name, list(shape), dt, kind="ExternalInput").ap()

    xb = din("xb", [SEQ, D])
    xo = din("xo", [NTB * 128, D])
    meta = din("meta", [NMETA, D])
    lnE_T = din("lnE_T", [128, 16])
    w_in = din("w_in", [D, 640])
    lru_p = din("lru_p", [128, 8])
    gate_w = din("gate_w", [128, 256])
    cst = din("cst", [128, 384 + 5 * 512])
    identf_d = din("identf", [128, 128])
    w_out_p = din("w_out_p", [D, D])
    gmix_d = din("gmix", [128, 8])
    lnB = din("lnB", [6, D])
    wq_d = din("wq", [D, 2048])
    skT_d = din("skT", [128, 2048])
    down = din("down", [16384, D])
    up = din("up", [16384, D])
    idxm_d = din("idxm", [128, 64], I32)
    iota_d = din("iota16", [128, 16])
    out = None if mode == "A" else nc.dram_tensor("out", [NTB * 128, D], F32, kind="ExternalOutput").ap()
    mixbuf = [nc.dram_tensor("mixbuf%d" % i, [1024, 256], F32, kind=("ExternalOutput" if mode == "A" else "Internal")).ap() for i in range(8)]
    mixall = [nc.dram_tensor("mixall%d" % i, [4 * 1024, 256], F32, kind=("ExternalInput" if mode == "B" else "Internal")).ap() for i in range(8)]

    def mixdst(ti):
        r0 = (ti % 8) * 128
        return mixbuf[ti // 8][r0:r0 + 128, :]
    downb = upb = None
    if mode != "A":
        downb = nc.dram_tensor("downb", [16384, D], BF16, kind="Internal").ap()
        upb = nc.dram_tensor("upb", [16384, D], BF16, kind="Internal").ap()
    conv_jobs = []
    if mode != "A":
        for j in range(8):
            conv_jobs.append((down, downb, "downb", j))
            conv_jobs.append((up, upb, "upb", j))

    def convert_step(S, n=1):
        for _ in range(n):
            if not conv_jobs:
                return
            src, dst, nm, j = conv_jobs.pop(0)
            S.dma("pool", lambda e, src=src, dst=dst, j=j: e.dma_start(out=dst[j * 2048:(j + 1) * 2048, :], in_=src[j * 2048:(j + 1) * 2048, :]),
                  writes=[(nm, j)])

    dbg = None
    if debug:
        dbg = nc.dram_tensor("dbg", [SEQ, 256], F32, kind="ExternalOutput").ap()

    top = ExitStack()
    with top:
        S = Sched(nc, top)

        def sb(es, name, shape, dt=F32):
            return es.enter_context(nc.sbuf_tensor(name, list(shape), dt))

        def ps(es, name, shape, dt=F32):
            return es.enter_context(nc.psum_tensor(name, list(shape), dt))

        esA = ExitStack()
        if mode == "B":
            build_phase_b(nc, S, sb, ps, locals())
            return nc
        with esA:
            ident = sb(esA, "ident", [128, 128], BF16)
            identf = sb(esA, "identf_s", [128, 128], F32)
            negtri = sb(esA, "negtri", [128, 128], BF16)
            negones = sb(esA, "negones", [128, 128], BF16)
            masks = sb(esA, "masks", [128, 5, 512], BF16)
            lnT = sb(esA, "lnT", [128, 16])
            lrup = sb(esA, "lrup", [128, 8])
            lruc = sb(esA, "lruc", [128, 8])
            gw = sb(esA, "gw", [128, 256], BF16)
            qT0 = sb(esA, "qT0", [128, T], BF16)
            qT1 = sb(esA, "qT1", [128, T], BF16)
            kT = sb(esA, "kT", [128, T], BF16)
            vtok = sb(esA, "vtok", [128, NT + 1, 128], BF16)

            S.dma("pool", lambda e: e.dma_start(out=ident[:], in_=cst[:, 0:128]), writes=["ident"])
            S.dma("pool", lambda e: e.dma_start(out=negtri[:], in_=cst[:, 128:256]), writes=["negtri"])
            S.dma("pool", lambda e: e.dma_start(out=negones[:], in_=cst[:, 256:384]), writes=["negones"])
            S.dma("pool", lambda e: e.dma_start(out=masks[:], in_=cst[:, 384:].rearrange("p (r c) -> p r c", r=5)),
                  writes=["masks"])
            S.dma("pool", lambda e: e.dma_start(out=gw[:], in_=gate_w), writes=["gw"])
            S.dma("sp", lambda e: e.dma_start(out=identf[:], in_=identf_d), writes=["identf"])
            S.dma("sp", lambda e: e.dma_start(out=lnT[:], in_=lnE_T), writes=["lnT"])
            S.dma("sp", lambda e: e.dma_start(out=lrup[:], in_=lru_p), writes=["lrup"])
            S.op("dve", lambda e: e.memset(qT0[:], 0.0), writes=["qT0"])
            S.op("pool", lambda e: e.memset(qT1[:], 0.0), writes=["qT1"])
            S.op("dve", lambda e: e.memset(vtok[:, NT, :], 0.0), writes=["vtok"])
            S.op("act", lambda e: e.activation(out=lruc[:, 2:3], in_=lrup[:, 7:8], func=AF.Exp, scale=-1.0),
                 reads=["lrup"], writes=["lruc"])
            S.op("act", lambda e: e.activation(out=lruc[:, 3:4], in_=lruc[:, 2:3], func=AF.Ln, bias=1.0),
                 reads=["lruc"], writes=["lruc"])
            S.op("dve", lambda e: e.tensor_scalar(lruc[:, 0:1], lruc[:, 3:4], -8.0, None, op0=ALU.mult),
                 reads=["lruc"], writes=["lruc"])
            S.op("dve", lambda e: e.tensor_scalar(lruc[:, 1:2], lruc[:, 3:4], -16.0, None, op0=ALU.mult),
                 reads=["lruc"], writes=["lruc"])
            S.op("dve", lambda e: e.tensor_scalar(lruc[:, 4:6], lrup[:, 5:7], -1.0, None, op0=ALU.mult),
                 reads=["lrup", "lruc"], writes=["lruc"])

            esA1 = ExitStack()
            with esA1:
                W = sb(esA1, "W", [128, 8, 640], BF16)
                NXB = 5
                NB6 = 8
                xt = [sb(esA1, "xt%d" % i, [128, D]) for i in range(NXB)]
                xn = [sb(esA1, "xn%d" % i, [128, D], BF16) for i in range(3)]
                hT = [sb(esA1, "hT%d" % i, [128, 8, 128], BF16) for i in range(3)]
                st = [sb(esA1, "st%d" % i, [128, 2, 6]) for i in range(4)]
                mv = [sb(esA1, "mv%d" % i, [128, 4]) for i in range(4)]
                xrb = [sb(esA1, "xrb%d" % i, [128, 3 + 128]) for i in range(2)]
                grb = [sb(esA1, "grb%d" % i, [128, 128]) for i in range(NB6)]
                L = {}
                for nm in ["xc", "e1", "e2", "ea", "ea2", "bi", "hh", "g1", "g2", "lo"]:
                    L[nm] = [sb(esA1, "L%s%d" % (nm, i), [128, 128]) for i in range(NB6)]
                xcb = [sb(esA1, "xcb%d" % i, [128, 128], BF16) for i in range(3)]
                hst = sb(esA1, "hst", [128, 1])
                lstage = [sb(esA1, "lstage%d" % i, [128, 128]) for i in range(3)]
                pT = [ps(esA1, "pT%d" % i, [128, 8, 128], BF16) for i in range(2)]
                pP = [ps(esA1, "pP%d" % i, [128, 512]) for i in range(2)]
                pV = [ps(esA1, "pV%d" % i, [128, 512]) for i in range(1)]
                pG = [ps(esA1, "pG%d" % i, [128, 512]) for i in range(2)]
                pL = ps(esA1, "pL", [128, 128])

                for kc in range(8):
                    S.dma("pool", lambda e, kc=kc: e.dma_start(out=W[:, kc, :], in_=w_in[kc * 128:(kc + 1) * 128, :]),
                          writes=[("W", kc)])
                S.op("dve", lambda e: e.memset(xrb[0][:], 0.0), writes=[("xrbh", 0), ("xrb", 0)])
                S.op("dve", lambda e: e.memset(hst[:], 0.0), writes=["hst"])

                tiles = [-1] + list(range(NT))

                def tinfo(ti):
                    if ti < 0:
                        return NMETA, 0, NT
                    return 128, NMETA + 128 * ti, ti

                def tile_prog(k):
                    ti = tiles[k]
                    n, c0, vs = tinfo(ti)
                    x_ = xt[k % NXB]; xk = ("xt", k % NXB)
                    s_ = st[k % 4]; m_ = mv[k % 4]
                    sk = ("st", k % 4); mk_ = ("mv", k % 4)
                    S.lv(0)
                    if ti < 0:
                        S.dma("sp", lambda e: e.dma_start(out=x_[0:n, :], in_=meta), writes=[xk])
                    else:
                        S.dma("sp", lambda e: e.dma_start(out=x_[:, :], in_=xb[ti * 128:(ti + 1) * 128, :]), writes=[xk])
                    S.lv(1)
                    S.op("dve", lambda e: e.bn_stats(s_[0:n, 0, :], x_[0:n, 0:512]), reads=[xk], writes=[sk])
                    S.op("dve", lambda e: e.bn_stats(s_[0:n, 1, :], x_[0:n, 512:1024]), reads=[xk], writes=[sk])
                    S.op("dve", lambda e: e.bn_aggr(m_[0:n, 0:2], s_[0:n].rearrange("p a b -> p (a b)")), reads=[sk], writes=[mk_])
                    S.lv(2)
                    S.op("act", lambda e: e.activation(out=m_[0:n, 2:3], in_=m_[0:n, 1:2], func=AF.Ln, bias=EPS),
                         reads=[mk_], writes=[mk_])
                    S.op("act", lambda e: e.activation(out=m_[0:n, 3:4], in_=m_[0:n, 2:3], func=AF.Exp, scale=-0.5),
                         reads=[mk_], writes=[mk_])
                    S.lv(3)
                    xn_ = xn[k % 3]; xnk = ("xn", k % 3)
                    S.op("dve", lambda e: e.tensor_scalar(xn_[0:n, :], x_[0:n, :], m_[0:n, 0:1], m_[0:n, 3:4],
                                                          op0=ALU.subtract, op1=ALU.mult), reads=[xk, mk_], writes=[xnk])
                    S.lv(4)
                    pT_ = pT[k % 2]; pk = ("pT", k % 2)
                    for kc in range(8):
                        S.op("pe", lambda e, kc=kc: e.transpose(pT_[:, kc, 0:n], xn_[0:n, kc * 128:(kc + 1) * 128], ident[0:n, 0:n]),
                             reads=[xnk, "ident"], writes=[pk])
                    S.lv(5)
                    h_ = hT[k % 3]; hk = ("hT", k % 3)
                    for kc in range(8):
                        S.op("act", lambda e, kc=kc: e.activation(out=h_[:, kc, 0:n], in_=pT_[:, kc, 0:n], func=AF.Identity,
                                                                    scale=lnT[:, kc:kc + 1], bias=lnT[:, 8 + kc:9 + kc]),
                             reads=[pk, "lnT"], writes=[hk])
                    S.lv(6)
                    pP_ = pP[k % 2]; ppk = ("pP", k % 2)
                    pV_ = pV[0]; pvk = ("pV", 0)
                    wk = [("W", kc) for kc in range(8)]
                    for j, cb in enumerate([0, 128, 384, 512]):
                        for kc in range(8):
                            S.op("pe", lambda e, j=j, cb=cb, kc=kc: e.matmul(pP_[:, j * 128:j * 128 + n], W[:, kc, cb:cb + 128], h_[:, kc, 0:n],
                                                                               start=(kc == 0), stop=(kc == 7)),
                                 reads=[hk] + wk, writes=[ppk])
                    for kc in range(8):
                        S.op("pe", lambda e, kc=kc: e.matmul(pV_[0:n, 0:128], h_[:, kc, 0:n], W[:, kc, 256:384],
                                                             start=(kc == 0), stop=(kc == 7)),
                             reads=[hk] + wk, writes=[pvk])
                    S.lv(7)
                    xr_ = xrb[k % 2]; xrn = xrb[(k + 1) % 2]
                    g_ = grb[k % NB6]; gk = ("grb", k % NB6)
                    S.op("act", lambda e: e.activation(out=qT0[0:64, c0:c0 + n], in_=pP_[0:64, 0:n], func=AF.Identity, scale=0.125),
                         reads=[ppk], writes=["qT0"])
                    S.op("act", lambda e: e.activation(out=qT1[64:128, c0:c0 + n], in_=pP_[64:128, 0:n], func=AF.Identity, scale=0.125),
                         reads=[ppk], writes=["qT1"])
                    S.op("act", lambda e: e.activation(out=kT[:, c0:c0 + n], in_=pP_[:, 128:128 + n], func=AF.Identity), reads=[ppk], writes=["kT"])
                    S.op("act", lambda e: e.activation(out=xr_[:, 3:3 + n], in_=pP_[:, 256:256 + n], func=AF.Identity),
                         reads=[ppk], writes=[("xrb", k % 2)])
                    S.op("act", lambda e: e.activation(out=g_[:, 0:n], in_=pP_[:, 384:384 + n], func=AF.Identity),
                         reads=[ppk], writes=[gk])
                    S.op("dve", lambda e: e.tensor_copy(vtok[0:n, vs, :], pV_[0:n, 0:128]), reads=[pvk], writes=["vtok"])
                    S.lv(8)
                    rk = [("xrb", k % 2), ("xrbh", k % 2)]
                    b = k % NB6
                    e1, e2, ea, ea2, bi, hh, g1, g2, lo, xc = (L[x][b] for x in ["e1", "e2", "ea", "ea2", "bi", "hh", "g1", "g2", "lo", "xc"])
                    K = lambda nm: (nm, b)
                    S.op("dve", lambda e: e.tensor_scalar(xc[:, 0:n], xr_[:, 3:3 + n], lrup[:, 3:4], lrup[:, 4:5], op0=ALU.mult, op1=ALU.add),
                         reads=rk + ["lrup"], writes=[K("xc")])
                    for kk in (2, 1, 0):
                        S.op("dve", lambda e, kk=kk: e.scalar_tensor_tensor(out=xc[:, 0:n], in0=xr_[:, kk:kk + n], scalar=lrup[:, kk:kk + 1],
                                                                             in1=xc[:, 0:n], op0=ALU.mult, op1=ALU.add),
                             reads=rk + ["lrup", K("xc")], writes=[K("xc")])
                    S.op("dve", lambda e: e.tensor_copy(xrn[:, 0:3], xr_[:, n:n + 3]), reads=rk, writes=[("xrbh", (k + 1) % 2)])
                    xb_ = xcb[k % 3]; xbk = ("xcb", k % 3)
                    S.op("dve", lambda e: e.tensor_copy(xb_[:, 0:n], xc[:, 0:n]), reads=[K("xc")], writes=[xbk])
                    if ti >= 0:
                        S.op("pool", lambda e: e.tensor_tensor(out=g1[:, 0:n], in0=g_[:, 0:n], in1=g_[:, 0:n], op=ALU.mult), reads=[gk], writes=[K("g1")])
                        S.op("pool", lambda e: e.tensor_scalar(g1[:, 0:n], g1[:, 0:n], 0.044715, 1.0, op0=ALU.mult, op1=ALU.add),
                             reads=[K("g1")], writes=[K("g1")])
                        S.op("pool", lambda e: e.tensor_tensor(out=g1[:, 0:n], in0=g1[:, 0:n], in1=g_[:, 0:n], op=ALU.mult), reads=[K("g1"), gk], writes=[K("g1")])
                    S.lv(9)
                    pG_ = pG[k % 2]; pgk = ("pG", k % 2)
                    S.op("pe", lambda e: e.matmul(pG_[:, 0:n], gw[:, 0:128], xb_[:, 0:n], start=True, stop=True),
                         reads=[xbk, "gw"], writes=[pgk])
                    S.op("pe", lambda e: e.matmul(pG_[:, 128:128 + n], gw[:, 128:256], xb_[:, 0:n], start=True, stop=True),
                         reads=[xbk, "gw"], writes=[pgk])
                    S.lv(10)
                    A = lambda out, in_, **kw: S.op("act", lambda e: e.activation(out=out, in_=in_, **kw["a"]), reads=kw["r"], writes=kw["w"])
                    A(e1[:, 0:n], pG_[:, 0:n], a=dict(func=AF.Exp, scale=-1.0, bias=lruc[:, 4:5]), r=[pgk, "lruc"], w=[K("e1")])
                    A(e2[:, 0:n], pG_[:, 128:128 + n], a=dict(func=AF.Exp, scale=-1.0, bias=lruc[:, 5:6]), r=[pgk, "lruc"], w=[K("e2")])
                    A(e1[:, 0:n], e1[:, 0:n], a=dict(func=AF.Ln, bias=1.0), r=[K("e1")], w=[K("e1")])
                    A(e2[:, 0:n], e2[:, 0:n], a=dict(func=AF.Ln, bias=1.0), r=[K("e2")], w=[K("e2")])
                    A(e1[:, 0:n], e1[:, 0:n], a=dict(func=AF.Exp, scale=-1.0), r=[K("e1")], w=[K("e1")])
                    A(e2[:, 0:n], e2[:, 0:n], a=dict(func=AF.Exp, scale=-1.0), r=[K("e2")], w=[K("e2")])
                    A(ea[:, 0:n], e1[:, 0:n], a=dict(func=AF.Exp, scale=lruc[:, 0:1]), r=[K("e1"), "lruc"], w=[K("ea")])
                    A(ea2[:, 0:n], e1[:, 0:n], a=dict(func=AF.Exp, scale=lruc[:, 1:2]), r=[K("e1"), "lruc"], w=[K("ea2")])
                    A(ea2[:, 0:n], ea2[:, 0:n], a=dict(func=AF.Ln, scale=-1.0, bias=1.0), r=[K("ea2")], w=[K("ea2")])
                    A(ea2[:, 0:n], ea2[:, 0:n], a=dict(func=AF.Exp, scale=0.5), r=[K("ea2")], w=[K("ea2")])
                    if ti >= 0:
                        A(g2[:, 0:n], g1[:, 0:n], a=dict(func=AF.Exp, scale=-GELU_C), r=[K("g1")], w=[K("g2")])
                        A(g2[:, 0:n], g2[:, 0:n], a=dict(func=AF.Ln, bias=1.0), r=[K("g2")], w=[K("g2")])
                        A(g2[:, 0:n], g2[:, 0:n], a=dict(func=AF.Exp, scale=-1.0), r=[K("g2")], w=[K("g2")])
                    S.lv(11)
                    S.op("dve", lambda e: e.tensor_tensor(out=bi[:, 0:n], in0=e2[:, 0:n], in1=xc[:, 0:n], op=ALU.mult),
                         reads=[K("e2"), K("xc")], writes=[K("bi")])
                    S.op("dve", lambda e: e.tensor_tensor(out=bi[:, 0:n], in0=bi[:, 0:n], in1=ea2[:, 0:n], op=ALU.mult),
                         reads=[K("bi"), K("ea2")], writes=[K("bi")])
                    S.op("dve", lambda e: e.tensor_tensor_scan(hh[:, 0:n], ea[:, 0:n], bi[:, 0:n], hst[:, 0:1], op0=ALU.mult, op1=ALU.add),
                         reads=[K("ea"), K("bi"), "hst"], writes=[K("hh")])
                    S.op("dve", lambda e: e.tensor_copy(hst[:, 0:1], hh[:, n - 1:n]), reads=[K("hh")], writes=["hst"])
                    if ti < 0:
                        return
                    S.op("dve", lambda e: e.tensor_tensor(out=g2[:, 0:n], in0=g2[:, 0:n], in1=g_[:, 0:n], op=ALU.mult), reads=[K("g2"), gk], writes=[K("g2")])
                    S.op("dve", lambda e: e.tensor_tensor(out=lo[:, 0:n], in0=g2[:, 0:n], in1=hh[:, 0:n], op=ALU.mult),
                         reads=[K("g2"), K("hh")], writes=[K("lo")])
                    S.lv(12)
                    S.op("pe", lambda e: e.transpose(pL[:, :], lo[:, :], identf[:, :]), reads=[K("lo"), "identf"], writes=["pL"])
                    S.lv(13)
                    ls = lstage[k % 3]
                    S.op("act", lambda e: e.activation(out=ls[:, :], in_=pL[:, :], func=AF.Identity), reads=["pL"], writes=[("ls", k % 3)])
                    S.lv(14)
                    S.dma("sp", lambda e: e.dma_start(out=mixdst(ti)[:, 128:256], in_=ls[:, :]),
                          reads=[("ls", k % 3)], writes=[("mixbuf", ti, 2)])

                nk = len(tiles)
                NLV = 15
                buckets = {}
                for k in range(nk):
                    S.capture_begin()
                    tile_prog(k)
                    for ent in S.capture_end():
                        buckets.setdefault((k + ent[5], k), []).append(ent)
                S.lv(0)
                laststep = -1
                for (step, k) in sorted(buckets):
                    if step != laststep:
                        laststep = step
                        if step >= 8 and step % 4 == 0:
                            convert_step(S)
                    lst = buckets[(step, k)]
                    S.replay(lst, len(lst))
                if stop == "A1":
                    S.fence("sp")
                S.emit()
            if stop == "A1":
                return nc

            esA2 = ExitStack()
            with esA2:
                Eb = [sb(esA2, "Eb%d" % i, [128, 512]) for i in range(2)]
                Lb = [sb(esA2, "Lb%d" % i, [128, 512], BF16) for i in range(2)]
                Ls = [sb(esA2, "Ls%d" % i, [128, 512], BF16) for i in range(2)]
                At = [sb(esA2, "At%d" % i, [128, 512], BF16) for i in range(2)]
                osb = [sb(esA2, "osb%d" % i, [128, 512]) for i in range(2)]
                ostg = [sb(esA2, "ostg%d" % i, [128, 4, 64]) for i in range(2)]
                pZ = [ps(esA2, "pZ%d" % i, [128, 512]) for i in range(3)]
                pO = [ps(esA2, "pO%d" % i, [128, 512]) for i in range(2)]
                pX = ps(esA2, "pX", [128, 512])
                qTs = [qT0, qT1]
                def attn_group(i, hd, gi):
                    if True:
                        qT = qTs[hd]
                        qk = "qT%d" % hd
                        q0 = NMETA + 512 * i
                        seq = [(4 * i + r, r) for r in (3, 2, 1, 0)] + [(m, None) for m in range(4 * i - 1, -1, -1)] + [(NT, 4)]
                        N = len(seq)
                        pO_ = pO[gi % 2]; pok = ("pO", gi % 2)

                        def kcols(m):
                            if m == NT:
                                return slice(0, 128)
                            return slice(NMETA + 128 * m, NMETA + 128 * (m + 1))

                        def sZ(n):
                            m, mk = seq[n]
                            b = n % 2
                            b3 = n % 3
                            S.op("pe", lambda e: e.matmul(pZ[b3][:, :], kT[:, kcols(m)], qT[:, q0:q0 + 512], start=True, stop=False),
                                 reads=["kT", qk], writes=[("pZ", b3)])
                            S.op("act", lambda e: e.activation(out=Eb[b][:, :], in_=pZ[b3][:, :], func=AF.Exp),
                                 reads=[("pZ", b3)], writes=[("Eb", b)])

                        def sZ2(n):
                            m, mk = seq[n]
                            b = n % 2
                            S.op("act", lambda e: e.activation(out=Lb[b][:, :], in_=Eb[b][:, :], func=AF.Ln, bias=1.0),
                                 reads=[("Eb", b)], writes=[("Lb", b)])
                            if mk is not None:
                                S.op("dve", lambda e: e.tensor_tensor(out=Lb[b][:, :], in0=Lb[b][:, :], in1=masks[:, mk, :], op=ALU.mult),
                                     reads=[("Lb", b), "masks"], writes=[("Lb", b)])

                        def sE(n):
                            m, mk = seq[n]
                            b = n % 2
                            b3 = n % 3
                            S.op("pe", lambda e: e.matmul(pZ[b3][:, :], negtri[:, :], Lb[b][:, :], start=False, stop=(n == 0), skip_group_check=True),
                                 reads=["negtri", ("Lb", b)], writes=[("pZ", b3)])
                            if n > 0:
                                S.op("pe", lambda e: e.matmul(pZ[b3][:, :], negones[:, :], Ls[b][:, :], start=False, stop=True, skip_group_check=True),
                                     reads=["negones", ("Ls", b)], writes=[("pZ", b3)])
                            S.op("act", lambda e: e.activation(out=At[b][:, :], in_=pZ[b3][:, :], func=AF.Exp),
                                 reads=[("pZ", b3)], writes=[("At", b)])
                            if mk is not None:
                                S.op("dve", lambda e: e.tensor_tensor(out=At[b][:, :], in0=At[b][:, :], in1=masks[:, mk, :], op=ALU.mult),
                                     reads=[("At", b), "masks"], writes=[("At", b)])
                            if n + 1 < N:
                                nb = (n + 1) % 2
                                if n == 0:
                                    S.op("dve", lambda e: e.tensor_copy(Ls[nb][:, :], Lb[b][:, :]), reads=[("Lb", b)], writes=[("Ls", nb)])
                                else:
                                    S.op("dve", lambda e: e.tensor_tensor(out=Ls[nb][:, :], in0=Ls[b][:, :], in1=Lb[b][:, :], op=ALU.add),
                                         reads=[("Lb", b), ("Ls", b)], writes=[("Ls", nb)])

                        def sO(n):
                            m, mk = seq[n]
                            b = n % 2
                            S.op("pe", lambda e: e.matmul(pO_[:, :], vtok[:, m, :], At[b][:, :], start=(n == 0), stop=(n == N - 1)),
                                 reads=["vtok", ("At", b)], writes=[pok])

                        for n in range(-1, N + 1):
                            if 0 <= n + 1 < N:
                                sZ(n + 1)
                                sZ2(n + 1)
                            if 0 <= n < N:
                                sE(n)
                            if 0 <= n - 1 < N:
                                sO(n - 1)
                        ob = osb[gi % 2]; obk = ("osb", gi % 2)
                        S.op("dve", lambda e: e.tensor_copy(ob[:, :], pO_[:, :]), reads=[pok], writes=[obk])
                        for j in range(4):
                            S.op("pe", lambda e, j=j: e.transpose(pX[:, j * 128:(j + 1) * 128], ob[:, j * 128:(j + 1) * 128], identf[:, :]),
                                 reads=[obk, "identf"], writes=["pX"])
                        og = ostg[gi % 2]; ogk = ("ostg", gi % 2)
                        S.op("act", lambda e: e.activation(out=og[:, :, :], in_=pX[:, :].rearrange("p (j c) -> p j c", j=4)[:, :, 64 * hd:64 * hd + 64],
                                                           func=AF.Identity), reads=["pX"], writes=[ogk])
                        for j in range(4):
                            S.dma("sp", lambda e, j=j: e.dma_start(out=mixdst(4 * i + j)[:, 64 * hd:64 * hd + 64], in_=og[:, j, :]),
                                  reads=[ogk], writes=[("mixbuf", 4 * i + j, hd)])

                gi = 0
                for i in range(16):
                    for hd in range(2):
                        attn_group(i, hd, gi)
                        gi += 1
                    if i % 2 == 1 and stop != "A2":
                        sp_ = i // 2
                        ks = [("mixbuf", ti, q) for ti in range(8 * sp_, 8 * sp_ + 8) for q in range(3)]
                        S.cc(lambda e, sp_=sp_: e.collective_compute("AllGather", ALU.bypass, replica_groups=[[0, 1, 2, 3], [4, 5, 6, 7]],
                                                                     ins=[mixbuf[sp_]], outs=[mixall[sp_]]), reads=ks, writes=[("mixall", sp_)])

                S.emit()

        if debug:
            esD = ExitStack()
            with esD:
                dt_ = sb(esD, "dbgt", [128, 64, 256])
                allk = [("mixbuf", ti, q) for ti in range(NT) for q in range(3)]
                for ti in range(NT):
                    S.dma("sp", lambda e, ti=ti: e.dma_start(out=dt_[:, ti, :], in_=mixdst(ti)), reads=allk, writes=["dbgt"])
                S.dma("sp", lambda e: e.dma_start(out=dbg.rearrange("(j p) c -> p j c", p=128), in_=dt_[:]), reads=["dbgt"], writes=["dbg"])
                S.fence("sp")
                S.emit()
        if stop == "A2":
            if not debug:
                S.fence("sp")
                S.emit()
            return nc
        ndel = int(os.environ.get("MK_DELAY", 0))
        if ndel:
            esX = ExitStack()
            with esX:
                scr = sb(esX, "scr", [128, 2048])
                S.op("act", lambda e: e.activation(out=scr[:], in_=scr[:], func=AF.Identity, scale=0.0), writes=["scr"])
                for _ in range(ndel):
                    S.op("act", lambda e: e.activation(out=scr[:], in_=scr[:], func=AF.Identity), reads=["scr"], writes=["scr"])
                S.op("pool", lambda e: e.memset(scr[:, 0:1], 0.0), reads=["scr"], writes=["scr"])
                S.fence("sp")
                S.emit()

        if stop == "CC":
            S.fence("sp")
            S.emit()
            return nc
        build_phase_b(nc, S, sb, ps, locals())
    return nc


def build_phase_b(nc, S, sb, ps, env):
    xo = env["xo"]; w_out_p = env["w_out_p"]; gmix_d = env["gmix_d"]; lnB = env["lnB"]; wq_d = env["wq_d"]
    skT_d = env["skT_d"]; down = env["down"]; up = env["up"]; idxm_d = env["idxm_d"]; iota_d = env["iota_d"]
    out = env["out"]; mixall = env["mixall"]; cst = env["cst"]
    downb = env["downb"]; upb = env["upb"]
    env["convert_step"](S, 16)
    tabk_d = [("downb", j) for j in range(8)]
    tabk_u = [("upb", j) for j in range(8)]
    es = ExitStack()
    with es:
        ident = sb(es, "identB", [128, 128], BF16)
        Wo = sb(es, "Wo", [128, 8, D], BF16)
        gmix = sb(es, "gmix_s", [128, 8])
        Wq = sb(es, "Wq", [128, 8, 2048], BF16)
        SK = sb(es, "SK", [128, 16, 128], BF16)
        LNb = sb(es, "LNb", [128, 6, D])
        idxm = sb(es, "idxm_s", [128, 64], I32)
        iota = sb(es, "iota_s", [128, 16])
        WD = sb(es, "WD", [128, 128 * 128], BF16)

        S.dma("pool", lambda e: e.dma_start(out=ident[:], in_=cst[:, 0:128]), writes=["identB"])
        S.dma("sp", lambda e: e.dma_start(out=idxm[:], in_=idxm_d), writes=["idxm"])
        S.dma("sp", lambda e: e.dma_start(out=gmix[:], in_=gmix_d), writes=["gmix"])
        for r in range(6):
            S.dma("sp", lambda e, r=r: e.dma_start(out=LNb[:, r, :], in_=bass.AP(lnB.tensor, r * D, [[0, 128], [1, D]])),
                  writes=[("LNb", r)])
        S.dma("sp", lambda e: e.dma_start(out=iota[:], in_=iota_d), writes=["iota"])
        S.op("dve", lambda e: e.memset(WD[:], 0.0), writes=["WD"])

        def load_weights():
            for kc in range(8):
                S.dma("pool", lambda e, kc=kc: e.dma_start(out=Wo[:, kc, :], in_=w_out_p[kc * 128:(kc + 1) * 128, :]), writes=[("Wo", kc)])
            for kc in range(8):
                S.op("dve", lambda e, kc=kc: e.tensor_scalar(Wo[:, kc, :], Wo[:, kc, :], gmix[:, kc:kc + 1], None, op0=ALU.mult),
                     reads=[("Wo", kc), "gmix"], writes=[("Wo", kc)])
            for kc in range(8):
                S.dma("pool", lambda e, kc=kc: e.dma_start(out=Wq[:, kc, :], in_=wq_d[kc * 128:(kc + 1) * 128, :]), writes=[("Wq", kc)])
            S.dma("pool", lambda e: e.dma_start(out=SK[:], in_=skT_d.rearrange("p (c n) -> p c n", c=16)), writes=["SK"])

        xt = sb(es, "bx", [128, D])
        mix = sb(es, "mix", [128, 4, 256])
        mixn = sb(es, "mixn", [128, 4, 256], BF16)
        mixT = sb(es, "mixT", [128, 8, 128], BF16)
        ot = sb(es, "ot", [128, D])
        junk = sb(es, "junk", [128, D], BF16)
        sq = sb(es, "sq", [128, 4, 128], BF16)
        st = sb(es, "bst", [128, 2, 6])
        sm = sb(es, "bsm", [128, 16])
        st2 = sb(es, "bst2", [128, 2, 6])
        sm2 = sb(es, "bsm2", [128, 16])
        U2 = [sb(es, "bu%d" % i, [128, D]) for i in range(2)]
        H1B = [sb(es, "h1b%d" % i, [128, D], BF16) for i in range(2)]
        h1T = mixT
        qsb = sb(es, "qsb", [128, 16, 128], BF16)
        Ssb = sb(es, "Ssb", [128, 16, 128])
        wk = sb(es, "wk", [128, 128])
        wk2 = sb(es, "wk2", [128, 256])
        v1 = sb(es, "v1", [128, 8, 16]); v2 = sb(es, "v2", [128, 8, 16])
        i1 = sb(es, "i1", [128, 8, 16], U32); i2 = sb(es, "i2", [128, 8, 16], U32)
        i1f = sb(es, "i1f", [128, 8, 16]); i2f = sb(es, "i2f", [128, 8, 16])
        cand = sb(es, "cand", [128, 8, 256])
        eq = cand[:].rearrange("p h (a b) -> p h a b", a=16)
        scv = sb(es, "scv", [128, 8, 16])
        pos = sb(es, "pos", [128, 8, 16], U32)
        ph = sb(es, "ph", [128, 8, 16], U32)
        phf = sb(es, "phf", [128, 8, 16]); plf = sb(es, "plf", [128, 8, 16])
        S1 = [sb(es, "s1_%d" % i, [128, 8, 16]) for i in range(2)]
        S2 = [sb(es, "s2_%d" % i, [128, 8, 16]) for i in range(2)]
        GSM = [sb(es, "gsm%d" % i, [128, 8, 16]) for i in range(2)]
        zs = sb(es, "zs", [128, 16])
        idxf = sb(es, "idxf", [128, 128])
        IDX32 = [sb(es, "idx32_%d" % i, [128, 128], I32) for i in range(2)]
        actv = sb(es, "actv", [128, 128])
        gl = sb(es, "gl", [128, 128])
        tb = sb(es, "tb", [128, 3, 128], BF16)
        tT = sb(es, "tT", [128, 3, 128], BF16)
        idxT = sb(es, "idxT", [128, 128], I32)
        NB = 8
        Db = [sb(es, "Db%d" % i, [128, D], BF16) for i in range(NB)]
        Ub = [sb(es, "Ub%d" % i, [128, D], BF16) for i in range(NB)]
        Dslot = [S.new_slot("dsl%d" % i) for i in range(NB)]
        Uslot = [S.new_slot("usl%d" % i) for i in range(NB)]
        S.extra_slots = Dslot + Uslot
        pT = ps(es, "bpT", [128, 8, 128], BF16)
        pTb = ps(es, "bpTb", [128, 8, 128], BF16)
        pY = [ps(es, "bpY%d" % i, [128, 512]) for i in range(2)]
        pU = [ps(es, "bpU%d" % i, [128, 512]) for i in range(2)]
        pQ = [ps(es, "bpQ%d" % i, [128, 512]) for i in range(1)] * 2
        pS = [ps(es, "bpS%d" % i, [128, 512]) for i in range(1)] * 2

        if os.environ.get("MK_VERBOSE"):
            print("phase B sbuf bytes remaining per partition:", nc.sbuf_bytes_remaining)

        def layernorm(src, srck, dst, dstk, gi_, st=st, sm=sm, stk="bst", smk="bsm"):
            S.op("dve", lambda e: e.bn_stats(st[:, 0, :], src[:, 0:512]), reads=[srck], writes=[stk])
            S.op("dve", lambda e: e.bn_stats(st[:, 1, :], src[:, 512:1024]), reads=[srck], writes=[stk])
            S.op("dve", lambda e: e.bn_aggr(sm[:, 0:2], st[:].rearrange("p a b -> p (a b)")), reads=[stk], writes=[smk])
            S.op("act", lambda e: e.activation(out=sm[:, 2:3], in_=sm[:, 1:2], func=AF.Ln, bias=EPS), reads=[smk], writes=[smk])
            S.op("act", lambda e: e.activation(out=sm[:, 3:4], in_=sm[:, 2:3], func=AF.Exp, scale=-0.5), reads=[smk], writes=[smk])
            S.op("dve", lambda e: e.tensor_scalar(dst[:, :], src[:, :], sm[:, 0:1], sm[:, 3:4], op0=ALU.subtract, op1=ALU.mult),
                 reads=[srck, smk], writes=[dstk])
            S.op("dve", lambda e: e.tensor_tensor(out=dst[:, :], in0=dst[:, :], in1=LNb[:, gi_, :], op=ALU.mult),
                 reads=[dstk, ("LNb", gi_)], writes=[dstk])
            S.op("dve", lambda e: e.tensor_tensor(out=dst[:, :], in0=dst[:, :], in1=LNb[:, gi_ + 1, :], op=ALU.add),
                 reads=[dstk, ("LNb", gi_ + 1)], writes=[dstk])

        def front(jt):
            pb = jt % 2
            u = U2[pb]; h1 = u; uk = ("bu", pb)
            s1 = S1[pb]; s2 = S2[pb]; gsm = GSM[pb]; idx32 = IDX32[pb]
            s1k = ("s1", pb); s2k = ("s2", pb); gsmk = ("gsm", pb); idxk = ("idx32", pb)
            h1b = H1B[pb]; h1bk = ("h1b", pb)
            x_ = xt; xk = "bx"
            h_ = xt; hk = "bx"
            mx = mix; mxk = "mix"
            S.dma("sp", lambda e: e.dma_start(out=x_[:, :], in_=xo[jt * 128:(jt + 1) * 128, :]), writes=[xk])
            for r in range(4):
                S.dma("pool", lambda e, r=r: e.indirect_dma_start(out=mx[:, r, :], out_offset=None, in_=mixall[jt // 2],
                                                                   in_offset=bass.IndirectOffsetOnAxis(ap=idxm[:, jt * 4 + r:jt * 4 + r + 1], axis=0)),
                      reads=["idxm", ("mixall", jt // 2)], writes=[mxk])
            layernorm(x_, xk, h_, hk, 0)
            S.op("act", lambda e: e.activation(out=sq[:, :, :], in_=mx[:, :, 0:128], func=AF.Square, accum_out=sm[:, 4:5]),
                 reads=[mxk], writes=["sq", "bsm"])
            S.op("act", lambda e: e.activation(out=sq[:, :, :], in_=mx[:, :, 128:256], func=AF.Square, accum_out=sm[:, 5:6]),
                 reads=[mxk, "sq"], writes=["sq", "bsm"])
            S.op("act", lambda e: e.activation(out=sm[:, 6:8], in_=sm[:, 4:6], func=AF.Ln, scale=1.0 / 512, bias=EPS), reads=["bsm"], writes=["bsm"])
            S.op("act", lambda e: e.activation(out=sm[:, 6:8], in_=sm[:, 6:8], func=AF.Exp, scale=-0.5), reads=["bsm"], writes=["bsm"])
            S.op("dve", lambda e: e.tensor_scalar(mixn[:, :, 0:128], mx[:, :, 0:128], sm[:, 6:7], None, op0=ALU.mult),
                 reads=[mxk, "bsm"], writes=["mixn"])
            S.op("dve", lambda e: e.tensor_scalar(mixn[:, :, 128:256], mx[:, :, 128:256], sm[:, 7:8], None, op0=ALU.mult),
                 reads=[mxk, "bsm"], writes=["mixn"])
            mflat = mixn[:].rearrange("p a b -> p (a b)")
            for kc in range(8):
                S.op("pe", lambda e, kc=kc: e.transpose(pT[:, kc, :], mflat[:, kc * 128:(kc + 1) * 128], ident[:, :]),
                     reads=["mixn", "identB"], writes=["bpT"])
            S.op("act", lambda e: e.activation(out=mixT[:, :, :], in_=pT[:, :, :], func=AF.Identity), reads=["bpT"], writes=["mixT"])
            for hf in range(2):
                for kc in range(8):
                    S.op("pe", lambda e, hf=hf, kc=kc: e.matmul(pY[hf][:, :], mixT[:, kc, :], Wo[:, kc, hf * 512:(hf + 1) * 512],
                                                                  start=(kc == 0), stop=(kc == 7)),
                         reads=["mixT", ("Wo", kc)], writes=[("bpY", hf)])
            for hf in range(2):
                S.op("dve", lambda e, hf=hf: e.scalar_tensor_tensor(out=u[:, hf * 512:(hf + 1) * 512], in0=h_[:, hf * 512:(hf + 1) * 512], scalar=ALPHA,
                                                                     in1=pY[hf][:, :], op0=ALU.mult, op1=ALU.add),
                     reads=[hk, ("bpY", hf)], writes=[uk])
            layernorm(u, uk, u, uk, 2)
            S.op("act", lambda e: e.activation(out=h1b[:, :], in_=h1[:, :], func=AF.Identity), reads=[uk], writes=[h1bk])
            for kc in range(8):
                S.op("pe", lambda e, kc=kc: e.transpose(pT[:, kc, :], h1b[:, kc * 128:(kc + 1) * 128], ident[:, :]),
                     reads=[h1bk, "identB"], writes=["bpT"])
            S.op("act", lambda e: e.activation(out=h1T[:, :, :], in_=pT[:, :, :], func=AF.Identity), reads=["bpT"], writes=["mixT"])
            for c4 in range(4):
                pq = pQ[c4 % 2]; pqk = ("bpQ", 0)
                for cj in range(4):
                    cc = c4 * 4 + cj
                    for kc in range(8):
                        S.op("pe", lambda e, cc=cc, cj=cj, kc=kc, pq=pq: e.matmul(pq[:, cj * 128:(cj + 1) * 128], Wq[:, kc, cc * 128:(cc + 1) * 128], h1T[:, kc, :],
                                                                                   start=(kc == 0), stop=(kc == 7)),
                             reads=["mixT", ("Wq", kc)], writes=[pqk])
                eng = "act" if c4 % 2 == 0 else "dve"
                if eng == "act":
                    S.op("act", lambda e, c4=c4, pq=pq: e.activation(out=qsb[:, c4 * 4:(c4 + 1) * 4, :], in_=pq[:, :].rearrange("p (a b) -> p a b", a=4), func=AF.Identity),
                         reads=[pqk], writes=[("qsb", c4)])
                else:
                    S.op("dve", lambda e, c4=c4, pq=pq: e.tensor_copy(qsb[:, c4 * 4:(c4 + 1) * 4, :], pq[:, :].rearrange("p (a b) -> p a b", a=4)),
                         reads=[pqk], writes=[("qsb", c4)])
            for c4 in range(4):
                pz = pS[c4 % 2]; pzk = ("bpS", 0)
                for cj in range(4):
                    cc = c4 * 4 + cj
                    S.op("pe", lambda e, cc=cc, cj=cj, pz=pz: e.matmul(pz[:, cj * 128:(cj + 1) * 128], qsb[:, cc, :], SK[:, cc, :], start=True, stop=True),
                         reads=[("qsb", c4), "SK"], writes=[pzk])
                S.op("act", lambda e, c4=c4, pz=pz: e.activation(out=Ssb[:, c4 * 4:(c4 + 1) * 4, :], in_=pz[:, :].rearrange("p (a b) -> p a b", a=4), func=AF.Identity),
                     reads=[pzk], writes=[("Ssb", c4)])
            for hh in range(8):
                for half, (vv, ii) in enumerate(((v1, i1), (v2, i2))):
                    cc = 2 * hh + half
                    sk_ = ("Ssb", cc // 4)
                    S.op("dve", lambda e, cc=cc, vv=vv, hh=hh: e.max(out=vv[:, hh, 0:8], in_=Ssb[:, cc, :]), reads=[sk_], writes=["vv"])
                    S.op("dve", lambda e, cc=cc, vv=vv, hh=hh: e.match_replace(out=wk[:, :], in_to_replace=vv[:, hh, 0:8], in_values=Ssb[:, cc, :], imm_value=-1e30),
                         reads=[sk_, "vv"], writes=["wk"])
                    S.op("dve", lambda e, vv=vv, hh=hh: e.max(out=vv[:, hh, 8:16], in_=wk[:, :]), reads=["wk"], writes=["vv"])
                    S.op("dve", lambda e, cc=cc, vv=vv, ii=ii, hh=hh: e.max_index(out=ii[:, hh, 0:8], in_max=vv[:, hh, 0:8], in_values=Ssb[:, cc, :]),
                         reads=[sk_, "vv"], writes=["ii"])
                    S.op("dve", lambda e, cc=cc, vv=vv, ii=ii, hh=hh: e.max_index(out=ii[:, hh, 8:16], in_max=vv[:, hh, 8:16], in_values=Ssb[:, cc, :]),
                         reads=[sk_, "vv"], writes=["ii"])
            S.op("dve", lambda e: e.tensor_tensor(out=cand[:].rearrange("p h (a b) -> p h a b", a=16),
                                                  in0=APx(v1[:], [[16, 8], [1, 16], [0, 16]]),
                                                  in1=APx(v2[:], [[16, 8], [0, 16], [1, 16]]), op=ALU.add),
                 reads=["vv"], writes=["cand"])
            for hh in range(8):
                S.op("dve", lambda e, hh=hh: e.max(out=scv[:, hh, 0:8], in_=cand[:, hh, :]), reads=["cand"], writes=["scv"])
                S.op("dve", lambda e, hh=hh: e.match_replace(out=wk2[:, :], in_to_replace=scv[:, hh, 0:8], in_values=cand[:, hh, :], imm_value=-1e30),
                     reads=["cand", "scv"], writes=["wk2"])
                S.op("dve", lambda e, hh=hh: e.max(out=scv[:, hh, 8:16], in_=wk2[:, :]), reads=["wk2"], writes=["scv"])
                S.op("dve", lambda e, hh=hh: e.max_index(out=pos[:, hh, 0:8], in_max=scv[:, hh, 0:8], in_values=cand[:, hh, :]), reads=["cand", "scv"], writes=["pos"])
                S.op("dve", lambda e, hh=hh: e.max_index(out=pos[:, hh, 8:16], in_max=scv[:, hh, 8:16], in_values=cand[:, hh, :]), reads=["cand", "scv"], writes=["pos"])
            S.op("dve", lambda e: e.tensor_single_scalar(ph[:], pos[:], 4, op=ALU.logical_shift_right), reads=["pos"], writes=["ph"])
            S.op("dve", lambda e: e.tensor_copy(phf[:], ph[:]), reads=["ph"], writes=["phf"])
            S.op("dve", lambda e: e.tensor_single_scalar(ph[:], pos[:], 15, op=ALU.bitwise_and), reads=["pos", "phf"], writes=["ph"])
            S.op("dve", lambda e: e.tensor_copy(plf[:], ph[:]), reads=["ph"], writes=["plf"])
            S.op("dve", lambda e: e.tensor_copy(i1f[:], i1[:]), reads=["ii"], writes=["i1f"])
            S.op("dve", lambda e: e.tensor_copy(i2f[:], i2[:]), reads=["ii"], writes=["i2f"])
            for (pf, pfk, iif, iifk, so, sok) in ((phf, "phf", i1f, "i1f", s1, s1k), (plf, "plf", i2f, "i2f", s2, s2k)):
                S.op("dve", lambda e, pf=pf: e.tensor_tensor(out=eq[:], in0=APx(pf[:], [[16, 8], [1, 16], [0, 16]]),
                                                             in1=APx(iota[:], [[0, 8], [0, 16], [1, 16]]), op=ALU.is_equal),
                     reads=[pfk, "iota"], writes=["cand"])
                S.op("dve", lambda e, iif=iif: e.tensor_tensor(out=eq[:], in0=eq[:], in1=APx(iif[:], [[16, 8], [0, 16], [1, 16]]), op=ALU.mult),
                     reads=["cand", iifk], writes=["cand"])
                S.op("dve", lambda e, so=so: e.tensor_reduce(out=so[:], in_=eq[:], axis=AX.X, op=ALU.add), reads=["cand"], writes=[sok])
            S.op("dve", lambda e: e.scalar_tensor_tensor(out=idxf[:].rearrange("p (h k) -> p h k", h=8), in0=s1[:], scalar=128.0, in1=s2[:],
                                                         op0=ALU.mult, op1=ALU.add), reads=[s1k, s2k], writes=["idxf"])
            S.op("dve", lambda e: e.tensor_copy(idx32[:], idxf[:]), reads=["idxf"], writes=[idxk])
            S.op("dve", lambda e: e.tensor_tensor(out=gsm[:], in0=scv[:], in1=APx(scv[:], [[16, 8], [0, 16]]), op=ALU.subtract),
                 reads=["scv"], writes=[gsmk])
            S.op("act", lambda e: e.activation(out=gsm[:], in_=gsm[:], func=AF.Exp), reads=[gsmk], writes=[gsmk])
            S.op("dve", lambda e: e.tensor_reduce(out=zs[:, 0:8], in_=gsm[:], axis=AX.X, op=ALU.add), reads=[gsmk], writes=["zs"])
            S.op("dve", lambda e: e.reciprocal(zs[:, 8:16], zs[:, 0:8]), reads=["zs"], writes=["zs"])
            S.op("dve", lambda e: e.tensor_tensor(out=gsm[:], in0=gsm[:], in1=APx(zs[:, 8:16], [[1, 8], [0, 16]]), op=ALU.mult),
                 reads=[gsmk, "zs"], writes=[gsmk])

        def back(jt, cap):
            pb = jt % 2
            u = U2[pb]; h1 = u; uk = ("bu", pb)
            s1 = S1[pb]; s2 = S2[pb]; gsm = GSM[pb]; idx32 = IDX32[pb]
            s1k = ("s1", pb); s2k = ("s2", pb); gsmk = ("gsm", pb); idxk = ("idx32", pb)
            h1b = H1B[pb]; h1bk = ("h1b", pb)
            nrep = (len(cap) + 239) // 240 if cap else 0
            def prep_idxT():
              S.op("dve", lambda e: e.tensor_copy(tb[:, 1, :], s1[:].rearrange("p h k -> p (h k)")), reads=[s1k], writes=["tb12"])
              S.op("dve", lambda e: e.tensor_copy(tb[:, 2, :], s2[:].rearrange("p h k -> p (h k)")), reads=[s2k], writes=["tb12"])
              for j in (1, 2):
                S.op("pe", lambda e, j=j: e.transpose(pTb[:, j, :], tb[:, j, :], ident[:, :]), reads=["tb12", "identB"], writes=["bpTb"])
              S.op("act", lambda e: e.activation(out=tT[:, 1:3, :], in_=pTb[:, 1:3, :], func=AF.Identity), reads=["bpTb"], writes=["tT12"])
              S.op("dve", lambda e: e.scalar_tensor_tensor(out=idxT[:], in0=tT[:, 1, :], scalar=128.0, in1=tT[:, 2, :], op0=ALU.mult, op1=ALU.add),
                 reads=["tT12"], writes=["idxT"])
            S.op("dve", lambda e: e.memset(actv[:], 0.0), writes=["actv"])
            dot_ev = {}
            for s_ in range(128):
                db = Db[s_ % NB]; dk = ("Db", s_ % NB)
                if s_ % 2 == 0 and s_ - (NB - 1) in dot_ev:
                    S._wait("pool", dot_ev[s_ - (NB - 1)])
                S.dma("pool", lambda e, s_=s_, db=db: e.indirect_dma_start(out=db[:, :], out_offset=None, in_=downb,
                                                                            in_offset=bass.IndirectOffsetOnAxis(ap=idx32[:, s_:s_ + 1], axis=0)),
                      reads=[idxk] + tabk_d, writes=[dk], slot=Dslot[s_ % NB])
                dot_ev[s_] = S.op("dve", lambda e, s_=s_, db=db: e.scalar_tensor_tensor(out=junk[:, :], in0=db[:, :], scalar=1.0, in1=h1b[:, :], op0=ALU.mult, op1=ALU.mult,
                                                                            accum_out=actv[:, s_:s_ + 1]),
                     reads=[dk, h1bk, "actv"], writes=["junk", ("actv", s_)])
                if s_ == 24:
                    prep_idxT()
                if cap and s_ % 2 == 1:
                    S.replay(cap, 1)
            akeys = [("actv", s_) for s_ in range(128)]
            S.op("dve", lambda e: e.tensor_tensor(out=gl[:], in0=actv[:], in1=actv[:], op=ALU.mult), reads=akeys + ["actv"], writes=["gl"])
            S.op("dve", lambda e: e.tensor_scalar(gl[:], gl[:], 0.044715, 1.0, op0=ALU.mult, op1=ALU.add), reads=["gl"], writes=["gl"])
            S.op("dve", lambda e: e.tensor_tensor(out=gl[:], in0=gl[:], in1=actv[:], op=ALU.mult), reads=["gl"], writes=["gl"])
            S.op("act", lambda e: e.activation(out=gl[:], in_=gl[:], func=AF.Exp, scale=-GELU_C), reads=["gl"], writes=["gl"])
            S.op("dve", lambda e: e.tensor_scalar(gl[:], gl[:], 1.0, None, op0=ALU.add), reads=["gl"], writes=["gl"])
            S.op("dve", lambda e: e.reciprocal(gl[:], gl[:]), reads=["gl"], writes=["gl"])
            S.op("dve", lambda e: e.tensor_tensor(out=gl[:], in0=gl[:], in1=actv[:], op=ALU.mult), reads=["gl"], writes=["gl"])
            S.op("dve", lambda e: e.tensor_tensor(out=tb[:, 0, :], in0=gl[:], in1=gsm[:].rearrange("p h k -> p (h k)"), op=ALU.mult),
                 reads=["gl", gsmk], writes=["tb"])
            S.op("pe", lambda e: e.transpose(pTb[:, 0, :], tb[:, 0, :], ident[:, :]), reads=["tb", "identB"], writes=["bpTb"])
            S.op("act", lambda e: e.activation(out=tT[:, 0, :], in_=pTb[:, 0, :], func=AF.Identity), reads=["bpTb"], writes=["tT"])
            S.op("dve", lambda e: e.tensor_copy(APx(WD[:], [[129, 128]]), tT[:, 0, :]), reads=["tT"], writes=["WD"])
            mm_ev = {}
            nrep_up = (len(cap) + 99) // 100 if cap else 0
            for t_ in range(128):
                ub = Ub[t_ % NB]; ubk = ("Ub", t_ % NB)
                if t_ % 2 == 0 and t_ - (NB - 1) in mm_ev:
                    S._wait("pool", mm_ev[t_ - (NB - 1)])
                S.dma("pool", lambda e, t_=t_, ub=ub: e.indirect_dma_start(out=ub[:, :], out_offset=None, in_=upb,
                                                                            in_offset=bass.IndirectOffsetOnAxis(ap=idxT[:, t_:t_ + 1], axis=0)),
                      reads=["idxT"] + tabk_u, writes=[ubk], slot=Uslot[t_ % NB])
                for hf in range(2):
                    mm_ev[t_] = S.op("pe", lambda e, t_=t_, ub=ub, hf=hf: e.matmul(pU[hf][:, :], WD[:, t_ * 128:(t_ + 1) * 128], ub[:, hf * 512:(hf + 1) * 512],
                                                                        start=(t_ == 0), stop=(t_ == 127)),
                         reads=["WD", ubk], writes=[("bpU", hf)])
                if cap:
                    S.replay(cap, nrep_up)
            for hf in range(2):
                S.op("dve", lambda e, hf=hf: e.scalar_tensor_tensor(out=ot[:, hf * 512:(hf + 1) * 512], in0=h1[:, hf * 512:(hf + 1) * 512], scalar=ALPHA,
                                                                     in1=pU[hf][:, :], op0=ALU.mult, op1=ALU.add),
                     reads=[uk, ("bpU", hf)], writes=["ot"])
            layernorm(ot, "ot", ot, "ot", 4, st=st2, sm=sm2, stk="bst2", smk="bsm2")
            S.dma("sp", lambda e: e.dma_start(out=out[jt * 128:(jt + 1) * 128, :], in_=ot[:, :]), reads=["ot"], writes=["out"])
        S.capture_begin()
        front(0)
        cap0 = S.capture_end()
        S.replay(cap0, 5)
        load_weights()
        S.replay(cap0, len(cap0))
        for jt in range(NTB):
            cap = None
            if jt + 1 < NTB:
                S.capture_begin()
                front(jt + 1)
                cap = S.capture_end()
            back(jt, cap)
            if cap:
                S.replay(cap, len(cap))
        S.fence("sp")
        S.emit()


def own_tiles(g):
    return [8 * (jt // 2) + 2 * g + (jt % 2) for jt in range(NTB)]


def host_inputs(x, meta_tokens, ln_emb_g, ln_emb_b, w_in, conv_w, conv_b, gate_a_w, gate_a_b,
                gate_x_w, gate_x_b, lru_lambda, attn_norm_g, lru_norm_g, w_out, ln1_g, ln1_b,
                peer_query_w, peer_sub_keys, peer_down, peer_up, ln2_g, ln2_b):
    f = lambda a: np.ascontiguousarray(np.asarray(a, dtype=np.float32))
    x = f(x); meta_tokens = f(meta_tokens)
    w_in0 = f(w_in)[0]
    lnE_T = np.concatenate([f(ln_emb_g).reshape(8, 128).T, f(ln_emb_b).reshape(8, 128).T], axis=1)
    p = np.arange(128)
    ident = np.eye(128, dtype=np.float32)
    negtri = -(p[:, None] >= p[None, :]).astype(np.float32)
    negones = -np.ones((128, 128), np.float32)
    j = np.arange(512)
    masks = [((j[None, :] - 128 * r - p[:, None]) > 0).astype(np.float32) for r in range(4)]
    masks.append(np.broadcast_to((p[:, None] < NMETA), (128, 512)).astype(np.float32))
    cst = np.ascontiguousarray(np.concatenate([ident, negtri, negones] + masks, axis=1))
    iota16 = np.ascontiguousarray(np.broadcast_to(np.arange(16, dtype=np.float32), (128, 16)))
    w_out0 = f(w_out)[0]
    rows = []
    gains = np.zeros((128, 8), np.float32)
    ag = f(attn_norm_g)[0]; lg = f(lru_norm_g)[0]
    for r in range(4):
        rows.append(w_out0[128 * r:128 * (r + 1)])
        rows.append(w_out0[512 + 128 * r:512 + 128 * (r + 1)])
        gains[:, 2 * r] = ag[128 * r:128 * (r + 1)]
        gains[:, 2 * r + 1] = lg[128 * r:128 * (r + 1)]
    w_out_p = np.ascontiguousarray(np.concatenate(rows, axis=0))
    lnB = np.ascontiguousarray(np.stack([f(ln_emb_g), f(ln_emb_b), f(ln1_g)[0], f(ln1_b)[0], f(ln2_g)[0], f(ln2_b)[0]], axis=0))
    wq = np.ascontiguousarray(f(peer_query_w)[0].reshape(D, 2048))
    sk = f(peer_sub_keys)[0]
    skT = np.ascontiguousarray(sk.transpose(3, 0, 1, 2).reshape(128, 2048))
    down = f(peer_down)[0]; upm = f(peer_up)[0]
    cw = f(conv_w)[0]; cb = f(conv_b)[0]; gab = f(gate_a_b)[0]; gxb = f(gate_x_b)[0]; lam = f(lru_lambda)[0]
    gaw = f(gate_a_w)[0]; gxw = f(gate_x_w)[0]
    in_maps = []
    for c in range(8):
        b, g = divmod(c, 4)
        cols = np.concatenate([np.arange(base + 128 * g, base + 128 * (g + 1)) for base in (0, 512, 1024, 1536, 2048)])
        ch = slice(128 * g, 128 * (g + 1))
        lru_p = np.stack([cw[0, ch], cw[1, ch], cw[2, ch], cw[3, ch], cb[ch], gab[ch], gxb[ch], lam[ch]], axis=1)
        gate_w = np.zeros((128, 256), np.float32)
        for hh in range(2):
            gate_w[64 * hh:64 * hh + 64, 64 * hh:64 * hh + 64] = gaw[2 * g + hh]
            gate_w[64 * hh:64 * hh + 64, 128 + 64 * hh:128 + 64 * hh + 64] = gxw[2 * g + hh]
        idxm = np.zeros((128, 64), np.int32)
        for jt in range(NTB):
            for r in range(4):
                idxm[:, jt * 4 + r] = r * 1024 + (2 * g + jt % 2) * 128 + p
        in_maps.append({
            "xb": x[b], "xo": np.ascontiguousarray(x[b].reshape(NT, 128, D)[own_tiles(g)].reshape(NTB * 128, D)), "meta": meta_tokens,
            "lnE_T": np.ascontiguousarray(lnE_T), "w_in": np.ascontiguousarray(w_in0[:, cols]),
            "lru_p": np.ascontiguousarray(lru_p), "gate_w": gate_w, "cst": cst, "identf": ident,
            "w_out_p": w_out_p, "gmix": gains, "lnB": lnB, "wq": wq, "skT": skT, "down": down, "up": upm,
            "idxm": idxm, "iota16": iota16,
        })
    return in_maps


_NC_CACHE = {}
FUSED = True
A_KEYS = ["xb", "meta", "lnE_T", "w_in", "lru_p", "gate_w", "cst", "identf"]
B_KEYS = ["xo", "cst", "w_out_p", "gmix", "lnB", "wq", "skT", "down", "up", "idxm", "iota16"]


def kernel(**inputs):
    debug = bool(os.environ.get("MK_DEBUG"))
    in_maps = host_inputs(**inputs)
    if not FUSED:
        if "A" not in _NC_CACHE:
            _NC_CACHE["A"] = build(False, mode="A")
            _NC_CACHE["B"] = build(False, mode="B")
        ra = run_bass_kernel_spmd(_NC_CACHE["A"], [{k: m[k] for k in A_KEYS} for m in in_maps], core_ids=list(range(8)))
        mapsb = []
        for c in range(8):
            b = c // 4
            m = {k: in_maps[c][k] for k in B_KEYS}
            for i in range(8):
                m["mixall%d" % i] = np.ascontiguousarray(np.concatenate([ra.results[4 * b + r]["mixbuf%d" % i] for r in range(4)], axis=0))
            mapsb.append(m)
        res = run_bass_kernel_spmd(_NC_CACHE["B"], mapsb, core_ids=list(range(8)))
        outp = np.zeros((2, SEQ, D), np.float32)
        for c in range(8):
            b, g = divmod(c, 4)
            outp[b].reshape(NT, 128, D)[own_tiles(g)] = res.results[c]["out"].reshape(NTB, 128, D)
        return outp
    if debug not in _NC_CACHE:
        _NC_CACHE[debug] = build(debug)
    nc = _NC_CACHE[debug]
    res = run_bass_kernel_spmd(nc, in_maps, core_ids=list(range(8)))
    outp = np.zeros((2, SEQ, D), np.float32)
    for c in range(8):
        b, g = divmod(c, 4)
        outp[b].reshape(NT, 128, D)[own_tiles(g)] = res.results[c]["out"].reshape(NTB, 128, D)
    if debug:
        kernel.dbg = [res.results[c]["dbg"] for c in range(8)]
    return outp
```

```python
import os
from contextlib import ExitStack
import numpy as np
import concourse.bass as bass
import concourse.mybir as mybir
from concourse.bass_utils import run_bass_kernel_spmd

F32 = mybir.dt.float32
BF16 = mybir.dt.bfloat16
I32 = mybir.dt.int32
U32 = mybir.dt.uint32
ALU = mybir.AluOpType
AF = mybir.ActivationFunctionType
AX = mybir.AxisListType

D = 1024
SEQ = 8192
NMETA = 16
T = SEQ + NMETA
NT = 64
EPS = 1e-5
ALPHA = 2.0 ** 0.25
NTB = 16
NSLOT = 6
GELU_C = 1.5957691216057308


class Sched:
    def __init__(self, nc, es):
        self.nc = nc
        self.es = es
        self.epoch = 0
        self.engs = ["pe", "act", "dve", "pool", "sp"]
        self.csem = {e: es.enter_context(nc.semaphore("c_" + e)) for e in self.engs}
        self.csem_ids = {id(v) for v in self.csem.values()}
        self.cnt = {e: 0 for e in self.engs}
        self.slots = {}
        for e in ["sp", "pool", "act"]:
            self.slots[e] = [[es.enter_context(nc.semaphore("d_%s%d" % (e, i))), 0] for i in range(NSLOT)]
        self.slot_next = {e: 0 for e in self.slots}
        self.ccsem = es.enter_context(nc.semaphore("ccsem"))
        self.cccnt = 0
        self.ops = {e: [] for e in self.engs}
        self.waited = {e: {} for e in self.engs}
        self.last_w = {}
        self.readers = {}
        self.nwaits = 0

    def _wait(self, eng, ev):
        sem, val = ev
        k = id(sem)
        if self.waited[eng].get(k, 0) >= val:
            return
        self.waited[eng][k] = val
        self.ops[eng].append(("wait", sem, val))
        self.nwaits += 1

    PSUM_KEYS = {"pT", "pP", "pV", "pG", "pL", "pZ", "pE", "pO", "pX", "bpT", "bpY", "bpQ", "bpS", "bpU", "bpTb"}

    def _split(self, reads, writes):
        r2 = []; w2 = list(writes)
        for b in reads:
            nm = b[0] if isinstance(b, tuple) else b
            if nm in self.PSUM_KEYS:
                if b not in w2:
                    w2.append(b)
            else:
                r2.append(b)
        return r2, w2

    def _deps(self, eng, reads, writes):
        deps = []
        for b in reads:
            if b in self.last_w:
                deps.append(self.last_w[b])
        for b in writes:
            rd = self.readers.get(b, ())
            if rd:
                deps.extend(rd)
            elif b in self.last_w:
                deps.append(self.last_w[b])
        for ev in deps:
            if eng == "pe" and ev[0] is self.csem["pe"]:
                continue
            self._wait(eng, ev)

    def _commit(self, ev, reads, writes):
        for b in writes:
            self.last_w[b] = ev
            self.readers[b] = []
        for b in reads:
            self.readers.setdefault(b, []).append(ev)

    cur_level = 0

    def lv(self, level):
        self.cur_level = level

    def capture_begin(self):
        self.cap = []

    def capture_end(self):
        c = self.cap
        self.cap = None
        return c

    def replay(self, lst, n):
        for _ in range(min(n, len(lst))):
            kind, eng, fn, reads, writes = lst.pop(0)[:5]
            (self.op if kind == "op" else self.dma)(eng, fn, reads, writes)

    def op(self, eng, fn, reads=(), writes=()):
        if getattr(self, "cap", None) is not None:
            self.cap.append(("op", eng, fn, list(reads), list(writes), self.cur_level))
            return None
        reads, writes = self._split(reads, writes)
        self._deps(eng, reads, writes)
        self.cnt[eng] += 1
        ev = (self.csem[eng], self.cnt[eng])
        self.ops[eng].append(("op", fn, ev[0], 1))
        self._commit(ev, reads, writes)
        return ev

    def new_slot(self, name):
        return [self.es.enter_context(self.nc.semaphore(name)), 0]

    def dma(self, eng, fn, reads=(), writes=(), slot=None):
        if getattr(self, "cap", None) is not None:
            assert slot is None
            self.cap.append(("dma", eng, fn, list(reads), list(writes), self.cur_level))
            return None
        reads, writes = self._split(reads, writes)
        self._deps(eng, reads, writes)
        if slot is None:
            i = self.slot_next[eng]
            self.slot_next[eng] = (i + 1) % NSLOT
            slot = self.slots[eng][i]
            if slot[1] > 0:
                self._wait(eng, (slot[0], slot[1]))
            self.all_slots_touch = True
        slot[1] += 16
        ev = (slot[0], slot[1])
        self.ops[eng].append(("op", fn, ev[0], 16))
        self._commit(ev, reads, writes)
        return ev

    def cc(self, fn, reads=(), writes=()):
        eng = "pool"
        self._deps(eng, reads, writes)
        self.cccnt += 1
        ev = (self.ccsem, self.cccnt)
        self.ops[eng].append(("op", fn, ev[0], 1))
        self._commit(ev, reads, writes)
        return ev

    def fence(self, eng="sp"):
        for e in self.engs:
            if self.cnt[e] > 0:
                self._wait(eng, (self.csem[e], self.cnt[e]))
        for e in self.slots:
            for s in self.slots[e]:
                if s[1] > 0:
                    self._wait(eng, (s[0], s[1]))
        for s in getattr(self, "extra_slots", []):
            if s[1] > 0:
                self._wait(eng, (s[0], s[1]))
        if self.cccnt:
            self._wait(eng, (self.ccsem, self.cccnt))

    def emit(self):
        nc = self.nc
        ops = self.ops

        def mk(name):
            def body(e):
                for it in ops[name]:
                    if it[0] == "wait":
                        e.wait_ge(it[1], it[2])
                    else:
                        it[1](e).then_inc(it[2], it[3])
            return body

        with nc.Block() as block:
            block.tensor(mk("pe"))
            block.scalar(mk("act"))
            block.vector(mk("dve"))
            block.gpsimd(mk("pool"))
            block.sync(mk("sp"))
        self.ops = {e: [] for e in self.engs}
        self.new_epoch()

    def new_epoch(self):
        self.epoch += 1
        old_ids = self.csem_ids
        self.csem = {e: self.es.enter_context(self.nc.semaphore("c%d_%s" % (self.epoch, e))) for e in self.engs}
        self.csem_ids = old_ids | {id(v) for v in self.csem.values()}
        self.cnt = {e: 0 for e in self.engs}
        keep = lambda ev: id(ev[0]) not in old_ids
        self.last_w = {b: ev for b, ev in self.last_w.items() if keep(ev)}
        self.readers = {b: [ev for ev in evs if keep(ev)] for b, evs in self.readers.items()}


def APx(base, free, off=0):
    p = base.ap[0]
    return bass.AP(base.tensor, base.offset + off, [list(p)] + [list(f) for f in free])


def build(debug=False, stop=None, mode="fused"):
    stop = stop or os.environ.get('MK_STOP')
    if mode == "A":
        stop = "A2"
    nc = bass.Bass("TRN2", target_bir_lowering=False)

    A_IN = {"xb", "meta", "lnE_T", "w_in", "lru_p", "gate_w", "cst", "identf"}
    B_IN = {"xo", "cst", "w_out_p", "gmix", "lnB", "wq", "skT", "down", "up", "idxm", "iota16"}

    def din(name, shape, dt=F32):
        if (mode == "A" and name not in A_IN) or (mode == "B" and name not in B_IN):
            return None
        return nc.dram_tensor(name, list(shape), dt, kind="ExternalInput").ap()

    xb = din("xb", [SEQ, D])
    xo = din("xo", [NTB * 128, D])
    meta = din("meta", [NMETA, D])
    lnE_T = din("lnE_T", [128, 16])
    w_in = din("w_in", [D, 640])
    lru_p = din("lru_p", [128, 8])
    gate_w = din("gate_w", [128, 256])
    cst = din("cst", [128, 384 + 5 * 512])
    identf_d = din("identf", [128, 128])
    w_out_p = din("w_out_p", [D, D])
    gmix_d = din("gmix", [128, 8])
    lnB = din("lnB", [6, D])
    wq_d = din("wq", [D, 2048])
    skT_d = din("skT", [128, 2048])
    down = din("down", [16384, D])
    up = din("up", [16384, D])
    idxm_d = din("idxm", [128, 64], I32)
    iota_d = din("iota16", [128, 16])
    out = None if mode == "A" else nc.dram_tensor("out", [NTB * 128, D], F32, kind="ExternalOutput").ap()
    mixbuf = [nc.dram_tensor("mixbuf%d" % i, [1024, 256], F32, kind=("ExternalOutput" if mode == "A" else "Internal")).ap() for i in range(8)]
    mixall = [nc.dram_tensor("mixall%d" % i, [4 * 1024, 256], F32, kind=("ExternalInput" if mode == "B" else "Internal")).ap() for i in range(8)]

    def mixdst(ti):
        r0 = (ti % 8) * 128
        return mixbuf[ti // 8][r0:r0 + 128, :]
    downb = upb = None
    if mode != "A":
        downb = nc.dram_tensor("downb", [16384, D], BF16, kind="Internal").ap()
        upb = nc.dram_tensor("upb", [16384, D], BF16, kind="Internal").ap()
    conv_jobs = []
    if mode != "A":
        for j in range(8):
            conv_jobs.append((down, downb, "downb", j))
            conv_jobs.append((up, upb, "upb", j))

    def convert_step(S, n=1):
        for _ in range(n):
            if not conv_jobs:
                return
            src, dst, nm, j = conv_jobs.pop(0)
            S.dma("pool", lambda e, src=src, dst=dst, j=j: e.dma_start(out=dst[j * 2048:(j + 1) * 2048, :], in_=src[j * 2048:(j + 1) * 2048, :]),
                  writes=[(nm, j)])

    dbg = None
    if debug:
        dbg = nc.dram_tensor("dbg", [SEQ, 256], F32, kind="ExternalOutput").ap()

    top = ExitStack()
    with top:
        S = Sched(nc, top)

        def sb(es, name, shape, dt=F32):
            return es.enter_context(nc.sbuf_tensor(name, list(shape), dt))

        def ps(es, name, shape, dt=F32):
            return es.enter_context(nc.psum_tensor(name, list(shape), dt))

        esA = ExitStack()
        if mode == "B":
            build_phase_b(nc, S, sb, ps, locals())
            return nc
        with esA:
            ident = sb(esA, "ident", [128, 128], BF16)
            identf = sb(esA, "identf_s", [128, 128], F32)
            negtri = sb(esA, "negtri", [128, 128], BF16)
            negones = sb(esA, "negones", [128, 128], BF16)
            masks = sb(esA, "masks", [128, 5, 512], BF16)
            lnT = sb(esA, "lnT", [128, 16])
            lrup = sb(esA, "lrup", [128, 8])
            lruc = sb(esA, "lruc", [128, 8])
            gw = sb(esA, "gw", [128, 256], BF16)
            qT0 = sb(esA, "qT0", [128, T], BF16)
            qT1 = sb(esA, "qT1", [128, T], BF16)
            kT = sb(esA, "kT", [128, T], BF16)
            vtok = sb(esA, "vtok", [128, NT + 1, 128], BF16)

            S.dma("pool", lambda e: e.dma_start(out=ident[:], in_=cst[:, 0:128]), writes=["ident"])
            S.dma("pool", lambda e: e.dma_start(out=negtri[:], in_=cst[:, 128:256]), writes=["negtri"])
            S.dma("pool", lambda e: e.dma_start(out=negones[:], in_=cst[:, 256:384]), writes=["negones"])
            S.dma("pool", lambda e: e.dma_start(out=masks[:], in_=cst[:, 384:].rearrange("p (r c) -> p r c", r=5)),
                  writes=["masks"])
            S.dma("pool", lambda e: e.dma_start(out=gw[:], in_=gate_w), writes=["gw"])
            S.dma("sp", lambda e: e.dma_start(out=identf[:], in_=identf_d), writes=["identf"])
            S.dma("sp", lambda e: e.dma_start(out=lnT[:], in_=lnE_T), writes=["lnT"])
            S.dma("sp", lambda e: e.dma_start(out=lrup[:], in_=lru_p), writes=["lrup"])
            S.op("dve", lambda e: e.memset(qT0[:], 0.0), writes=["qT0"])
            S.op("pool", lambda e: e.memset(qT1[:], 0.0), writes=["qT1"])
            S.op("dve", lambda e: e.memset(vtok[:, NT, :], 0.0), writes=["vtok"])
            S.op("act", lambda e: e.activation(out=lruc[:, 2:3], in_=lrup[:, 7:8], func=AF.Exp, scale=-1.0),
                 reads=["lrup"], writes=["lruc"])
            S.op("act", lambda e: e.activation(out=lruc[:, 3:4], in_=lruc[:, 2:3], func=AF.Ln, bias=1.0),
                 reads=["lruc"], writes=["lruc"])
            S.op("dve", lambda e: e.tensor_scalar(lruc[:, 0:1], lruc[:, 3:4], -8.0, None, op0=ALU.mult),
                 reads=["lruc"], writes=["lruc"])
            S.op("dve", lambda e: e.tensor_scalar(lruc[:, 1:2], lruc[:, 3:4], -16.0, None, op0=ALU.mult),
                 reads=["lruc"], writes=["lruc"])
            S.op("dve", lambda e: e.tensor_scalar(lruc[:, 4:6], lrup[:, 5:7], -1.0, None, op0=ALU.mult),
                 reads=["lrup", "lruc"], writes=["lruc"])

            esA1 = ExitStack()
            with esA1:
                W = sb(esA1, "W", [128, 8, 640], BF16)
                NXB = 5
                NB6 = 8
                xt = [sb(esA1, "xt%d" % i, [128, D]) for i in range(NXB)]
                xn = [sb(esA1, "xn%d" % i, [128, D], BF16) for i in range(3)]
                hT = [sb(esA1, "hT%d" % i, [128, 8, 128], BF16) for i in range(3)]
                st = [sb(esA1, "st%d" % i, [128, 2, 6]) for i in range(4)]
                mv = [sb(esA1, "mv%d" % i, [128, 4]) for i in range(4)]
                xrb = [sb(esA1, "xrb%d" % i, [128, 3 + 128]) for i in range(2)]
                grb = [sb(esA1, "grb%d" % i, [128, 128]) for i in range(NB6)]
                L = {}
                for nm in ["xc", "e1", "e2", "ea", "ea2", "bi", "hh", "g1", "g2", "lo"]:
                    L[nm] = [sb(esA1, "L%s%d" % (nm, i), [128, 128]) for i in range(NB6)]
                xcb = [sb(esA1, "xcb%d" % i, [128, 128], BF16) for i in range(3)]
                hst = sb(esA1, "hst", [128, 1])
                lstage = [sb(esA1, "lstage%d" % i, [128, 128]) for i in range(3)]
                pT = [ps(esA1, "pT%d" % i, [128, 8, 128], BF16) for i in range(2)]
                pP = [ps(esA1, "pP%d" % i, [128, 512]) for i in range(2)]
                pV = [ps(esA1, "pV%d" % i, [128, 512]) for i in range(1)]
                pG = [ps(esA1, "pG%d" % i, [128, 512]) for i in range(2)]
                pL = ps(esA1, "pL", [128, 128])

                for kc in range(8):
                    S.dma("pool", lambda e, kc=kc: e.dma_start(out=W[:, kc, :], in_=w_in[kc * 128:(kc + 1) * 128, :]),
                          writes=[("W", kc)])
                S.op("dve", lambda e: e.memset(xrb[0][:], 0.0), writes=[("xrbh", 0), ("xrb", 0)])
                S.op("dve", lambda e: e.memset(hst[:], 0.0), writes=["hst"])

                tiles = [-1] + list(range(NT))

                def tinfo(ti):
                    if ti < 0:
                        return NMETA, 0, NT
                    return 128, NMETA + 128 * ti, ti

                def tile_prog(k):
                    ti = tiles[k]
                    n, c0, vs = tinfo(ti)
                    x_ = xt[k % NXB]; xk = ("xt", k % NXB)
                    s_ = st[k % 4]; m_ = mv[k % 4]
                    sk = ("st", k % 4); mk_ = ("mv", k % 4)
                    S.lv(0)
                    if ti < 0:
                        S.dma("sp", lambda e: e.dma_start(out=x_[0:n, :], in_=meta), writes=[xk])
                    else:
                        S.dma("sp", lambda e: e.dma_start(out=x_[:, :], in_=xb[ti * 128:(ti + 1) * 128, :]), writes=[xk])
                    S.lv(1)
                    S.op("dve", lambda e: e.bn_stats(s_[0:n, 0, :], x_[0:n, 0:512]), reads=[xk], writes=[sk])
                    S.op("dve", lambda e: e.bn_stats(s_[0:n, 1, :], x_[0:n, 512:1024]), reads=[xk], writes=[sk])
                    S.op("dve", lambda e: e.bn_aggr(m_[0:n, 0:2], s_[0:n].rearrange("p a b -> p (a b)")), reads=[sk], writes=[mk_])
                    S.lv(2)
                    S.op("act", lambda e: e.activation(out=m_[0:n, 2:3], in_=m_[0:n, 1:2], func=AF.Ln, bias=EPS),
                         reads=[mk_], writes=[mk_])
                    S.op("act", lambda e: e.activation(out=m_[0:n, 3:4], in_=m_[0:n, 2:3], func=AF.Exp, scale=-0.5),
                         reads=[mk_], writes=[mk_])
                    S.lv(3)
                    xn_ = xn[k % 3]; xnk = ("xn", k % 3)
                    S.op("dve", lambda e: e.tensor_scalar(xn_[0:n, :], x_[0:n, :], m_[0:n, 0:1], m_[0:n, 3:4],
                                                          op0=ALU.subtract, op1=ALU.mult), reads=[xk, mk_], writes=[xnk])
                    S.lv(4)
                    pT_ = pT[k % 2]; pk = ("pT", k % 2)
                    for kc in range(8):
                        S.op("pe", lambda e, kc=kc: e.transpose(pT_[:, kc, 0:n], xn_[0:n, kc * 128:(kc + 1) * 128], ident[0:n, 0:n]),
                             reads=[xnk, "ident"], writes=[pk])
                    S.lv(5)
                    h_ = hT[k % 3]; hk = ("hT", k % 3)
                    for kc in range(8):
                        S.op("act", lambda e, kc=kc: e.activation(out=h_[:, kc, 0:n], in_=pT_[:, kc, 0:n], func=AF.Identity,
                                                                    scale=lnT[:, kc:kc + 1], bias=lnT[:, 8 + kc:9 + kc]),
                             reads=[pk, "lnT"], writes=[hk])
                    S.lv(6)
                    pP_ = pP[k % 2]; ppk = ("pP", k % 2)
                    pV_ = pV[0]; pvk = ("pV", 0)
                    wk = [("W", kc) for kc in range(8)]
                    for j, cb in enumerate([0, 128, 384, 512]):
                        for kc in range(8):
                            S.op("pe", lambda e, j=j, cb=cb, kc=kc: e.matmul(pP_[:, j * 128:j * 128 + n], W[:, kc, cb:cb + 128], h_[:, kc, 0:n],
                                                                               start=(kc == 0), stop=(kc == 7)),
                                 reads=[hk] + wk, writes=[ppk])
                    for kc in range(8):
                        S.op("pe", lambda e, kc=kc: e.matmul(pV_[0:n, 0:128], h_[:, kc, 0:n], W[:, kc, 256:384],
                                                             start=(kc == 0), stop=(kc == 7)),
                             reads=[hk] + wk, writes=[pvk])
                    S.lv(7)
                    xr_ = xrb[k % 2]; xrn = xrb[(k + 1) % 2]
                    g_ = grb[k % NB6]; gk = ("grb", k % NB6)
                    S.op("act", lambda e: e.activation(out=qT0[0:64, c0:c0 + n], in_=pP_[0:64, 0:n], func=AF.Identity, scale=0.125),
                         reads=[ppk], writes=["qT0"])
                    S.op("act", lambda e: e.activation(out=qT1[64:128, c0:c0 + n], in_=pP_[64:128, 0:n], func=AF.Identity, scale=0.125),
                         reads=[ppk], writes=["qT1"])
                    S.op("act", lambda e: e.activation(out=kT[:, c0:c0 + n], in_=pP_[:, 128:128 + n], func=AF.Identity), reads=[ppk], writes=["kT"])
                    S.op("act", lambda e: e.activation(out=xr_[:, 3:3 + n], in_=pP_[:, 256:256 + n], func=AF.Identity),
                         reads=[ppk], writes=[("xrb", k % 2)])
                    S.op("act", lambda e: e.activation(out=g_[:, 0:n], in_=pP_[:, 384:384 + n], func=AF.Identity),
                         reads=[ppk], writes=[gk])
                    S.op("dve", lambda e: e.tensor_copy(vtok[0:n, vs, :], pV_[0:n, 0:128]), reads=[pvk], writes=["vtok"])
                    S.lv(8)
                    rk = [("xrb", k % 2), ("xrbh", k % 2)]
                    b = k % NB6
                    e1, e2, ea, ea2, bi, hh, g1, g2, lo, xc = (L[x][b] for x in ["e1", "e2", "ea", "ea2", "bi", "hh", "g1", "g2", "lo", "xc"])
                    K = lambda nm: (nm, b)
                    S.op("dve", lambda e: e.tensor_scalar(xc[:, 0:n], xr_[:, 3:3 + n], lrup[:, 3:4], lrup[:, 4:5], op0=ALU.mult, op1=ALU.add),
                         reads=rk + ["lrup"], writes=[K("xc")])
                    for kk in (2, 1, 0):
                        S.op("dve", lambda e, kk=kk: e.scalar_tensor_tensor(out=xc[:, 0:n], in0=xr_[:, kk:kk + n], scalar=lrup[:, kk:kk + 1],
                                                                             in1=xc[:, 0:n], op0=ALU.mult, op1=ALU.add),
                             reads=rk + ["lrup", K("xc")], writes=[K("xc")])
                    S.op("dve", lambda e: e.tensor_copy(xrn[:, 0:3], xr_[:, n:n + 3]), reads=rk, writes=[("xrbh", (k + 1) % 2)])
                    xb_ = xcb[k % 3]; xbk = ("xcb", k % 3)
                    S.op("dve", lambda e: e.tensor_copy(xb_[:, 0:n], xc[:, 0:n]), reads=[K("xc")], writes=[xbk])
                    if ti >= 0:
                        S.op("pool", lambda e: e.tensor_tensor(out=g1[:, 0:n], in0=g_[:, 0:n], in1=g_[:, 0:n], op=ALU.mult), reads=[gk], writes=[K("g1")])
                        S.op("pool", lambda e: e.tensor_scalar(g1[:, 0:n], g1[:, 0:n], 0.044715, 1.0, op0=ALU.mult, op1=ALU.add),
                             reads=[K("g1")], writes=[K("g1")])
                        S.op("pool", lambda e: e.tensor_tensor(out=g1[:, 0:n], in0=g1[:, 0:n], in1=g_[:, 0:n], op=ALU.mult), reads=[K("g1"), gk], writes=[K("g1")])
                    S.lv(9)
                    pG_ = pG[k % 2]; pgk = ("pG", k % 2)
                    S.op("pe", lambda e: e.matmul(pG_[:, 0:n], gw[:, 0:128], xb_[:, 0:n], start=True, stop=True),
                         reads=[xbk, "gw"], writes=[pgk])
                    S.op("pe", lambda e: e.matmul(pG_[:, 128:128 + n], gw[:, 128:256], xb_[:, 0:n], start=True, stop=True),
                         reads=[xbk, "gw"], writes=[pgk])
                    S.lv(10)
                    A = lambda out, in_, **kw: S.op("act", lambda e: e.activation(out=out, in_=in_, **kw["a"]), reads=kw["r"], writes=kw["w"])
                    A(e1[:, 0:n], pG_[:, 0:n], a=dict(func=AF.Exp, scale=-1.0, bias=lruc[:, 4:5]), r=[pgk, "lruc"], w=[K("e1")])
                    A(e2[:, 0:n], pG_[:, 128:128 + n], a=dict(func=AF.Exp, scale=-1.0, bias=lruc[:, 5:6]), r=[pgk, "lruc"], w=[K("e2")])
                    A(e1[:, 0:n], e1[:, 0:n], a=dict(func=AF.Ln, bias=1.0), r=[K("e1")], w=[K("e1")])
                    A(e2[:, 0:n], e2[:, 0:n], a=dict(func=AF.Ln, bias=1.0), r=[K("e2")], w=[K("e2")])
                    A(e1[:, 0:n], e1[:, 0:n], a=dict(func=AF.Exp, scale=-1.0), r=[K("e1")], w=[K("e1")])
                    A(e2[:, 0:n], e2[:, 0:n], a=dict(func=AF.Exp, scale=-1.0), r=[K("e2")], w=[K("e2")])
                    A(ea[:, 0:n], e1[:, 0:n], a=dict(func=AF.Exp, scale=lruc[:, 0:1]), r=[K("e1"), "lruc"], w=[K("ea")])
                    A(ea2[:, 0:n], e1[:, 0:n], a=dict(func=AF.Exp, scale=lruc[:, 1:2]), r=[K("e1"), "lruc"], w=[K("ea2")])
                    A(ea2[:, 0:n], ea2[:, 0:n], a=dict(func=AF.Ln, scale=-1.0, bias=1.0), r=[K("ea2")], w=[K("ea2")])
                    A(ea2[:, 0:n], ea2[:, 0:n], a=dict(func=AF.Exp, scale=0.5), r=[K("ea2")], w=[K("ea2")])
                    if ti >= 0:
                        A(g2[:, 0:n], g1[:, 0:n], a=dict(func=AF.Exp, scale=-GELU_C), r=[K("g1")], w=[K("g2")])
                        A(g2[:, 0:n], g2[:, 0:n], a=dict(func=AF.Ln, bias=1.0), r=[K("g2")], w=[K("g2")])
                        A(g2[:, 0:n], g2[:, 0:n], a=dict(func=AF.Exp, scale=-1.0), r=[K("g2")], w=[K("g2")])
                    S.lv(11)
                    S.op("dve", lambda e: e.tensor_tensor(out=bi[:, 0:n], in0=e2[:, 0:n], in1=xc[:, 0:n], op=ALU.mult),
                         reads=[K("e2"), K("xc")], writes=[K("bi")])
                    S.op("dve", lambda e: e.tensor_tensor(out=bi[:, 0:n], in0=bi[:, 0:n], in1=ea2[:, 0:n], op=ALU.mult),
                         reads=[K("bi"), K("ea2")], writes=[K("bi")])
                    S.op("dve", lambda e: e.tensor_tensor_scan(hh[:, 0:n], ea[:, 0:n], bi[:, 0:n], hst[:, 0:1], op0=ALU.mult, op1=ALU.add),
                         reads=[K("ea"), K("bi"), "hst"], writes=[K("hh")])
                    S.op("dve", lambda e: e.tensor_copy(hst[:, 0:1], hh[:, n - 1:n]), reads=[K("hh")], writes=["hst"])
                    if ti < 0:
                        return
                    S.op("dve", lambda e: e.tensor_tensor(out=g2[:, 0:n], in0=g2[:, 0:n], in1=g_[:, 0:n], op=ALU.mult), reads=[K("g2"), gk], writes=[K("g2")])
                    S.op("dve", lambda e: e.tensor_tensor(out=lo[:, 0:n], in0=g2[:, 0:n], in1=hh[:, 0:n], op=ALU.mult),
                         reads=[K("g2"), K("hh")], writes=[K("lo")])
                    S.lv(12)
                    S.op("pe", lambda e: e.transpose(pL[:, :], lo[:, :], identf[:, :]), reads=[K("lo"), "identf"], writes=["pL"])
                    S.lv(13)
                    ls = lstage[k % 3]
                    S.op("act", lambda e: e.activation(out=ls[:, :], in_=pL[:, :], func=AF.Identity), reads=["pL"], writes=[("ls", k % 3)])
                    S.lv(14)
                    S.dma("sp", lambda e: e.dma_start(out=mixdst(ti)[:, 128:256], in_=ls[:, :]),
                          reads=[("ls", k % 3)], writes=[("mixbuf", ti, 2)])

                nk = len(tiles)
                NLV = 15
                buckets = {}
                for k in range(nk):
                    S.capture_begin()
                    tile_prog(k)
                    for ent in S.capture_end():
                        buckets.setdefault((k + ent[5], k), []).append(ent)
                S.lv(0)
                laststep = -1
                for (step, k) in sorted(buckets):
                    if step != laststep:
                        laststep = step
                        if step >= 8 and step % 4 == 0:
                            convert_step(S)
                    lst = buckets[(step, k)]
                    S.replay(lst, len(lst))
                if stop == "A1":
                    S.fence("sp")
                S.emit()
            if stop == "A1":
                return nc

            esA2 = ExitStack()
            with esA2:
                Eb = [sb(esA2, "Eb%d" % i, [128, 512]) for i in range(2)]
                Lb = [sb(esA2, "Lb%d" % i, [128, 512], BF16) for i in range(2)]
                Ls = [sb(esA2, "Ls%d" % i, [128, 512], BF16) for i in range(2)]
                At = [sb(esA2, "At%d" % i, [128, 512], BF16) for i in range(2)]
                osb = [sb(esA2, "osb%d" % i, [128, 512]) for i in range(2)]
                ostg = [sb(esA2, "ostg%d" % i, [128, 4, 64]) for i in range(2)]
                pZ = [ps(esA2, "pZ%d" % i, [128, 512]) for i in range(3)]
                pO = [ps(esA2, "pO%d" % i, [128, 512]) for i in range(2)]
                pX = ps(esA2, "pX", [128, 512])
                qTs = [qT0, qT1]
                def attn_group(i, hd, gi):
                    if True:
                        qT = qTs[hd]
                        qk = "qT%d" % hd
                        q0 = NMETA + 512 * i
                        seq = [(4 * i + r, r) for r in (3, 2, 1, 0)] + [(m, None) for m in range(4 * i - 1, -1, -1)] + [(NT, 4)]
                        N = len(seq)
                        pO_ = pO[gi % 2]; pok = ("pO", gi % 2)

                        def kcols(m):
                            if m == NT:
                                return slice(0, 128)
                            return slice(NMETA + 128 * m, NMETA + 128 * (m + 1))

                        def sZ(n):
                            m, mk = seq[n]
                            b = n % 2
                            b3 = n % 3
                            S.op("pe", lambda e: e.matmul(pZ[b3][:, :], kT[:, kcols(m)], qT[:, q0:q0 + 512], start=True, stop=False),
                                 reads=["kT", qk], writes=[("pZ", b3)])
                            S.op("act", lambda e: e.activation(out=Eb[b][:, :], in_=pZ[b3][:, :], func=AF.Exp),
                                 reads=[("pZ", b3)], writes=[("Eb", b)])

                        def sZ2(n):
                            m, mk = seq[n]
                            b = n % 2
                            S.op("act", lambda e: e.activation(out=Lb[b][:, :], in_=Eb[b][:, :], func=AF.Ln, bias=1.0),
                                 reads=[("Eb", b)], writes=[("Lb", b)])
                            if mk is not None:
                                S.op("dve", lambda e: e.tensor_tensor(out=Lb[b][:, :], in0=Lb[b][:, :], in1=masks[:, mk, :], op=ALU.mult),
                                     reads=[("Lb", b), "masks"], writes=[("Lb", b)])

                        def sE(n):
                            m, mk = seq[n]
                            b = n % 2
                            b3 = n % 3
                            S.op("pe", lambda e: e.matmul(pZ[b3][:, :], negtri[:, :], Lb[b][:, :], start=False, stop=(n == 0), skip_group_check=True),
                                 reads=["negtri", ("Lb", b)], writes=[("pZ", b3)])
                            if n > 0:
                                S.op("pe", lambda e: e.matmul(pZ[b3][:, :], negones[:, :], Ls[b][:, :], start=False, stop=True, skip_group_check=True),
                                     reads=["negones", ("Ls", b)], writes=[("pZ", b3)])
                            S.op("act", lambda e: e.activation(out=At[b][:, :], in_=pZ[b3][:, :], func=AF.Exp),
                                 reads=[("pZ", b3)], writes=[("At", b)])
                            if mk is not None:
                                S.op("dve", lambda e: e.tensor_tensor(out=At[b][:, :], in0=At[b][:, :], in1=masks[:, mk, :], op=ALU.mult),
                                     reads=[("At", b), "masks"], writes=[("At", b)])
                            if n + 1 < N:
                                nb = (n + 1) % 2
                                if n == 0:
                                    S.op("dve", lambda e: e.tensor_copy(Ls[nb][:, :], Lb[b][:, :]), reads=[("Lb", b)], writes=[("Ls", nb)])
                                else:
                                    S.op("dve", lambda e: e.tensor_tensor(out=Ls[nb][:, :], in0=Ls[b][:, :], in1=Lb[b][:, :], op=ALU.add),
                                         reads=[("Lb", b), ("Ls", b)], writes=[("Ls", nb)])

                        def sO(n):
                            m, mk = seq[n]
                            b = n % 2
                            S.op("pe", lambda e: e.matmul(pO_[:, :], vtok[:, m, :], At[b][:, :], start=(n == 0), stop=(n == N - 1)),
                                 reads=["vtok", ("At", b)], writes=[pok])

                        for n in range(-1, N + 1):
                            if 0 <= n + 1 < N:
                                sZ(n + 1)
                                sZ2(n + 1)
                            if 0 <= n < N:
                                sE(n)
                            if 0 <= n - 1 < N:
                                sO(n - 1)
                        ob = osb[gi % 2]; obk = ("osb", gi % 2)
                        S.op("dve", lambda e: e.tensor_copy(ob[:, :], pO_[:, :]), reads=[pok], writes=[obk])
                        for j in range(4):
                            S.op("pe", lambda e, j=j: e.transpose(pX[:, j * 128:(j + 1) * 128], ob[:, j * 128:(j + 1) * 128], identf[:, :]),
                                 reads=[obk, "identf"], writes=["pX"])
                        og = ostg[gi % 2]; ogk = ("ostg", gi % 2)
                        S.op("act", lambda e: e.activation(out=og[:, :, :], in_=pX[:, :].rearrange("p (j c) -> p j c", j=4)[:, :, 64 * hd:64 * hd + 64],
                                                           func=AF.Identity), reads=["pX"], writes=[ogk])
                        for j in range(4):
                            S.dma("sp", lambda e, j=j: e.dma_start(out=mixdst(4 * i + j)[:, 64 * hd:64 * hd + 64], in_=og[:, j, :]),
                                  reads=[ogk], writes=[("mixbuf", 4 * i + j, hd)])

                gi = 0
                for i in range(16):
                    for hd in range(2):
                        attn_group(i, hd, gi)
                        gi += 1
                    if i % 2 == 1 and stop != "A2":
                        sp_ = i // 2
                        ks = [("mixbuf", ti, q) for ti in range(8 * sp_, 8 * sp_ + 8) for q in range(3)]
                        S.cc(lambda e, sp_=sp_: e.collective_compute("AllGather", ALU.bypass, replica_groups=[[0, 1, 2, 3], [4, 5, 6, 7]],
                                                                     ins=[mixbuf[sp_]], outs=[mixall[sp_]]), reads=ks, writes=[("mixall", sp_)])

                S.emit()

        if debug:
            esD = ExitStack()
            with esD:
                dt_ = sb(esD, "dbgt", [128, 64, 256])
                allk = [("mixbuf", ti, q) for ti in range(NT) for q in range(3)]
                for ti in range(NT):
                    S.dma("sp", lambda e, ti=ti: e.dma_start(out=dt_[:, ti, :], in_=mixdst(ti)), reads=allk, writes=["dbgt"])
                S.dma("sp", lambda e: e.dma_start(out=dbg.rearrange("(j p) c -> p j c", p=128), in_=dt_[:]), reads=["dbgt"], writes=["dbg"])
                S.fence("sp")
                S.emit()
        if stop == "A2":
            if not debug:
                S.fence("sp")
                S.emit()
            return nc
        ndel = int(os.environ.get("MK_DELAY", 0))
        if ndel:
            esX = ExitStack()
            with esX:
                scr = sb(esX, "scr", [128, 2048])
                S.op("act", lambda e: e.activation(out=scr[:], in_=scr[:], func=AF.Identity, scale=0.0), writes=["scr"])
                for _ in range(ndel):
                    S.op("act", lambda e: e.activation(out=scr[:], in_=scr[:], func=AF.Identity), reads=["scr"], writes=["scr"])
                S.op("pool", lambda e: e.memset(scr[:, 0:1], 0.0), reads=["scr"], writes=["scr"])
                S.fence("sp")
                S.emit()

        if stop == "CC":
            S.fence("sp")
            S.emit()
            return nc
        build_phase_b(nc, S, sb, ps, locals())
    return nc


def build_phase_b(nc, S, sb, ps, env):
    xo = env["xo"]; w_out_p = env["w_out_p"]; gmix_d = env["gmix_d"]; lnB = env["lnB"]; wq_d = env["wq_d"]
    skT_d = env["skT_d"]; down = env["down"]; up = env["up"]; idxm_d = env["idxm_d"]; iota_d = env["iota_d"]
    out = env["out"]; mixall = env["mixall"]; cst = env["cst"]
    downb = env["downb"]; upb = env["upb"]
    env["convert_step"](S, 16)
    tabk_d = [("downb", j) for j in range(8)]
    tabk_u = [("upb", j) for j in range(8)]
    es = ExitStack()
    with es:
        ident = sb(es, "identB", [128, 128], BF16)
        Wo = sb(es, "Wo", [128, 8, D], BF16)
        gmix = sb(es, "gmix_s", [128, 8])
        Wq = sb(es, "Wq", [128, 8, 2048], BF16)
        SK = sb(es, "SK", [128, 16, 128], BF16)
        LNb = sb(es, "LNb", [128, 6, D])
        idxm = sb(es, "idxm_s", [128, 64], I32)
        iota = sb(es, "iota_s", [128, 16])
        WD = sb(es, "WD", [128, 128 * 128], BF16)

        S.dma("pool", lambda e: e.dma_start(out=ident[:], in_=cst[:, 0:128]), writes=["identB"])
        S.dma("sp", lambda e: e.dma_start(out=idxm[:], in_=idxm_d), writes=["idxm"])
        S.dma("sp", lambda e: e.dma_start(out=gmix[:], in_=gmix_d), writes=["gmix"])
        for r in range(6):
            S.dma("sp", lambda e, r=r: e.dma_start(out=LNb[:, r, :], in_=bass.AP(lnB.tensor, r * D, [[0, 128], [1, D]])),
                  writes=[("LNb", r)])
        S.dma("sp", lambda e: e.dma_start(out=iota[:], in_=iota_d), writes=["iota"])
        S.op("dve", lambda e: e.memset(WD[:], 0.0), writes=["WD"])

        def load_weights():
            for kc in range(8):
                S.dma("pool", lambda e, kc=kc: e.dma_start(out=Wo[:, kc, :], in_=w_out_p[kc * 128:(kc + 1) * 128, :]), writes=[("Wo", kc)])
            for kc in range(8):
                S.op("dve", lambda e, kc=kc: e.tensor_scalar(Wo[:, kc, :], Wo[:, kc, :], gmix[:, kc:kc + 1], None, op0=ALU.mult),
                     reads=[("Wo", kc), "gmix"], writes=[("Wo", kc)])
            for kc in range(8):
                S.dma("pool", lambda e, kc=kc: e.dma_start(out=Wq[:, kc, :], in_=wq_d[kc * 128:(kc + 1) * 128, :]), writes=[("Wq", kc)])
            S.dma("pool", lambda e: e.dma_start(out=SK[:], in_=skT_d.rearrange("p (c n) -> p c n", c=16)), writes=["SK"])

        xt = sb(es, "bx", [128, D])
        mix = sb(es, "mix", [128, 4, 256])
        mixn = sb(es, "mixn", [128, 4, 256], BF16)
        mixT = sb(es, "mixT", [128, 8, 128], BF16)
        ot = sb(es, "ot", [128, D])
        junk = sb(es, "junk", [128, D], BF16)
        sq = sb(es, "sq", [128, 4, 128], BF16)
        st = sb(es, "bst", [128, 2, 6])
        sm = sb(es, "bsm", [128, 16])
        st2 = sb(es, "bst2", [128, 2, 6])
        sm2 = sb(es, "bsm2", [128, 16])
        U2 = [sb(es, "bu%d" % i, [128, D]) for i in range(2)]
        H1B = [sb(es, "h1b%d" % i, [128, D], BF16) for i in range(2)]
        h1T = mixT
        qsb = sb(es, "qsb", [128, 16, 128], BF16)
        Ssb = sb(es, "Ssb", [128, 16, 128])
        wk = sb(es, "wk", [128, 128])
        wk2 = sb(es, "wk2", [128, 256])
        v1 = sb(es, "v1", [128, 8, 16]); v2 = sb(es, "v2", [128, 8, 16])
        i1 = sb(es, "i1", [128, 8, 16], U32); i2 = sb(es, "i2", [128, 8, 16], U32)
        i1f = sb(es, "i1f", [128, 8, 16]); i2f = sb(es, "i2f", [128, 8, 16])
        cand = sb(es, "cand", [128, 8, 256])
        eq = cand[:].rearrange("p h (a b) -> p h a b", a=16)
        scv = sb(es, "scv", [128, 8, 16])
        pos = sb(es, "pos", [128, 8, 16], U32)
        ph = sb(es, "ph", [128, 8, 16], U32)
        phf = sb(es, "phf", [128, 8, 16]); plf = sb(es, "plf", [128, 8, 16])
        S1 = [sb(es, "s1_%d" % i, [128, 8, 16]) for i in range(2)]
        S2 = [sb(es, "s2_%d" % i, [128, 8, 16]) for i in range(2)]
        GSM = [sb(es, "gsm%d" % i, [128, 8, 16]) for i in range(2)]
        zs = sb(es, "zs", [128, 16])
        idxf = sb(es, "idxf", [128, 128])
        IDX32 = [sb(es, "idx32_%d" % i, [128, 128], I32) for i in range(2)]
        actv = sb(es, "actv", [128, 128])
        gl = sb(es, "gl", [128, 128])
        tb = sb(es, "tb", [128, 3, 128], BF16)
        tT = sb(es, "tT", [128, 3, 128], BF16)
        idxT = sb(es, "idxT", [128, 128], I32)
        NB = 8
        Db = [sb(es, "Db%d" % i, [128, D], BF16) for i in range(NB)]
        Ub = [sb(es, "Ub%d" % i, [128, D], BF16) for i in range(NB)]
        Dslot = [S.new_slot("dsl%d" % i) for i in range(NB)]
        Uslot = [S.new_slot("usl%d" % i) for i in range(NB)]
        S.extra_slots = Dslot + Uslot
        pT = ps(es, "bpT", [128, 8, 128], BF16)
        pTb = ps(es, "bpTb", [128, 8, 128], BF16)
        pY = [ps(es, "bpY%d" % i, [128, 512]) for i in range(2)]
        pU = [ps(es, "bpU%d" % i, [128, 512]) for i in range(2)]
        pQ = [ps(es, "bpQ%d" % i, [128, 512]) for i in range(1)] * 2
        pS = [ps(es, "bpS%d" % i, [128, 512]) for i in range(1)] * 2

        if os.environ.get("MK_VERBOSE"):
            print("phase B sbuf bytes remaining per partition:", nc.sbuf_bytes_remaining)

        def layernorm(src, srck, dst, dstk, gi_, st=st, sm=sm, stk="bst", smk="bsm"):
            S.op("dve", lambda e: e.bn_stats(st[:, 0, :], src[:, 0:512]), reads=[srck], writes=[stk])
            S.op("dve", lambda e: e.bn_stats(st[:, 1, :], src[:, 512:1024]), reads=[srck], writes=[stk])
            S.op("dve", lambda e: e.bn_aggr(sm[:, 0:2], st[:].rearrange("p a b -> p (a b)")), reads=[stk], writes=[smk])
            S.op("act", lambda e: e.activation(out=sm[:, 2:3], in_=sm[:, 1:2], func=AF.Ln, bias=EPS), reads=[smk], writes=[smk])
            S.op("act", lambda e: e.activation(out=sm[:, 3:4], in_=sm[:, 2:3], func=AF.Exp, scale=-0.5), reads=[smk], writes=[smk])
            S.op("dve", lambda e: e.tensor_scalar(dst[:, :], src[:, :], sm[:, 0:1], sm[:, 3:4], op0=ALU.subtract, op1=ALU.mult),
                 reads=[srck, smk], writes=[dstk])
            S.op("dve", lambda e: e.tensor_tensor(out=dst[:, :], in0=dst[:, :], in1=LNb[:, gi_, :], op=ALU.mult),
                 reads=[dstk, ("LNb", gi_)], writes=[dstk])
            S.op("dve", lambda e: e.tensor_tensor(out=dst[:, :], in0=dst[:, :], in1=LNb[:, gi_ + 1, :], op=ALU.add),
                 reads=[dstk, ("LNb", gi_ + 1)], writes=[dstk])

        def front(jt):
            pb = jt % 2
            u = U2[pb]; h1 = u; uk = ("bu", pb)
            s1 = S1[pb]; s2 = S2[pb]; gsm = GSM[pb]; idx32 = IDX32[pb]
            s1k = ("s1", pb); s2k = ("s2", pb); gsmk = ("gsm", pb); idxk = ("idx32", pb)
            h1b = H1B[pb]; h1bk = ("h1b", pb)
            x_ = xt; xk = "bx"
            h_ = xt; hk = "bx"
            mx = mix; mxk = "mix"
            S.dma("sp", lambda e: e.dma_start(out=x_[:, :], in_=xo[jt * 128:(jt + 1) * 128, :]), writes=[xk])
            for r in range(4):
                S.dma("pool", lambda e, r=r: e.indirect_dma_start(out=mx[:, r, :], out_offset=None, in_=mixall[jt // 2],
                                                                   in_offset=bass.IndirectOffsetOnAxis(ap=idxm[:, jt * 4 + r:jt * 4 + r + 1], axis=0)),
                      reads=["idxm", ("mixall", jt // 2)], writes=[mxk])
            layernorm(x_, xk, h_, hk, 0)
            S.op("act", lambda e: e.activation(out=sq[:, :, :], in_=mx[:, :, 0:128], func=AF.Square, accum_out=sm[:, 4:5]),
                 reads=[mxk], writes=["sq", "bsm"])
            S.op("act", lambda e: e.activation(out=sq[:, :, :], in_=mx[:, :, 128:256], func=AF.Square, accum_out=sm[:, 5:6]),
                 reads=[mxk, "sq"], writes=["sq", "bsm"])
            S.op("act", lambda e: e.activation(out=sm[:, 6:8], in_=sm[:, 4:6], func=AF.Ln, scale=1.0 / 512, bias=EPS), reads=["bsm"], writes=["bsm"])
            S.op("act", lambda e: e.activation(out=sm[:, 6:8], in_=sm[:, 6:8], func=AF.Exp, scale=-0.5), reads=["bsm"], writes=["bsm"])
            S.op("dve", lambda e: e.tensor_scalar(mixn[:, :, 0:128], mx[:, :, 0:128], sm[:, 6:7], None, op0=ALU.mult),
                 reads=[mxk, "bsm"], writes=["mixn"])
            S.op("dve", lambda e: e.tensor_scalar(mixn[:, :, 128:256], mx[:, :, 128:256], sm[:, 7:8], None, op0=ALU.mult),
                 reads=[mxk, "bsm"], writes=["mixn"])
            mflat = mixn[:].rearrange("p a b -> p (a b)")
            for kc in range(8):
                S.op("pe", lambda e, kc=kc: e.transpose(pT[:, kc, :], mflat[:, kc * 128:(kc + 1) * 128], ident[:, :]),
                     reads=["mixn", "identB"], writes=["bpT"])
            S.op("act", lambda e: e.activation(out=mixT[:, :, :], in_=pT[:, :, :], func=AF.Identity), reads=["bpT"], writes=["mixT"])
            for hf in range(2):
                for kc in range(8):
                    S.op("pe", lambda e, hf=hf, kc=kc: e.matmul(pY[hf][:, :], mixT[:, kc, :], Wo[:, kc, hf * 512:(hf + 1) * 512],
                                                                  start=(kc == 0), stop=(kc == 7)),
                         reads=["mixT", ("Wo", kc)], writes=[("bpY", hf)])
            for hf in range(2):
                S.op("dve", lambda e, hf=hf: e.scalar_tensor_tensor(out=u[:, hf * 512:(hf + 1) * 512], in0=h_[:, hf * 512:(hf + 1) * 512], scalar=ALPHA,
                                                                     in1=pY[hf][:, :], op0=ALU.mult, op1=ALU.add),
                     reads=[hk, ("bpY", hf)], writes=[uk])
            layernorm(u, uk, u, uk, 2)
            S.op("act", lambda e: e.activation(out=h1b[:, :], in_=h1[:, :], func=AF.Identity), reads=[uk], writes=[h1bk])
            for kc in range(8):
                S.op("pe", lambda e, kc=kc: e.transpose(pT[:, kc, :], h1b[:, kc * 128:(kc + 1) * 128], ident[:, :]),
                     reads=[h1bk, "identB"], writes=["bpT"])
            S.op("act", lambda e: e.activation(out=h1T[:, :, :], in_=pT[:, :, :], func=AF.Identity), reads=["bpT"], writes=["mixT"])
            for c4 in range(4):
                pq = pQ[c4 % 2]; pqk = ("bpQ", 0)
                for cj in range(4):
                    cc = c4 * 4 + cj
                    for kc in range(8):
                        S.op("pe", lambda e, cc=cc, cj=cj, kc=kc, pq=pq: e.matmul(pq[:, cj * 128:(cj + 1) * 128], Wq[:, kc, cc * 128:(cc + 1) * 128], h1T[:, kc, :],
                                                                                   start=(kc == 0), stop=(kc == 7)),
                             reads=["mixT", ("Wq", kc)], writes=[pqk])
                eng = "act" if c4 % 2 == 0 else "dve"
                if eng == "act":
                    S.op("act", lambda e, c4=c4, pq=pq: e.activation(out=qsb[:, c4 * 4:(c4 + 1) * 4, :], in_=pq[:, :].rearrange("p (a b) -> p a b", a=4), func=AF.Identity),
                         reads=[pqk], writes=[("qsb", c4)])
                else:
                    S.op("dve", lambda e, c4=c4, pq=pq: e.tensor_copy(qsb[:, c4 * 4:(c4 + 1) * 4, :], pq[:, :].rearrange("p (a b) -> p a b", a=4)),
                         reads=[pqk], writes=[("qsb", c4)])
            for c4 in range(4):
                pz = pS[c4 % 2]; pzk = ("bpS", 0)
                for cj in range(4):
                    cc = c4 * 4 + cj
                    S.op("pe", lambda e, cc=cc, cj=cj, pz=pz: e.matmul(pz[:, cj * 128:(cj + 1) * 128], qsb[:, cc, :], SK[:, cc, :], start=True, stop=True),
                         reads=[("qsb", c4), "SK"], writes=[pzk])
                S.op("act", lambda e, c4=c4, pz=pz: e.activation(out=Ssb[:, c4 * 4:(c4 + 1) * 4, :], in_=pz[:, :].rearrange("p (a b) -> p a b", a=4), func=AF.Identity),
                     reads=[pzk], writes=[("Ssb", c4)])
            for hh in range(8):
                for half, (vv, ii) in enumerate(((v1, i1), (v2, i2))):
                    cc = 2 * hh + half
                    sk_ = ("Ssb", cc // 4)
                    S.op("dve", lambda e, cc=cc, vv=vv, hh=hh: e.max(out=vv[:, hh, 0:8], in_=Ssb[:, cc, :]), reads=[sk_], writes=["vv"])
                    S.op("dve", lambda e, cc=cc, vv=vv, hh=hh: e.match_replace(out=wk[:, :], in_to_replace=vv[:, hh, 0:8], in_values=Ssb[:, cc, :], imm_value=-1e30),
                         reads=[sk_, "vv"], writes=["wk"])
                    S.op("dve", lambda e, vv=vv, hh=hh: e.max(out=vv[:, hh, 8:16], in_=wk[:, :]), reads=["wk"], writes=["vv"])
                    S.op("dve", lambda e, cc=cc, vv=vv, ii=ii, hh=hh: e.max_index(out=ii[:, hh, 0:8], in_max=vv[:, hh, 0:8], in_values=Ssb[:, cc, :]),
                         reads=[sk_, "vv"], writes=["ii"])
                    S.op("dve", lambda e, cc=cc, vv=vv, ii=ii, hh=hh: e.max_index(out=ii[:, hh, 8:16], in_max=vv[:, hh, 8:16], in_values=Ssb[:, cc, :]),
                         reads=[sk_, "vv"], writes=["ii"])
            S.op("dve", lambda e: e.tensor_tensor(out=cand[:].rearrange("p h (a b) -> p h a b", a=16),
                                                  in0=APx(v1[:], [[16, 8], [1, 16], [0, 16]]),
                                                  in1=APx(v2[:], [[16, 8], [0, 16], [1, 16]]), op=ALU.add),
                 reads=["vv"], writes=["cand"])
            for hh in range(8):
                S.op("dve", lambda e, hh=hh: e.max(out=scv[:, hh, 0:8], in_=cand[:, hh, :]), reads=["cand"], writes=["scv"])
                S.op("dve", lambda e, hh=hh: e.match_replace(out=wk2[:, :], in_to_replace=scv[:, hh, 0:8], in_values=cand[:, hh, :], imm_value=-1e30),
                     reads=["cand", "scv"], writes=["wk2"])
                S.op("dve", lambda e, hh=hh: e.max(out=scv[:, hh, 8:16], in_=wk2[:, :]), reads=["wk2"], writes=["scv"])
                S.op("dve", lambda e, hh=hh: e.max_index(out=pos[:, hh, 0:8], in_max=scv[:, hh, 0:8], in_values=cand[:, hh, :]), reads=["cand", "scv"], writes=["pos"])
                S.op("dve", lambda e, hh=hh: e.max_index(out=pos[:, hh, 8:16], in_max=scv[:, hh, 8:16], in_values=cand[:, hh, :]), reads=["cand", "scv"], writes=["pos"])
            S.op("dve", lambda e: e.tensor_single_scalar(ph[:], pos[:], 4, op=ALU.logical_shift_right), reads=["pos"], writes=["ph"])
            S.op("dve", lambda e: e.tensor_copy(phf[:], ph[:]), reads=["ph"], writes=["phf"])
            S.op("dve", lambda e: e.tensor_single_scalar(ph[:], pos[:], 15, op=ALU.bitwise_and), reads=["pos", "phf"], writes=["ph"])
            S.op("dve", lambda e: e.tensor_copy(plf[:], ph[:]), reads=["ph"], writes=["plf"])
            S.op("dve", lambda e: e.tensor_copy(i1f[:], i1[:]), reads=["ii"], writes=["i1f"])
            S.op("dve", lambda e: e.tensor_copy(i2f[:], i2[:]), reads=["ii"], writes=["i2f"])
            for (pf, pfk, iif, iifk, so, sok) in ((phf, "phf", i1f, "i1f", s1, s1k), (plf, "plf", i2f, "i2f", s2, s2k)):
                S.op("dve", lambda e, pf=pf: e.tensor_tensor(out=eq[:], in0=APx(pf[:], [[16, 8], [1, 16], [0, 16]]),
                                                             in1=APx(iota[:], [[0, 8], [0, 16], [1, 16]]), op=ALU.is_equal),
                     reads=[pfk, "iota"], writes=["cand"])
                S.op("dve", lambda e, iif=iif: e.tensor_tensor(out=eq[:], in0=eq[:], in1=APx(iif[:], [[16, 8], [0, 16], [1, 16]]), op=ALU.mult),
                     reads=["cand", iifk], writes=["cand"])
                S.op("dve", lambda e, so=so: e.tensor_reduce(out=so[:], in_=eq[:], axis=AX.X, op=ALU.add), reads=["cand"], writes=[sok])
            S.op("dve", lambda e: e.scalar_tensor_tensor(out=idxf[:].rearrange("p (h k) -> p h k", h=8), in0=s1[:], scalar=128.0, in1=s2[:],
                                                         op0=ALU.mult, op1=ALU.add), reads=[s1k, s2k], writes=["idxf"])
            S.op("dve", lambda e: e.tensor_copy(idx32[:], idxf[:]), reads=["idxf"], writes=[idxk])
            S.op("dve", lambda e: e.tensor_tensor(out=gsm[:], in0=scv[:], in1=APx(scv[:], [[16, 8], [0, 16]]), op=ALU.subtract),
                 reads=["scv"], writes=[gsmk])
            S.op("act", lambda e: e.activation(out=gsm[:], in_=gsm[:], func=AF.Exp), reads=[gsmk], writes=[gsmk])
            S.op("dve", lambda e: e.tensor_reduce(out=zs[:, 0:8], in_=gsm[:], axis=AX.X, op=ALU.add), reads=[gsmk], writes=["zs"])
            S.op("dve", lambda e: e.reciprocal(zs[:, 8:16], zs[:, 0:8]), reads=["zs"], writes=["zs"])
            S.op("dve", lambda e: e.tensor_tensor(out=gsm[:], in0=gsm[:], in1=APx(zs[:, 8:16], [[1, 8], [0, 16]]), op=ALU.mult),
                 reads=[gsmk, "zs"], writes=[gsmk])

        def back(jt, cap):
            pb = jt % 2
            u = U2[pb]; h1 = u; uk = ("bu", pb)
            s1 = S1[pb]; s2 = S2[pb]; gsm = GSM[pb]; idx32 = IDX32[pb]
            s1k = ("s1", pb); s2k = ("s2", pb); gsmk = ("gsm", pb); idxk = ("idx32", pb)
            h1b = H1B[pb]; h1bk = ("h1b", pb)
            nrep = (len(cap) + 239) // 240 if cap else 0
            S.op("dve", lambda e: e.tensor_copy(tb[:, 1, :], s1[:].rearrange("p h k -> p (h k)")), reads=[s1k], writes=["tb12"])
            S.op("dve", lambda e: e.tensor_copy(tb[:, 2, :], s2[:].rearrange("p h k -> p (h k)")), reads=[s2k], writes=["tb12"])
            for j in (1, 2):
                S.op("pe", lambda e, j=j: e.transpose(pTb[:, j, :], tb[:, j, :], ident[:, :]), reads=["tb12", "identB"], writes=["bpTb"])
            S.op("act", lambda e: e.activation(out=tT[:, 1:3, :], in_=pTb[:, 1:3, :], func=AF.Identity), reads=["bpTb"], writes=["tT12"])
            S.op("dve", lambda e: e.scalar_tensor_tensor(out=idxT[:], in0=tT[:, 1, :], scalar=128.0, in1=tT[:, 2, :], op0=ALU.mult, op1=ALU.add),
                 reads=["tT12"], writes=["idxT"])
            S.op("dve", lambda e: e.memset(actv[:], 0.0), writes=["actv"])
            dot_ev = {}
            for s_ in range(128):
                db = Db[s_ % NB]; dk = ("Db", s_ % NB)
                S.dma("pool", lambda e, s_=s_, db=db: e.indirect_dma_start(out=db[:, :], out_offset=None, in_=downb,
                                                                            in_offset=bass.IndirectOffsetOnAxis(ap=idx32[:, s_:s_ + 1], axis=0)),
                      reads=[idxk] + tabk_d, writes=[dk], slot=Dslot[s_ % NB])
                dot_ev[s_] = S.op("dve", lambda e, s_=s_, db=db: e.scalar_tensor_tensor(out=junk[:, :], in0=db[:, :], scalar=1.0, in1=h1b[:, :], op0=ALU.mult, op1=ALU.mult,
                                                                            accum_out=actv[:, s_:s_ + 1]),
                     reads=[dk, h1bk, "actv"], writes=["junk", ("actv", s_)])
                if cap and s_ % 2 == 1:
                    S.replay(cap, 1)
            akeys = [("actv", s_) for s_ in range(128)]
            S.op("dve", lambda e: e.tensor_tensor(out=gl[:], in0=actv[:], in1=actv[:], op=ALU.mult), reads=akeys + ["actv"], writes=["gl"])
            S.op("dve", lambda e: e.tensor_scalar(gl[:], gl[:], 0.044715, 1.0, op0=ALU.mult, op1=ALU.add), reads=["gl"], writes=["gl"])
            S.op("dve", lambda e: e.tensor_tensor(out=gl[:], in0=gl[:], in1=actv[:], op=ALU.mult), reads=["gl"], writes=["gl"])
            S.op("act", lambda e: e.activation(out=gl[:], in_=gl[:], func=AF.Exp, scale=-GELU_C), reads=["gl"], writes=["gl"])
            S.op("dve", lambda e: e.tensor_scalar(gl[:], gl[:], 1.0, None, op0=ALU.add), reads=["gl"], writes=["gl"])
            S.op("dve", lambda e: e.reciprocal(gl[:], gl[:]), reads=["gl"], writes=["gl"])
            S.op("dve", lambda e: e.tensor_tensor(out=gl[:], in0=gl[:], in1=actv[:], op=ALU.mult), reads=["gl"], writes=["gl"])
            S.op("dve", lambda e: e.tensor_tensor(out=tb[:, 0, :], in0=gl[:], in1=gsm[:].rearrange("p h k -> p (h k)"), op=ALU.mult),
                 reads=["gl", gsmk], writes=["tb"])
            S.op("pe", lambda e: e.transpose(pTb[:, 0, :], tb[:, 0, :], ident[:, :]), reads=["tb", "identB"], writes=["bpTb"])
            S.op("act", lambda e: e.activation(out=tT[:, 0, :], in_=pTb[:, 0, :], func=AF.Identity), reads=["bpTb"], writes=["tT"])
            S.op("dve", lambda e: e.tensor_copy(APx(WD[:], [[129, 128]]), tT[:, 0, :]), reads=["tT"], writes=["WD"])
            mm_ev = {}
            nrep_up = (len(cap) + 99) // 100 if cap else 0
            for t_ in range(128):
                ub = Ub[t_ % NB]; ubk = ("Ub", t_ % NB)
                S.dma("pool", lambda e, t_=t_, ub=ub: e.indirect_dma_start(out=ub[:, :], out_offset=None, in_=upb,
                                                                            in_offset=bass.IndirectOffsetOnAxis(ap=idxT[:, t_:t_ + 1], axis=0)),
                      reads=["idxT"] + tabk_u, writes=[ubk], slot=Uslot[t_ % NB])
                for hf in range(2):
                    mm_ev[t_] = S.op("pe", lambda e, t_=t_, ub=ub, hf=hf: e.matmul(pU[hf][:, :], WD[:, t_ * 128:(t_ + 1) * 128], ub[:, hf * 512:(hf + 1) * 512],
                                                                        start=(t_ == 0), stop=(t_ == 127)),
                         reads=["WD", ubk], writes=[("bpU", hf)])
                if cap:
                    S.replay(cap, nrep_up)
            for hf in range(2):
                S.op("dve", lambda e, hf=hf: e.scalar_tensor_tensor(out=ot[:, hf * 512:(hf + 1) * 512], in0=h1[:, hf * 512:(hf + 1) * 512], scalar=ALPHA,
                                                                     in1=pU[hf][:, :], op0=ALU.mult, op1=ALU.add),
                     reads=[uk, ("bpU", hf)], writes=["ot"])
            layernorm(ot, "ot", ot, "ot", 4, st=st2, sm=sm2, stk="bst2", smk="bsm2")
            S.dma("sp", lambda e: e.dma_start(out=out[jt * 128:(jt + 1) * 128, :], in_=ot[:, :]), reads=["ot"], writes=["out"])
        S.capture_begin()
        front(0)
        cap0 = S.capture_end()
        S.replay(cap0, 5)
        load_weights()
        S.replay(cap0, len(cap0))
        for jt in range(NTB):
            cap = None
            if jt + 1 < NTB:
                S.capture_begin()
                front(jt + 1)
                cap = S.capture_end()
            back(jt, cap)
            if cap:
                S.replay(cap, len(cap))
        S.fence("sp")
        S.emit()


def own_tiles(g):
    return [8 * (jt // 2) + 2 * g + (jt % 2) for jt in range(NTB)]


def host_inputs(x, meta_tokens, ln_emb_g, ln_emb_b, w_in, conv_w, conv_b, gate_a_w, gate_a_b,
                gate_x_w, gate_x_b, lru_lambda, attn_norm_g, lru_norm_g, w_out, ln1_g, ln1_b,
                peer_query_w, peer_sub_keys, peer_down, peer_up, ln2_g, ln2_b):
    f = lambda a: np.ascontiguousarray(np.asarray(a, dtype=np.float32))
    x = f(x); meta_tokens = f(meta_tokens)
    w_in0 = f(w_in)[0]
    lnE_T = np.concatenate([f(ln_emb_g).reshape(8, 128).T, f(ln_emb_b).reshape(8, 128).T], axis=1)
    p = np.arange(128)
    ident = np.eye(128, dtype=np.float32)
    negtri = -(p[:, None] >= p[None, :]).astype(np.float32)
    negones = -np.ones((128, 128), np.float32)
    j = np.arange(512)
    masks = [((j[None, :] - 128 * r - p[:, None]) > 0).astype(np.float32) for r in range(4)]
    masks.append(np.broadcast_to((p[:, None] < NMETA), (128, 512)).astype(np.float32))
    cst = np.ascontiguousarray(np.concatenate([ident, negtri, negones] + masks, axis=1))
    iota16 = np.ascontiguousarray(np.broadcast_to(np.arange(16, dtype=np.float32), (128, 16)))
    w_out0 = f(w_out)[0]
    rows = []
    gains = np.zeros((128, 8), np.float32)
    ag = f(attn_norm_g)[0]; lg = f(lru_norm_g)[0]
    for r in range(4):
        rows.append(w_out0[128 * r:128 * (r + 1)])
        rows.append(w_out0[512 + 128 * r:512 + 128 * (r + 1)])
        gains[:, 2 * r] = ag[128 * r:128 * (r + 1)]
        gains[:, 2 * r + 1] = lg[128 * r:128 * (r + 1)]
    w_out_p = np.ascontiguousarray(np.concatenate(rows, axis=0))
    lnB = np.ascontiguousarray(np.stack([f(ln_emb_g), f(ln_emb_b), f(ln1_g)[0], f(ln1_b)[0], f(ln2_g)[0], f(ln2_b)[0]], axis=0))
    wq = np.ascontiguousarray(f(peer_query_w)[0].reshape(D, 2048))
    sk = f(peer_sub_keys)[0]
    skT = np.ascontiguousarray(sk.transpose(3, 0, 1, 2).reshape(128, 2048))
    down = f(peer_down)[0]; upm = f(peer_up)[0]
    cw = f(conv_w)[0]; cb = f(conv_b)[0]; gab = f(gate_a_b)[0]; gxb = f(gate_x_b)[0]; lam = f(lru_lambda)[0]
    gaw = f(gate_a_w)[0]; gxw = f(gate_x_w)[0]
    in_maps = []
    for c in range(8):
        b, g = divmod(c, 4)
        cols = np.concatenate([np.arange(base + 128 * g, base + 128 * (g + 1)) for base in (0, 512, 1024, 1536, 2048)])
        ch = slice(128 * g, 128 * (g + 1))
        lru_p = np.stack([cw[0, ch], cw[1, ch], cw[2, ch], cw[3, ch], cb[ch], gab[ch], gxb[ch], lam[ch]], axis=1)
        gate_w = np.zeros((128, 256), np.float32)
        for hh in range(2):
            gate_w[64 * hh:64 * hh + 64, 64 * hh:64 * hh + 64] = gaw[2 * g + hh]
            gate_w[64 * hh:64 * hh + 64, 128 + 64 * hh:128 + 64 * hh + 64] = gxw[2 * g + hh]
        idxm = np.zeros((128, 64), np.int32)
        for jt in range(NTB):
            for r in range(4):
                idxm[:, jt * 4 + r] = r * 1024 + (2 * g + jt % 2) * 128 + p
        in_maps.append({
            "xb": x[b], "xo": np.ascontiguousarray(x[b].reshape(NT, 128, D)[own_tiles(g)].reshape(NTB * 128, D)), "meta": meta_tokens,
            "lnE_T": np.ascontiguousarray(lnE_T), "w_in": np.ascontiguousarray(w_in0[:, cols]),
            "lru_p": np.ascontiguousarray(lru_p), "gate_w": gate_w, "cst": cst, "identf": ident,
            "w_out_p": w_out_p, "gmix": gains, "lnB": lnB, "wq": wq, "skT": skT, "down": down, "up": upm,
            "idxm": idxm, "iota16": iota16,
        })
    return in_maps


_NC_CACHE = {}
FUSED = True
A_KEYS = ["xb", "meta", "lnE_T", "w_in", "lru_p", "gate_w", "cst", "identf"]
B_KEYS = ["xo", "cst", "w_out_p", "gmix", "lnB", "wq", "skT", "down", "up", "idxm", "iota16"]


def kernel(**inputs):
    debug = bool(os.environ.get("MK_DEBUG"))
    in_maps = host_inputs(**inputs)
    if not FUSED:
        if "A" not in _NC_CACHE:
            _NC_CACHE["A"] = build(False, mode="A")
            _NC_CACHE["B"] = build(False, mode="B")
        ra = run_bass_kernel_spmd(_NC_CACHE["A"], [{k: m[k] for k in A_KEYS} for m in in_maps], core_ids=list(range(8)))
        mapsb = []
        for c in range(8):
            b = c // 4
            m = {k: in_maps[c][k] for k in B_KEYS}
            for i in range(8):
                m["mixall%d" % i] = np.ascontiguousarray(np.concatenate([ra.results[4 * b + r]["mixbuf%d" % i] for r in range(4)], axis=0))
            mapsb.append(m)
        res = run_bass_kernel_spmd(_NC_CACHE["B"], mapsb, core_ids=list(range(8)))
        outp = np.zeros((2, SEQ, D), np.float32)
        for c in range(8):
            b, g = divmod(c, 4)
            outp[b].reshape(NT, 128, D)[own_tiles(g)] = res.results[c]["out"].reshape(NTB, 128, D)
        return outp
    if debug not in _NC_CACHE:
        _NC_CACHE[debug] = build(debug)
    nc = _NC_CACHE[debug]
    res = run_bass_kernel_spmd(nc, in_maps, core_ids=list(range(8)))
    outp = np.zeros((2, SEQ, D), np.float32)
    for c in range(8):
        b, g = divmod(c, 4)
        outp[b].reshape(NT, 128, D)[own_tiles(g)] = res.results[c]["out"].reshape(NTB, 128, D)
    if debug:
        kernel.dbg = [res.results[c]["dbg"] for c in range(8)]
    return outp
```
